# Optimizing a Trainium2 kernel written in Bass

```python
import math
import jax, jax.numpy as jnp
from jax import lax
import numpy as np

D_MODEL = 1024
BATCH = 8
SEQ = 2048
DEPTH = 1
DEC_BATCH = 128
DEC_SEQ = 4
PAST_LEN = 2048
PAGE_SIZE = 128

R_HEADS = 8
R_HEAD_DIM = 64
R_WIDTH = R_HEADS * R_HEAD_DIM
DECAY_LORA = 64
AAA_LORA = 64
GATE_LORA = 128
R_COLS = 3 * R_WIDTH + DECAY_LORA + AAA_LORA + GATE_LORA
A_HEADS = 4
A_QK_DIM = 64
A_V_DIM = 2 * A_QK_DIM
A_WIDTH = A_HEADS * A_V_DIM
A_COLS = A_HEADS * (2 * 2 * A_QK_DIM + A_V_DIM)
MIX_WIDTH = R_WIDTH + A_WIDTH
PROJ_COLS = A_COLS + R_COLS
D_FF = 2816
NUM_BUCKETS = 32
MAX_DISTANCE = 128
Q_BLOCK = 128
NORM_EPS = 1e-6
SUBLN_EPS = 1e-5
GN_EPS = 64e-5
N_SUBNORMS = 6
NEG_INF = -1e30

kernel_name = 'hybrid_rwkv7_diffattn_macaron_step'


def rms_norm(x, g, eps):
    xf = x.astype(jnp.float32)
    y = xf * lax.rsqrt(jnp.mean(xf * xf, axis=-1, keepdims=True) + eps)
    return (y * g.astype(jnp.float32)).astype(x.dtype)


def swiglu(x, w_in, w_out):
    gate, up = jnp.split(x @ w_in, 2, axis=-1)
    return (jax.nn.silu(gate) * up) @ w_out


def rel_bias(q_pos, k_pos, table):
    n = jnp.maximum(q_pos[:, None] - k_pos[None, :], 0)
    max_exact = NUM_BUCKETS // 2
    large = max_exact + (jnp.log(jnp.maximum(n, 1).astype(jnp.float32) / max_exact)
                         / math.log(MAX_DISTANCE / max_exact)
                         * (NUM_BUCKETS - max_exact)).astype(jnp.int32)
    large = jnp.minimum(large, NUM_BUCKETS - 1)
    bucket = jnp.where(n < max_exact, n, large)
    return jnp.transpose(table[bucket], (2, 0, 1)).astype(jnp.float32)


def diff_attend(q, k, v, q_pos, k_pos, table, lam):
    bias = rel_bias(q_pos, k_pos, table)
    s = jnp.einsum('bqhmd,bkhmd->bhmqk', q, k).astype(jnp.float32) * (A_QK_DIM ** -0.5)
    s = s + bias[None, :, None]
    s = jnp.where(k_pos[None, :] <= q_pos[:, None], s, NEG_INF)
    p = jax.nn.softmax(s, axis=-1)
    p = p[:, :, 0] - lam * p[:, :, 1]
    return jnp.einsum('bhqk,bkhd->bqhd', p.astype(v.dtype), v)


def diff_attend_blocked(q, k, v, table, lam):
    B, T = q.shape[0], q.shape[1]
    nb = T // Q_BLOCK
    qb = jnp.moveaxis(q.reshape(B, nb, Q_BLOCK, A_HEADS, 2, A_QK_DIM), 1, 0)
    k_pos = jnp.arange(T)
    starts = jnp.arange(nb) * Q_BLOCK

    def one(args):
        q_blk, s0 = args
        return diff_attend(q_blk, k, v, s0 + jnp.arange(Q_BLOCK), k_pos, table, lam)

    out = lax.map(one, (qb, starts))
    return jnp.moveaxis(out, 0, 1).reshape(B, T, A_HEADS, A_V_DIM)


def gather_pages(pool, page_table):
    g = pool[page_table]
    return g.reshape((page_table.shape[0], page_table.shape[1] * pool.shape[1]) + pool.shape[2:])


def wkv_scan(r, w, k, v, kk, a, s0):
    def step(S, inp):
        r_t, w_t, k_t, v_t, kk_t, a_t = inp
        sa = jnp.einsum('bhvk,bhk->bhv', S, -kk_t)
        S = (S * w_t[:, :, None, :] + sa[..., None] * (kk_t * a_t)[:, :, None, :]
             + v_t[..., None] * k_t[:, :, None, :])
        return S, jnp.einsum('bhvk,bhk->bhv', S, r_t)

    xs = tuple(jnp.moveaxis(t, 1, 0) for t in (r, w, k, v, kk, a))
    S, ys = lax.scan(step, s0.astype(jnp.float32), xs)
    return jnp.moveaxis(ys, 0, 1), S


def mixer(h, l, shift0, wkv0, attend, P):
    f32 = jnp.float32
    B, T = h.shape[0], h.shape[1]
    p = h @ P['w_in'][l]
    QK = A_HEADS * 2 * A_QK_DIM
    q = p[..., :QK].reshape(B, T, A_HEADS, 2, A_QK_DIM)
    k = p[..., QK:2 * QK].reshape(B, T, A_HEADS, 2, A_QK_DIM)
    v = p[..., 2 * QK:A_COLS].reshape(B, T, A_HEADS, A_V_DIM)
    lam_init = 0.8 - 0.6 * math.exp(-0.3 * l)
    lam = (jnp.exp(jnp.sum(P['da_lq1'][l].astype(f32) * P['da_lk1'][l].astype(f32)))
           - jnp.exp(jnp.sum(P['da_lq2'][l].astype(f32) * P['da_lk2'][l].astype(f32)))
           + lam_init)
    o = attend(q, k, v, lam)
    o_attn = (rms_norm(o, P['da_subln'][l], SUBLN_EPS) * (1.0 - lam_init)).reshape(B, T, A_WIDTH)
    pr = p[..., A_COLS:]
    prev = jnp.concatenate([shift0[:, None, :].astype(pr.dtype), pr[:, :-1]], axis=1)
    xr = pr + (prev - pr) * P['rw_mu'][l]
    r, kr, vr, w_lo, a_lo, g_lo = jnp.split(
        xr, [R_WIDTH, 2 * R_WIDTH, 3 * R_WIDTH, 3 * R_WIDTH + DECAY_LORA,
             3 * R_WIDTH + DECAY_LORA + AAA_LORA], axis=-1)
    log_w = -jax.nn.softplus(-(P['rw_w0'][l] + jnp.tanh(w_lo) @ P['rw_w2'][l])) - 0.5
    decay = jnp.exp(-jnp.exp(log_w.astype(f32)))
    a = jax.nn.sigmoid(P['rw_a0'][l] + a_lo @ P['rw_a2'][l])
    g = jax.nn.sigmoid(g_lo) @ P['rw_g2'][l]

    def heads(t):
        return t.reshape(B, T, R_HEADS, R_HEAD_DIM).astype(f32)

    kk = heads(kr * P['rw_kk'][l])
    kk = kk / jnp.maximum(jnp.sqrt(jnp.sum(kk * kk, axis=-1, keepdims=True)), 1e-12)
    kr = kr * (1.0 + (a - 1.0) * P['rw_ka'][l])
    rh, kh, vh, ah = heads(r), heads(kr), heads(vr), heads(a)
    y, wkv_new = wkv_scan(rh, heads(decay), kh, vh, kk, ah, wkv0)
    mu = jnp.mean(y, axis=-1, keepdims=True)
    var = jnp.mean(jnp.square(y - mu), axis=-1, keepdims=True)
    yn = ((y - mu) * lax.rsqrt(var + GN_EPS)).reshape(B, T, R_WIDTH)
    yn = yn * P['rw_gn_w'][l].astype(f32) + P['rw_gn_b'][l].astype(f32)
    bonus = jnp.sum(rh * kh * P['rw_rk'][l].astype(f32), axis=-1, keepdims=True) * vh
    o_rwkv = ((yn + bonus.reshape(B, T, R_WIDTH)) * g.astype(f32)).astype(h.dtype)
    out = jnp.concatenate([o_rwkv, o_attn.astype(h.dtype)], axis=-1) @ P['w_out'][l]
    return out, k, v, wkv_new, pr[:, -1]


def layer(x, l, shift0, wkv0, attend, P):
    g = P['norm_gains'][l]
    h = x + 0.5 * rms_norm(swiglu(rms_norm(x, g[0], NORM_EPS), P['ff1_in'][l], P['ff1_out'][l]), g[1], NORM_EPS)
    m, k, v, wkv, shift = mixer(rms_norm(h, g[2], NORM_EPS), l, shift0, wkv0, attend, P)
    h = h + rms_norm(m, g[3], NORM_EPS)
    h = h + 0.5 * rms_norm(swiglu(rms_norm(h, g[4], NORM_EPS), P['ff2_in'][l], P['ff2_out'][l]), g[5], NORM_EPS)
    return h, k, v, wkv, shift


def setup_inputs(seed: int = 0) -> dict:
    key = jax.random.key(seed)
    ks = iter(jax.random.split(key, 40))
    n_pages = PAST_LEN // PAGE_SIZE
    used = DEC_BATCH * n_pages
    n_pool = used + max(1, used // 4)

    def nrm(shape, scale):
        return jax.random.normal(next(ks), shape, jnp.float32) * scale

    inp = {}
    inp['x_prompt'] = nrm((BATCH, SEQ, D_MODEL), 1.0)
    inp['x_sample'] = nrm((DEC_BATCH, DEC_SEQ, D_MODEL), 1.0)
    inp['cache_k'] = nrm((DEPTH, n_pool, PAGE_SIZE, A_HEADS, 2, A_QK_DIM), 1.0)
    inp['cache_v'] = nrm((DEPTH, n_pool, PAGE_SIZE, A_HEADS, A_V_DIM), 1.0)
    inp['page_table'] = jax.random.permutation(next(ks), n_pool)[:used].reshape(DEC_BATCH, n_pages).astype(jnp.int32)
    inp['state_wkv'] = nrm((DEPTH, DEC_BATCH, R_HEADS, R_HEAD_DIM, R_HEAD_DIM), 0.3)
    inp['state_shift'] = nrm((DEPTH, DEC_BATCH, R_COLS), 1.0)
    inp['norm_gains'] = 1.0 + nrm((DEPTH, N_SUBNORMS, D_MODEL), 0.05)
    inp['ff1_in'] = nrm((DEPTH, D_MODEL, 2 * D_FF), D_MODEL ** -0.5)
    inp['ff1_out'] = nrm((DEPTH, D_FF, D_MODEL), D_FF ** -0.5)
    inp['ff2_in'] = nrm((DEPTH, D_MODEL, 2 * D_FF), D_MODEL ** -0.5)
    inp['ff2_out'] = nrm((DEPTH, D_FF, D_MODEL), D_FF ** -0.5)
    inp['w_in'] = nrm((DEPTH, D_MODEL, PROJ_COLS), D_MODEL ** -0.5)
    inp['w_out'] = nrm((DEPTH, MIX_WIDTH, D_MODEL), MIX_WIDTH ** -0.5)
    inp['rel_bias_table'] = nrm((NUM_BUCKETS, A_HEADS), 0.3)
    inp['da_lq1'] = nrm((DEPTH, A_QK_DIM), 0.1)
    inp['da_lk1'] = nrm((DEPTH, A_QK_DIM), 0.1)
    inp['da_lq2'] = nrm((DEPTH, A_QK_DIM), 0.1)
    inp['da_lk2'] = nrm((DEPTH, A_QK_DIM), 0.1)
    inp['da_subln'] = 1.0 + nrm((DEPTH, A_V_DIM), 0.05)
    inp['rw_mu'] = jax.random.uniform(next(ks), (DEPTH, R_COLS), jnp.float32)
    inp['rw_w0'] = nrm((DEPTH, R_WIDTH), 0.5) - 0.5
    inp['rw_w2'] = nrm((DEPTH, DECAY_LORA, R_WIDTH), 0.5 * DECAY_LORA ** -0.5)
    inp['rw_a0'] = nrm((DEPTH, R_WIDTH), 0.1)
    inp['rw_a2'] = nrm((DEPTH, AAA_LORA, R_WIDTH), 0.5 * AAA_LORA ** -0.5)
    inp['rw_g2'] = nrm((DEPTH, GATE_LORA, R_WIDTH), GATE_LORA ** -0.5)
    inp['rw_kk'] = 0.85 + nrm((DEPTH, R_WIDTH), 0.05)
    inp['rw_ka'] = 1.0 + nrm((DEPTH, R_WIDTH), 0.05)
    inp['rw_rk'] = nrm((DEPTH, R_HEADS, R_HEAD_DIM), 0.1)
    inp['rw_gn_w'] = 1.0 + nrm((DEPTH, R_WIDTH), 0.05)
    inp['rw_gn_b'] = nrm((DEPTH, R_WIDTH), 0.01)
    return inp


def reference(x_prompt, x_sample, cache_k, cache_v, page_table, state_wkv, state_shift,
              norm_gains, ff1_in, ff1_out, ff2_in, ff2_out, w_in, w_out, rel_bias_table,
              da_lq1, da_lk1, da_lq2, da_lk2, da_subln, rw_mu, rw_w0, rw_w2, rw_a0, rw_a2,
              rw_g2, rw_kk, rw_ka, rw_rk, rw_gn_w, rw_gn_b):
    P = {'norm_gains': norm_gains, 'ff1_in': ff1_in, 'ff1_out': ff1_out,
         'ff2_in': ff2_in, 'ff2_out': ff2_out, 'w_in': w_in, 'w_out': w_out,
         'da_lq1': da_lq1, 'da_lk1': da_lk1, 'da_lq2': da_lq2, 'da_lk2': da_lk2,
         'da_subln': da_subln, 'rw_mu': rw_mu, 'rw_w0': rw_w0, 'rw_w2': rw_w2,
         'rw_a0': rw_a0, 'rw_a2': rw_a2, 'rw_g2': rw_g2, 'rw_kk': rw_kk,
         'rw_ka': rw_ka, 'rw_rk': rw_rk, 'rw_gn_w': rw_gn_w, 'rw_gn_b': rw_gn_b}
    B = x_prompt.shape[0]
    hp, hs = x_prompt, x_sample
    kps, vps, kss, vss, wps, wss, sps, sss = [], [], [], [], [], [], [], []

    def attend_p(q, k, v, lam):
        return diff_attend_blocked(q, k, v, rel_bias_table, lam)

    for l in range(DEPTH):
        past_k = gather_pages(cache_k[l], page_table)
        past_v = gather_pages(cache_v[l], page_table)

        def attend_s(q, k, v, lam, past_k=past_k, past_v=past_v):
            tp, tn = past_k.shape[1], q.shape[1]
            kf = jnp.concatenate([past_k.astype(k.dtype), k], axis=1)
            vf = jnp.concatenate([past_v.astype(v.dtype), v], axis=1)
            return diff_attend(q, kf, vf, tp + jnp.arange(tn), jnp.arange(tp + tn),
                               rel_bias_table, lam)

        hp, kp, vp, wp, sp = layer(hp, l, jnp.zeros((B, R_COLS), x_prompt.dtype),
                                   jnp.zeros((B, R_HEADS, R_HEAD_DIM, R_HEAD_DIM), jnp.float32),
                                   attend_p, P)
        hs, ks_, vs_, ws_, ss_ = layer(hs, l, state_shift[l], state_wkv[l], attend_s, P)
        kps.append(kp); vps.append(vp); wps.append(wp); sps.append(sp)
        kss.append(ks_); vss.append(vs_); wss.append(ws_); sss.append(ss_)

    return (hp, hs, jnp.stack(kps), jnp.stack(vps), jnp.stack(kss), jnp.stack(vss),
            jnp.stack(wps), jnp.stack(wss), jnp.stack(sps), jnp.stack(sss))
```

```python
import math
import os
import time
import numpy as np
import concourse.bass as bass
import concourse.mybir as mybir
from concourse.bass_utils import run_bass_kernel_spmd

F32 = mybir.dt.float32
BF16 = mybir.dt.bfloat16
I32 = mybir.dt.int32
AF = mybir.ActivationFunctionType
ALU = mybir.AluOpType
AX = mybir.AxisListType

NCORES = 8
D = 1024
TP = 2048
TS = 64
TT = TP + TS
DFF = 2816
NFC = 22
PROJ = 3328
RCOLS = 1792
NPOOL = 2560
EPS = 1e-6

STAGE = int(os.environ.get('KDEV_STAGE', 3))


class Buf:
    __slots__ = ("name", "w", "r", "x")

    def __init__(self, name, excl=False):
        self.name = name
        self.w = None
        self.r = {}
        self.x = excl


class Sched:
    def __init__(self, nc):
        self.nc = nc
        self.prog = {k: [] for k in ("pe", "act", "dve", "pool", "sp")}
        self.csem = {k: nc.alloc_semaphore("c_" + k) for k in ("pe", "act", "dve", "pool")}
        self.cnt = {k: 0 for k in self.csem}
        self.known = {k: {} for k in self.prog}
        self.dsem = {}
        self.sems = {}

    def _dsem(self, name):
        if name not in self.dsem:
            s = self.nc.alloc_semaphore("d_" + name)
            self.dsem[name] = [s, 0]
        return self.dsem[name]

    def op(self, e, fn, r=(), w=(), dma=None):
        deps = []
        for b in r:
            if b.w is not None:
                deps.append(b.w)
            if b.x:
                deps.extend(b.r.values())
        for b in w:
            if b.w is not None:
                deps.append(b.w)
            deps.extend(b.r.values())
        waits = {}
        for (key, sem, val, prod) in deps:
            if prod == "pe" and e == "pe" and dma is None:
                continue
            if self.known[e].get(key, 0) >= val:
                continue
            if key not in waits or waits[key][1] < val:
                waits[key] = (sem, val)
        for key, (sem, val) in waits.items():
            self.known[e][key] = val
            self.prog[e].append(("wait", sem, val))
        if dma is None:
            self.cnt[e] += 1
            tok = ("c_" + e, self.csem[e], self.cnt[e], e)
            self.prog[e].append(("op", fn, self.csem[e], 1))
        else:
            d = self._dsem(dma)
            d[1] += 16
            tok = ("d_" + dma, d[0], d[1], None)
            self.prog[e].append(("op", fn, d[0], 16))
        for b in r:
            old = b.r.get(tok[0])
            if old is None or old[2] < tok[2]:
                b.r[tok[0]] = tok
        for b in w:
            b.w = tok
            b.r = {}
        return tok

    def barrier(self):
        for e in self.prog:
            for k in self.csem:
                if self.cnt[k] > self.known[e].get("c_" + k, 0):
                    self.known[e]["c_" + k] = self.cnt[k]
                    self.prog[e].append(("wait", self.csem[k], self.cnt[k]))
            for name, (sem, val) in self.dsem.items():
                if val > self.known[e].get("d_" + name, 0):
                    self.known[e]["d_" + name] = val
                    self.prog[e].append(("wait", sem, val))

    def emit(self):
        nc = self.nc
        engs = {"pe": "tensor", "act": "scalar", "dve": "vector", "pool": "gpsimd", "sp": "sync"}
        with nc.Block() as block:
            for k, attr in engs.items():
                prog = self.prog[k]

                def body(eng, prog=prog):
                    for it in prog:
                        if it[0] == "wait":
                            eng.wait_ge(it[1], it[2])
                        else:
                            it[1](eng).then_inc(it[2], it[3])

                getattr(block, attr)(body)


def build_program():
    nc = bass.Bass("TRN2", target_bir_lowering=False)
    S = Sched(nc)

    def din(name, shape, dt=F32):
        return nc.dram_tensor(name, list(shape), dt, kind="ExternalInput").ap()

    def dout(name, shape, dt=F32):
        return nc.dram_tensor(name, list(shape), dt, kind="ExternalOutput").ap()

    def dscr(name, shape, dt=F32):
        return nc.dram_tensor(name, list(shape), dt, kind="Internal").ap()

    x_all = din("x_all", [TT, D])
    gains = din("gains", [6, D])
    ff1_in = din("ff1_in", [D, 2 * DFF])
    ff1_out = din("ff1_out", [DFF, D])
    ff2_in = din("ff2_in", [D, 2 * DFF])
    ff2_out = din("ff2_out", [DFF, D])
    w_in = din("w_in", [D, PROJ])
    w_out = din("w_out", [D, D])
    identf_d = din("identf", [128, 128])

    y_all = dout("y_all", [TT, D])
    k_out = dout("k_out", [TT, 512])
    v_out = dout("v_out", [TT, 512])
    shp_out = dout("shp_out", [1, RCOLS])
    shs_out = dout("shs_out", [16, RCOLS])
    wkvp_out = dout("wkvp_out", [8, 64, 64])
    wkvs_out = dout("wkvs_out", [128, 4096])

    rwvec = din("rwvec", [1, 5376])
    cmask = din("cmask", [64, 4, 64])
    rw_w2 = din("rw_w2", [64, 512])
    rw_a2 = din("rw_a2", [64, 512])
    rw_g2 = din("rw_g2", [128, 512])
    shift0 = din("shift0", [16, RCOLS])
    wkv0 = din("wkv0", [128, 4096])
    prev_scr = dscr("prev_scr", [TT, RCOLS])
    lamvec = din("lamvec", [1, 256])
    subln = din("subln", [1, 128])
    relb = din("relb", [32, 4])
    onehot = din("onehot", [33, 383])
    selc = din("selc", [8, 2, 4])
    ptab = din("ptab", [1, 256], I32)
    iotap = din("iotap", [128, 1], I32)
    HAVE_SAMP = STAGE >= 3 and os.environ.get("KDEV_NOSAMP") is None
    ck2 = din("ck2", [NPOOL * 128, 512]) if HAVE_SAMP else None
    cv2 = din("cv2", [NPOOL * 128, 512]) if HAVE_SAMP else None
    r_scr = dscr("r_scr", [4, 128, 383])
    oat_scr = (dout if os.environ.get("KDEV_DBG") else dscr)("oat_scr", [TT, 512])
    h2_scr = (dout if os.environ.get("KDEV_DBG") else dscr)("h2_scr", [TT, D])
    rs_scr = dscr("rs_scr", [64, 3072])
    ys_scr = dscr("ys_scr", [128, 256])
    orw_scr = (dout if os.environ.get("KDEV_DBG") else dscr)("orw_scr", [TT, 512])
    h_scr = dout("h_scr", [TT, D]) if os.environ.get("KDEV_DBG") else dscr("h_scr", [TT, D])
    pr_scr = dscr("pr_scr", [TT, RCOLS])

    NW = 52000
    BIG = nc.alloc_sbuf_tensor("BIG", [128, NW], F32)
    identf = nc.alloc_sbuf_tensor("identf_sb", [128, 128], F32)
    identb = nc.alloc_sbuf_tensor("identb_sb", [128, 128], BF16)

    class Arena:
        def __init__(self):
            self.off = 0

        def alloc(self, shape, dt=F32):
            n = 1
            for d_ in shape[1:]:
                n *= d_
            words = n if dt in (F32, I32) else (n + 1) // 2
            assert self.off + words <= NW, ("arena overflow", self.off, words)
            v = BIG[:, self.off:self.off + words]
            self.off += words
            if dt != F32:
                v = v.bitcast(dt)[:, 0:n]
            if len(shape) > 2:
                names = " ".join("a%d" % i for i in range(len(shape) - 1))
                kw = {"a%d" % i: shape[i + 1] for i in range(len(shape) - 1)}
                v = v.rearrange("p (%s) -> p %s" % (names, names), **kw)
            if shape[0] < 128:
                v = v[0:shape[0]]
            return v

    AR = Arena()
    PS = [nc.alloc_psum_tensor("ps%d" % i, [128, 512], F32) for i in range(8)]
    PSB = [Buf("ps%d" % i, True) for i in range(8)]
    B_ident = Buf("ident")
    B_GA = [Buf("ga0"), Buf("ga1")]
    GAh = [None]

    def psb16(i):
        return PS[i][:].bitcast(BF16)

    S.op("sp", lambda e: e.dma_start(out=identf[:], in_=identf_d), w=[B_ident], dma="ident")
    S.op("dve", lambda e: e.tensor_copy(out=identb[:], in_=identf[:]), r=[B_ident], w=[B_ident])

    def load_gain(GA, slot, idx):
        S.op("sp", lambda e: e.dma_start(out=GA[:, slot, :], in_=gains[idx:idx + 1, :].partition_broadcast(128)),
             w=[B_GA[slot]], dma="ga%d" % slot)

    tiles_all = [(t * 128, 128) for t in range(16)] + [(TP, TS)]
    groups = [tiles_all[0:4], tiles_all[4:8], tiles_all[8:12], tiles_all[12:16], tiles_all[16:17]]
    if os.environ.get("KDEV_NG"):
        groups = groups[:int(os.environ["KDEV_NG"])]

    def rms_rstd(src_ap, rows, junk_ap, ss_ap, rstd_ap, rbufs, junk_b, st_b, eps=EPS, n=D):
        S.op("act", lambda e: e.activation(out=junk_ap, in_=src_ap, func=AF.Square, accum_out=ss_ap),
             r=rbufs, w=[junk_b, st_b])
        S.op("dve", lambda e: e.tensor_scalar(out=rstd_ap, in0=ss_ap, scalar1=1.0 / n, scalar2=eps,
                                              op0=ALU.mult, op1=ALU.add), r=[st_b], w=[st_b])
        S.op("act", lambda e: e.sqrt(out=rstd_ap, in_=rstd_ap), r=[st_b], w=[st_b])
        S.op("dve", lambda e: e.reciprocal(out=rstd_ap, in_=rstd_ap), r=[st_b], w=[st_b])

    def ffn_phase(tag, src_d, dst_d, w_in_d, w_out_d, gi, go):
        AR.off = 0
        Win = AR.alloc([128, 8, 2 * DFF], BF16)
        Wout = AR.alloc([128, NFC, D], BF16)
        GA = AR.alloc([128, 2, D], F32)
        B_wg = [Buf("wg%d" % i) for i in range(11)]
        B_wu = [Buf("wu%d" % i) for i in range(11)]
        B_wo = [Buf("wo%d" % i) for i in range(11)]
        w_in_v = w_in_d.rearrange("(k p) n -> p k n", p=128)
        w_out_v = w_out_d.rearrange("(f p) n -> p f n", p=128)
        for i in range(11):
            S.op("pool", lambda e, i=i: e.dma_start(out=Win[:, :, i * 256:(i + 1) * 256],
                                                     in_=w_in_v[:, :, i * 256:(i + 1) * 256]),
                 w=[B_wg[i]], dma="wg%d" % i)
            S.op("pool", lambda e, i=i: e.dma_start(out=Win[:, :, DFF + i * 256:DFF + (i + 1) * 256],
                                                     in_=w_in_v[:, :, DFF + i * 256:DFF + (i + 1) * 256]),
                 w=[B_wu[i]], dma="wu%d" % i)
        for i in range(11):
            S.op("pool", lambda e, i=i: e.dma_start(out=Wout[:, 2 * i:2 * i + 2, :], in_=w_out_v[:, 2 * i:2 * i + 2, :]),
                 w=[B_wo[i]], dma="wo%d" % i)
        load_gain(GA, 0, gi)
        load_gain(GA, 1, go)

        xld = [AR.alloc([128, D], F32) for i in range(2)]
        B_xld = [Buf("xld0"), Buf("xld1")]
        xres = [AR.alloc([128, D], F32)] * 2
        B_xres = [Buf("xres0")] * 2
        junk = AR.alloc([128, D], BF16)
        B_junk = Buf("junk")
        xn = AR.alloc([128, D], BF16)
        B_xn = Buf("xn")
        xnT = AR.alloc([128, 8, 512], BF16)
        B_xnT = [Buf("xnT%d" % i) for i in range(4)]
        hT = AR.alloc([128, NFC, 512], BF16)
        B_hT = [Buf("hT%d" % i) for i in range(NFC)]
        sg = [AR.alloc([128, 512], F32) for i in range(2)]
        B_sg = [Buf("sg0"), Buf("sg1")]
        yb = AR.alloc([128, D], F32)
        B_y = Buf("y")
        hb = [AR.alloc([128, D], F32) for i in range(2)]
        B_hb = [Buf("hb0"), Buf("hb1")]
        st = AR.alloc([128, 8], F32)
        B_st = [Buf("st0"), Buf("st1")]

        cnt = {'x': 0, 'o': 0}

        def do_group(grp):
            ntok = sum(r for _, r in grp)
            for ti, (t0, rows) in enumerate(grp):
                sl = cnt['x'] % 2
                cnt['x'] += 1
                xs = xld[sl]
                S.op("sp", lambda e, xs=xs, t0=t0, rows=rows: e.dma_start(out=xs[:rows, :], in_=src_d[t0:t0 + rows, :]),
                     w=[B_xld[sl]], dma="xld%d" % sl)
                rms_rstd(xs[:rows, :], rows, junk[:rows, :], st[:rows, 0:1], st[:rows, 1:2], [B_xld[sl]], B_junk, B_st[0])
                S.op("dve", lambda e, xs=xs, rows=rows: e.scalar_tensor_tensor(
                    out=xn[:rows, :], in0=xs[:rows, :], scalar=st[:rows, 1:2], in1=GA[:rows, 0, :],
                    op0=ALU.mult, op1=ALU.mult), r=[B_xld[sl], B_st[0], B_GA[0]], w=[B_xn])
                for kc in range(8):
                    S.op("pe", lambda e, kc=kc, rows=rows: e.transpose(
                        psb16(0)[:, kc * 128:kc * 128 + rows], xn[:rows, kc * 128:(kc + 1) * 128], identb[:rows, :rows]),
                        r=[B_xn, B_ident], w=[PSB[0]])
                S.op("act", lambda e, ti=ti, rows=rows: e.copy(
                    out=xnT[:, :, ti * 128:ti * 128 + rows],
                    in_=psb16(0).rearrange("p (k t) -> p k t", k=8)[:, :, 0:rows]),
                    r=[PSB[0]], w=[B_xnT[ti]])
            for fc in range(NFC):
                pa = 1 + (fc % 2) * 2
                pb = pa + 1
                for kc in range(8):
                    S.op("pe", lambda e, fc=fc, kc=kc, pa=pa: e.matmul(
                        PS[pa][:, 0:ntok], lhsT=Win[:, kc, fc * 128:(fc + 1) * 128], rhs=xnT[:, kc, 0:ntok],
                        start=(kc == 0), stop=(kc == 7)),
                        r=[B_wg[fc // 2]] + B_xnT[:len(grp)], w=[PSB[pa]])
                for kc in range(8):
                    S.op("pe", lambda e, fc=fc, kc=kc, pb=pb: e.matmul(
                        PS[pb][:, 0:ntok], lhsT=Win[:, kc, DFF + fc * 128:DFF + (fc + 1) * 128], rhs=xnT[:, kc, 0:ntok],
                        start=(kc == 0), stop=(kc == 7)),
                        r=[B_wu[fc // 2]] + B_xnT[:len(grp)], w=[PSB[pb]])
                sgi = fc % 2
                S.op("act", lambda e, pa=pa, sgi=sgi: e.activation(out=sg[sgi][:, 0:ntok], in_=PS[pa][:, 0:ntok], func=AF.Silu),
                     r=[PSB[pa]], w=[B_sg[sgi]])
                S.op("dve", lambda e, fc=fc, pb=pb, sgi=sgi: e.tensor_tensor(
                    out=hT[:, fc, 0:ntok], in0=sg[sgi][:, 0:ntok], in1=PS[pb][:, 0:ntok], op=ALU.mult),
                    r=[B_sg[sgi], PSB[pb]], w=[B_hT[fc]])
            for ti, (t0, rows) in enumerate(grp):
                rs = cnt['o'] % 2
                cnt['o'] += 1
                xr_ = xres[rs]
                S.op("sp", lambda e, xr_=xr_, t0=t0, rows=rows: e.dma_start(out=xr_[:rows, :], in_=src_d[t0:t0 + rows, :]),
                     w=[B_xres[rs]], dma="xres0")
                for nh in range(2):
                    pc = 5 + nh
                    for fc in range(NFC):
                        S.op("pe", lambda e, fc=fc, nh=nh, pc=pc, ti=ti, rows=rows: e.matmul(
                            PS[pc][:rows, :], lhsT=hT[:, fc, ti * 128:ti * 128 + rows], rhs=Wout[:, fc, nh * 512:(nh + 1) * 512],
                            start=(fc == 0), stop=(fc == NFC - 1)),
                            r=[B_hT[fc], B_wo[fc // 2]], w=[PSB[pc]])
                    S.op("act", lambda e, nh=nh, pc=pc, rows=rows: e.copy(out=yb[:rows, nh * 512:(nh + 1) * 512], in_=PS[pc][:rows, :]),
                         r=[PSB[pc]], w=[B_y])
                rms_rstd(yb[:rows, :], rows, junk[:rows, :], st[:rows, 2:3], st[:rows, 3:4], [B_y], B_junk, B_st[1])
                ho = hb[rs]
                S.op("dve", lambda e, rows=rows: e.scalar_tensor_tensor(
                    out=yb[:rows, :], in0=yb[:rows, :], scalar=st[:rows, 3:4], in1=GA[:rows, 1, :],
                    op0=ALU.mult, op1=ALU.mult), r=[B_st[1], B_GA[1]], w=[B_y])
                S.op("dve", lambda e, rows=rows, ho=ho, xr_=xr_: e.scalar_tensor_tensor(
                    out=ho[:rows, :], in0=yb[:rows, :], scalar=0.5, in1=xr_[:rows, :],
                    op0=ALU.mult, op1=ALU.add), r=[B_y, B_xres[rs]], w=[B_hb[rs]])
                S.op("sp", lambda e, rows=rows, ho=ho, t0=t0: e.dma_start(out=dst_d[t0:t0 + rows, :], in_=ho[:rows, :]),
                     r=[B_hb[rs]], dma="hb%d" % rs)

        for grp in groups:
            do_group(grp)
        S.barrier()

    def proj_phase(qT, kT, Vb):
        Wi = AR.alloc([128, 8, PROJ], BF16)
        GA = AR.alloc([128, 2, D], F32)
        blocks = [(0, 512), (512, 1024), (1024, 1536), (1536, 2048), (2048, 2560), (2560, 3072), (3072, 3328)]
        B_wi = [Buf("wi%d" % i) for i in range(7)]
        w_in_v = w_in.rearrange("(k p) n -> p k n", p=128)
        for i, (c0, c1) in enumerate(blocks):
            S.op("pool", lambda e, c0=c0, c1=c1: e.dma_start(out=Wi[:, :, c0:c1], in_=w_in_v[:, :, c0:c1]),
                 w=[B_wi[i]], dma="wg%d" % i)
        load_gain(GA, 0, 2)
        S.op("dve", lambda e: e.memset(Vb[:, :, :, 128:129], 1.0), w=[B_vb])
        xld = [AR.alloc([128, D], F32) for i in range(2)]
        B_xld = [Buf("xld0"), Buf("xld1")]
        junk = AR.alloc([128, D], BF16)
        B_junk = Buf("junk")
        xn = AR.alloc([128, D], BF16)
        B_xn = Buf("xn")
        xnT = AR.alloc([128, 8, 512], BF16)
        B_xnT = [Buf("xnT%d" % i) for i in range(4)]
        ob = [AR.alloc([128, 512], F32) for i in range(3)]
        B_ob = [Buf("hb0"), Buf("hb1"), Buf("y")]
        st = AR.alloc([128, 8], F32)
        B_st = Buf("st0")
        cnt = {'x': 0, 'o': 0}

        def do_group(grp):
            ntok = sum(r for _, r in grp)
            g0 = grp[0][0]
            for ti, (t0, rows) in enumerate(grp):
                sl = cnt['x'] % 2
                cnt['x'] += 1
                xs = xld[sl]
                S.op("sp", lambda e, xs=xs, t0=t0, rows=rows: e.dma_start(out=xs[:rows, :], in_=h_scr[t0:t0 + rows, :]),
                     w=[B_xld[sl]], dma="xld%d" % sl)
                rms_rstd(xs[:rows, :], rows, junk[:rows, :], st[:rows, 0:1], st[:rows, 1:2], [B_xld[sl]], B_junk, B_st)
                S.op("dve", lambda e, xs=xs, rows=rows: e.scalar_tensor_tensor(
                    out=xn[:rows, :], in0=xs[:rows, :], scalar=st[:rows, 1:2], in1=GA[:rows, 0, :],
                    op0=ALU.mult, op1=ALU.mult), r=[B_xld[sl], B_st, B_GA[0]], w=[B_xn])
                for kc in range(8):
                    S.op("pe", lambda e, kc=kc, rows=rows: e.transpose(
                        psb16(0)[:, kc * 128:kc * 128 + rows], xn[:rows, kc * 128:(kc + 1) * 128], identb[:rows, :rows]),
                        r=[B_xn, B_ident], w=[PSB[0]])
                S.op("act", lambda e, ti=ti, rows=rows: e.copy(
                    out=xnT[:, :, ti * 128:ti * 128 + rows],
                    in_=psb16(0).rearrange("p (k t) -> p k t", k=8)[:, :, 0:rows]),
                    r=[PSB[0]], w=[B_xnT[ti]])
            for which in range(2):
                for h in range(4):
                    pa = 1 + (h % 2)
                    c0 = which * 512 + h * 128
                    for kc in range(8):
                        S.op("pe", lambda e, kc=kc, pa=pa, c0=c0: e.matmul(
                            PS[pa][:, 0:ntok], lhsT=Wi[:, kc, c0:c0 + 128], rhs=xnT[:, kc, 0:ntok],
                            start=(kc == 0), stop=(kc == 7)),
                            r=[B_wi[which]] + B_xnT[:len(grp)], w=[PSB[pa]])
                    dstT = qT if which == 0 else kT
                    sc = 0.125 if which == 0 else 1.0
                    S.op("act", lambda e, pa=pa, dstT=dstT, h=h, sc=sc: e.activation(
                        out=dstT[:, h, g0:g0 + ntok], in_=PS[pa][:, 0:ntok], func=AF.Copy, scale=sc),
                        r=[PSB[pa]], w=[B_qk])
            for ti, (t0, rows) in enumerate(grp):
                for bi in range(1, 7):
                    c0, c1 = blocks[bi]
                    wdt = c1 - c0
                    pc = 3 + (cnt['o'] % 3)
                    oi = cnt['o'] % 3
                    cnt['o'] += 1
                    for kc in range(8):
                        S.op("pe", lambda e, kc=kc, pc=pc, c0=c0, c1=c1, wdt=wdt, ti=ti, rows=rows: e.matmul(
                            PS[pc][:rows, 0:wdt], lhsT=xnT[:, kc, ti * 128:ti * 128 + rows], rhs=Wi[:, kc, c0:c1],
                            start=(kc == 0), stop=(kc == 7)),
                            r=[B_wi[bi], B_xnT[ti]], w=[PSB[pc]])
                    o_ = ob[oi]
                    S.op("act", lambda e, pc=pc, o_=o_, wdt=wdt, rows=rows: e.copy(out=o_[:rows, 0:wdt], in_=PS[pc][:rows, 0:wdt]),
                         r=[PSB[pc]], w=[B_ob[oi]])
                    if bi == 1:
                        dst = k_out[t0:t0 + rows, :]
                    elif bi == 2:
                        dst = v_out[t0:t0 + rows, :]
                        if t0 < TP:
                            tl = t0 // 128
                            S.op("dve", lambda e, pc=pc, tl=tl: e.tensor_copy(
                                out=Vb[:, tl, :, 0:128], in_=PS[pc][:, :].rearrange("p (h d) -> p h d", h=4)),
                                r=[PSB[pc]], w=[B_vb])
                    else:
                        dst = pr_scr[t0:t0 + rows, c0 - 1536:c1 - 1536]
                    S.op("sp", lambda e, o_=o_, dst=dst, wdt=wdt, rows=rows: e.dma_start(out=dst, in_=o_[:rows, 0:wdt]),
                         r=[B_ob[oi]], dma="ob%d" % oi)

        for grp in groups:
            do_group(grp)
        S.barrier()
        if os.environ.get("KDEV_NOSH"):
            return
        S.op("sp", lambda e: e.dma_start(out=shp_out, in_=pr_scr[TP - 1:TP, :]), dma="sh0")
        S.op("sp", lambda e: e.dma_start(out=shs_out, in_=pr_scr[TP:TT, :].rearrange("(b t) n -> b t n", t=4)[:, 3, :]), dma="sh1")


    C0 = math.exp(-0.5)

    class Tl:
        def __init__(self, shape, dt=F32, name="t"):
            self.ap = AR.alloc(shape, dt)
            self.b = Buf(name)

    def rwkv_phase():
        AR.off = 0
        RV = Tl([64, 5376], F32, "RV")
        LW = Tl([128, 2, 512], BF16, "LW")
        CM = Tl([64, 4, 64], F32, "CM")
        S.op("sp", lambda e: e.dma_start(out=RV.ap, in_=rwvec.partition_broadcast(64)), w=[RV.b], dma="RV")
        S.op("sp", lambda e: e.dma_start(out=CM.ap, in_=cmask), w=[CM.b], dma="CM")
        S.op("pool", lambda e: e.dma_start(out=LW.ap[0:64, 0, :], in_=rw_w2), w=[LW.b], dma="LW")
        S.op("pool", lambda e: e.dma_start(out=LW.ap[64:128, 0, :], in_=rw_a2), w=[LW.b], dma="LW")
        S.op("pool", lambda e: e.dma_start(out=LW.ap[:, 1, :], in_=rw_g2), w=[LW.b], dma="LW")
        ZR = Tl([1, RCOLS], F32, "ZR")
        S.op("dve", lambda e: e.memset(ZR.ap, 0.0), w=[ZR.b])
        S.op("sp", lambda e: e.dma_start(out=prev_scr[0:1, :], in_=ZR.ap), r=[ZR.b], dma="pv0")
        S.op("sp", lambda e: e.dma_start(out=prev_scr[1:TP, :], in_=pr_scr[0:TP - 1, :]), dma="pv1")
        S.op("sp", lambda e: e.dma_start(
            out=prev_scr[TP:TT, :].rearrange("(b t) n -> b t n", t=4)[:, 1:4, :],
            in_=pr_scr[TP:TT, :].rearrange("(b t) n -> b t n", t=4)[:, 0:3, :]), dma="pv2")
        S.op("sp", lambda e: e.dma_start(
            out=prev_scr[TP:TT, :].rearrange("(b t) n -> b t n", t=4)[:, 0, :], in_=shift0), dma="pv3")
        S.barrier()
        mark = AR.off

        def vec(i0, n, nch):
            return RV.ap[:, i0:i0 + n].unsqueeze(1).to_broadcast([64, nch, n])

        def d1(tok0, nch):
            T = {}
            P = Tl([64, nch, RCOLS], F32, "P")
            PV = Tl([64, nch, RCOLS], F32, "PV")
            S.op("sp", lambda e: e.dma_start(out=P.ap, in_=pr_scr[tok0:tok0 + 64 * nch, :].rearrange("(c t) n -> t c n", t=64)),
                 w=[P.b], dma="P")
            S.op("sp", lambda e: e.dma_start(out=PV.ap, in_=prev_scr[tok0:tok0 + 64 * nch, :].rearrange("(c t) n -> t c n", t=64)),
                 w=[PV.b], dma="PV")
            S.op("dve", lambda e: e.tensor_tensor(out=PV.ap, in0=PV.ap, in1=P.ap, op=ALU.subtract), r=[P.b], w=[PV.b])
            S.op("dve", lambda e: e.tensor_tensor(out=PV.ap, in0=PV.ap, in1=vec(0, RCOLS, nch), op=ALU.mult), r=[RV.b], w=[PV.b])
            S.op("dve", lambda e: e.tensor_tensor(out=P.ap, in0=P.ap, in1=PV.ap, op=ALU.add), r=[PV.b], w=[P.b])
            LO = Tl([64, nch, 256], BF16, "LO")
            S.op("act", lambda e: e.activation(out=LO.ap[:, :, 0:64], in_=P.ap[:, :, 1536:1600], func=AF.Tanh), r=[P.b], w=[LO.b])
            S.op("act", lambda e: e.copy(out=LO.ap[:, :, 64:128], in_=P.ap[:, :, 1600:1664]), r=[P.b], w=[LO.b])
            S.op("act", lambda e: e.activation(out=LO.ap[:, :, 128:256], in_=P.ap[:, :, 1664:1792], func=AF.Sigmoid), r=[P.b], w=[LO.b])
            LOT = Tl([128, nch, 2, 64], BF16, "LOT")
            for c in range(nch):
                for j in range(2):
                    S.op("pe", lambda e, c=c, j=j: e.transpose(
                        psb16(0)[:, (c * 2 + j) * 64:(c * 2 + j + 1) * 64], LO.ap[:, c, j * 128:(j + 1) * 128], identb[:64, :64]),
                        r=[LO.b, B_ident], w=[PSB[0]])
            S.op("act", lambda e: e.copy(out=LOT.ap, in_=psb16(0)[:, 0:nch * 128].rearrange("p (c j t) -> p c j t", c=nch, j=2)),
                 r=[PSB[0]], w=[LOT.b])
            SG = Tl([64, nch, 512], F32, "SG")
            AA = Tl([64, nch, 512], F32, "AA")
            GG = Tl([64, nch, 512], F32, "GG")
            for c in range(nch):
                S.op("pe", lambda e, c=c: e.matmul(PS[1][:64, :], lhsT=LOT.ap[0:64, c, 0, :], rhs=LW.ap[0:64, 0, :], start=True, stop=True),
                     r=[LOT.b, LW.b], w=[PSB[1]])
                S.op("pe", lambda e, c=c: e.matmul(PS[2][:64, :], lhsT=LOT.ap[64:128, c, 0, :], rhs=LW.ap[64:128, 0, :], start=True, stop=True),
                     r=[LOT.b, LW.b], w=[PSB[2]])
                S.op("pe", lambda e, c=c: e.matmul(PS[3][:64, :], lhsT=LOT.ap[:, c, 1, :], rhs=LW.ap[:, 1, :], start=True, stop=True),
                     r=[LOT.b, LW.b], w=[PSB[3]])
                S.op("dve", lambda e, c=c: e.tensor_tensor(out=SG.ap[:, c, :], in0=PS[1][:64, :], in1=RV.ap[:, 1792:2304], op=ALU.add),
                     r=[PSB[1], RV.b], w=[SG.b])
                S.op("dve", lambda e, c=c: e.tensor_tensor(out=AA.ap[:, c, :], in0=PS[2][:64, :], in1=RV.ap[:, 2304:2816], op=ALU.add),
                     r=[PSB[2], RV.b], w=[AA.b])
                S.op("act", lambda e, c=c: e.copy(out=GG.ap[:, c, :], in_=PS[3][:64, :]), r=[PSB[3]], w=[GG.b])
            S.op("act", lambda e: e.activation(out=SG.ap, in_=SG.ap, func=AF.Sigmoid), r=[SG.b], w=[SG.b])
            S.op("act", lambda e: e.activation(out=AA.ap, in_=AA.ap, func=AF.Sigmoid), r=[AA.b], w=[AA.b])
            KK = Tl([64, nch, 512], F32, "KK")
            T1 = Tl([64, nch, 512], F32, "T1")
            SS = Tl([64, nch * 8], F32, "SS")
            kr = P.ap[:, :, 512:1024]
            S.op("dve", lambda e: e.tensor_tensor(out=KK.ap, in0=kr, in1=vec(2816, 512, nch), op=ALU.mult), r=[P.b, RV.b], w=[KK.b])
            S.op("dve", lambda e: e.tensor_tensor(out=T1.ap, in0=KK.ap, in1=KK.ap, op=ALU.mult), r=[KK.b], w=[T1.b])
            S.op("dve", lambda e: e.tensor_reduce(out=SS.ap, in_=T1.ap.rearrange("p c (h n) -> p (c h) n", n=64), axis=AX.X, op=ALU.add),
                 r=[T1.b], w=[SS.b])
            S.op("dve", lambda e: e.tensor_scalar(out=SS.ap, in0=SS.ap, scalar1=1e-24, scalar2=None, op0=ALU.add), r=[SS.b], w=[SS.b])
            S.op("act", lambda e: e.sqrt(out=SS.ap, in_=SS.ap), r=[SS.b], w=[SS.b])
            S.op("dve", lambda e: e.reciprocal(out=SS.ap, in_=SS.ap), r=[SS.b], w=[SS.b])
            S.op("dve", lambda e: e.tensor_tensor(
                out=KK.ap.rearrange("p c (h n) -> p (c h) n", n=64), in0=KK.ap.rearrange("p c (h n) -> p (c h) n", n=64),
                in1=SS.ap.unsqueeze(2).to_broadcast([64, nch * 8, 64]), op=ALU.mult), r=[SS.b], w=[KK.b])
            KM = Tl([64, nch, 512], F32, "KM")
            S.op("dve", lambda e: e.scalar_tensor_tensor(out=T1.ap, in0=AA.ap, scalar=-1.0, in1=vec(3328, 512, nch), op0=ALU.add, op1=ALU.mult),
                 r=[AA.b, RV.b], w=[T1.b])
            S.op("dve", lambda e: e.scalar_tensor_tensor(out=KM.ap, in0=T1.ap, scalar=1.0, in1=kr, op0=ALU.add, op1=ALU.mult),
                 r=[T1.b, P.b], w=[KM.b])
            BE = Tl([64, nch, 512], F32, "BE")
            S.op("dve", lambda e: e.tensor_tensor(out=BE.ap, in0=KK.ap, in1=AA.ap, op=ALU.mult), r=[KK.b, AA.b], w=[BE.b])
            T.update(P=P, SG=SG, AA=AA, GG=GG, KK=KK, KM=KM, BE=BE, T1=T1)
            return T

        def sample_part():
            AR.off = mark
            T = d1(TP, 1)
            P, SG, KK, KM, BE = T["P"], T["SG"], T["KK"], T["KM"], T["BE"]
            RS = Tl([64, 8, 6, 64], F32, "RS")

            def hv(ap):
                return ap.rearrange("p c (h n) -> p (c h) n", n=64)
            S.op("act", lambda e: e.activation(out=RS.ap[:, :, 0, :], in_=hv(SG.ap), func=AF.Exp, scale=-C0), r=[SG.b], w=[RS.b])
            S.op("dve", lambda e: e.tensor_copy(out=RS.ap[:, :, 1, :], in_=hv(KM.ap)), r=[KM.b], w=[RS.b])
            S.op("dve", lambda e: e.tensor_copy(out=RS.ap[:, :, 2, :], in_=hv(P.ap[:, :, 1024:1536])), r=[P.b], w=[RS.b])
            S.op("dve", lambda e: e.tensor_scalar(out=RS.ap[:, :, 3, :], in0=hv(KK.ap), scalar1=-1.0, scalar2=None, op0=ALU.mult), r=[KK.b], w=[RS.b])
            S.op("dve", lambda e: e.tensor_copy(out=RS.ap[:, :, 4, :], in_=hv(BE.ap)), r=[BE.b], w=[RS.b])
            S.op("dve", lambda e: e.tensor_copy(out=RS.ap[:, :, 5, :], in_=hv(P.ap[:, :, 0:512])), r=[P.b], w=[RS.b])
            S.op("sp", lambda e: e.dma_start(out=rs_scr, in_=RS.ap.rearrange("p h q n -> p (h q n)")), r=[RS.b], dma="RSo")
            X = Tl([128, 4, 6, 64], F32, "X")
            St = Tl([128, 64, 64], F32, "St")
            TM = Tl([128, 64, 64], F32, "TM")
            SA = Tl([128, 64], F32, "SA")
            YS = Tl([128, 4, 64], F32, "YS")
            S.op("sp", lambda e: e.dma_start(out=St.ap.rearrange("p v k -> p (v k)"), in_=wkv0), w=[St.b], dma="St")
            S.barrier()
            for b in range(16):
                S.op("sp", lambda e, b=b: e.dma_start(
                    out=X.ap[8 * b:8 * b + 8, :, :, :].rearrange("p t q n -> p t (q n)"),
                    in_=rs_scr[4 * b:4 * b + 4, :].rearrange("t (h x) -> h t x", h=8)), w=[X.b], dma="X")
            for t in range(4):
                def bk(q, t=t):
                    return X.ap[:, t, q, :].unsqueeze(1).to_broadcast([128, 64, 64])
                S.op("dve", lambda e, bk=bk: e.tensor_tensor(out=TM.ap, in0=St.ap, in1=bk(3), op=ALU.mult), r=[St.b, X.b], w=[TM.b])
                S.op("dve", lambda e: e.tensor_reduce(out=SA.ap, in_=TM.ap, axis=AX.X, op=ALU.add), r=[TM.b], w=[SA.b])
                S.op("dve", lambda e, bk=bk: e.tensor_tensor(out=St.ap, in0=St.ap, in1=bk(0), op=ALU.mult), r=[X.b], w=[St.b])
                S.op("dve", lambda e, bk=bk: e.tensor_tensor(out=TM.ap, in0=SA.ap.unsqueeze(2).to_broadcast([128, 64, 64]), in1=bk(4), op=ALU.mult),
                     r=[SA.b, X.b], w=[TM.b])
                S.op("dve", lambda e: e.tensor_tensor(out=St.ap, in0=St.ap, in1=TM.ap, op=ALU.add), r=[TM.b], w=[St.b])
                S.op("dve", lambda e, bk=bk, t=t: e.tensor_tensor(out=TM.ap, in0=X.ap[:, t, 2, :].unsqueeze(2).to_broadcast([128, 64, 64]), in1=bk(1), op=ALU.mult),
                     r=[X.b], w=[TM.b])
                S.op("dve", lambda e: e.tensor_tensor(out=St.ap, in0=St.ap, in1=TM.ap, op=ALU.add), r=[TM.b], w=[St.b])
                S.op("dve", lambda e, bk=bk: e.tensor_tensor(out=TM.ap, in0=St.ap, in1=bk(5), op=ALU.mult), r=[St.b, X.b], w=[TM.b])
                S.op("dve", lambda e, t=t: e.tensor_reduce(out=YS.ap[:, t, :], in_=TM.ap, axis=AX.X, op=ALU.add), r=[TM.b], w=[YS.b])
            S.op("sp", lambda e: e.dma_start(out=wkvs_out, in_=St.ap.rearrange("p v k -> p (v k)")), r=[St.b], dma="St")
            S.op("sp", lambda e: e.dma_start(out=ys_scr, in_=YS.ap.rearrange("p t n -> p (t n)")), r=[YS.b], dma="YSo")
            S.barrier()
            YT = Tl([64, 1, 512], F32, "YT")
            for b in range(16):
                S.op("sp", lambda e, b=b: e.dma_start(
                    out=YT.ap[4 * b:4 * b + 4, 0, :].rearrange("t (h n) -> t h n", h=8),
                    in_=ys_scr[8 * b:8 * b + 8, :].rearrange("h (t n) -> t h n", t=4)), w=[YT.b], dma="YT")
            d4(T, YT.ap, YT.b, TP, 1)
            S.barrier()

        GN_EPS = 64e-5

        def hv(ap):
            return ap.rearrange("p c (h n) -> p (c h) n", n=64)

        def d4(T, Yap, Yb, tok0, nch):
            P, AA, GG, KM, T1 = T["P"], T["AA"], T["GG"], T["KM"], T["T1"]
            G = nch * 8
            MU = Tl([64, G], F32, "MU")
            VR = Tl([64, G], F32, "VR")
            YC = Tl([64, nch, 512], F32, "YC")

            def hv(ap):
                return ap.rearrange("p c (h n) -> p c h n", n=64)

            def g3(t_):
                return t_.ap.rearrange("p (c h) -> p c h", h=8)

            def bl(t_):
                return g3(t_).unsqueeze(3).to_broadcast([64, nch, 8, 64])
            S.op("dve", lambda e: e.tensor_reduce(out=g3(MU), in_=hv(Yap), axis=AX.X, op=ALU.add), r=[Yb], w=[MU.b])
            S.op("dve", lambda e: e.tensor_scalar(out=MU.ap, in0=MU.ap, scalar1=1.0 / 64, scalar2=None, op0=ALU.mult), r=[MU.b], w=[MU.b])
            S.op("dve", lambda e: e.tensor_tensor(out=hv(YC.ap), in0=hv(Yap), in1=bl(MU), op=ALU.subtract), r=[Yb, MU.b], w=[YC.b])
            S.op("dve", lambda e: e.tensor_tensor(out=T1.ap, in0=YC.ap, in1=YC.ap, op=ALU.mult), r=[YC.b], w=[T1.b])
            S.op("dve", lambda e: e.tensor_reduce(out=g3(VR), in_=hv(T1.ap), axis=AX.X, op=ALU.add), r=[T1.b], w=[VR.b])
            S.op("dve", lambda e: e.tensor_scalar(out=VR.ap, in0=VR.ap, scalar1=1.0 / 64, scalar2=GN_EPS, op0=ALU.mult, op1=ALU.add), r=[VR.b], w=[VR.b])
            S.op("act", lambda e: e.sqrt(out=VR.ap, in_=VR.ap), r=[VR.b], w=[VR.b])
            S.op("dve", lambda e: e.reciprocal(out=VR.ap, in_=VR.ap), r=[VR.b], w=[VR.b])
            S.op("dve", lambda e: e.tensor_tensor(out=hv(YC.ap), in0=hv(YC.ap), in1=bl(VR), op=ALU.mult), r=[VR.b], w=[YC.b])
            S.op("dve", lambda e: e.tensor_tensor(out=YC.ap, in0=YC.ap, in1=vec(4352, 512, nch), op=ALU.mult), r=[RV.b], w=[YC.b])
            S.op("dve", lambda e: e.tensor_tensor(out=YC.ap, in0=YC.ap, in1=vec(4864, 512, nch), op=ALU.add), r=[RV.b], w=[YC.b])
            S.op("dve", lambda e: e.tensor_tensor(out=T1.ap, in0=P.ap[:, :, 0:512], in1=KM.ap, op=ALU.mult), r=[P.b, KM.b], w=[T1.b])
            S.op("dve", lambda e: e.tensor_tensor(out=T1.ap, in0=T1.ap, in1=vec(3840, 512, nch), op=ALU.mult), r=[RV.b], w=[T1.b])
            S.op("dve", lambda e: e.tensor_reduce(out=g3(MU), in_=hv(T1.ap), axis=AX.X, op=ALU.add), r=[T1.b], w=[MU.b])
            S.op("dve", lambda e: e.tensor_tensor(out=hv(T1.ap), in0=hv(P.ap[:, :, 1024:1536]), in1=bl(MU), op=ALU.mult), r=[P.b, MU.b], w=[T1.b])
            S.op("dve", lambda e: e.tensor_tensor(out=YC.ap, in0=YC.ap, in1=T1.ap, op=ALU.add), r=[T1.b], w=[YC.b])
            S.op("dve", lambda e: e.tensor_tensor(out=YC.ap, in0=YC.ap, in1=GG.ap, op=ALU.mult), r=[GG.b], w=[YC.b])
            S.op("sp", lambda e: e.dma_start(out=orw_scr[tok0:tok0 + 64 * nch, :].rearrange("(c t) n -> t c n", t=64), in_=YC.ap),
                 r=[YC.b], dma="YC")

        bank = [0]

        def nb():
            bank[0] = (bank[0] % 7) + 1
            return bank[0]

        def prompt_part():
            AR.off = mark
            H = [Tl([64, 8, 64], F32, "H0"), Tl([64, 8, 64], F32, "H1")]
            S.op("dve", lambda e: e.memset(H[0].ap, 0.0), w=[H[0].b])
            umark = AR.off
            cur = 0
            I64 = identf[:64, :64]
            for u in range(16):
                AR.off = umark
                T = d1(u * 128, 2)
                P, SG, AA, KK, KM, BE, T1 = T["P"], T["SG"], T["AA"], T["KK"], T["KM"], T["BE"], T["T1"]
                GM = Tl([64, 2, 512], F32, "GM")
                GI = Tl([64, 2, 512], F32, "GI")
                GP = Tl([64, 2, 512], F32, "GP")
                for c in range(2):
                    bk = nb()
                    S.op("pe", lambda e, c=c, bk=bk: e.matmul(PS[bk][:64, :], lhsT=CM.ap[:, 2, :], rhs=SG.ap[:, c, :], start=True, stop=True),
                         r=[CM.b, SG.b], w=[PSB[bk]])
                    S.op("act", lambda e, c=c, bk=bk: e.activation(out=GM.ap[:, c, :], in_=PS[bk][:64, :], func=AF.Exp, scale=-C0), r=[PSB[bk]], w=[GM.b])
                    S.op("act", lambda e, c=c, bk=bk: e.activation(out=GI.ap[:, c, :], in_=PS[bk][:64, :], func=AF.Exp, scale=C0), r=[PSB[bk]], w=[GI.b])
                    S.op("dve", lambda e, c=c, bk=bk: e.tensor_tensor(out=T1.ap[:, c, :], in0=PS[bk][:64, :], in1=SG.ap[:, c, :], op=ALU.subtract),
                         r=[PSB[bk], SG.b], w=[T1.b])
                S.op("act", lambda e: e.activation(out=GP.ap, in_=T1.ap, func=AF.Exp, scale=-C0), r=[T1.b], w=[GP.b])
                AL = Tl([64, 2, 512], F32, "AL")
                BT = Tl([64, 2, 512], F32, "BT")
                KT = Tl([64, 2, 512], F32, "KT")
                RB = Tl([64, 2, 512], F32, "RB")
                S.op("dve", lambda e: e.scalar_tensor_tensor(out=AL.ap, in0=KK.ap, scalar=-1.0, in1=GP.ap, op0=ALU.mult, op1=ALU.mult), r=[KK.b, GP.b], w=[AL.b])
                S.op("dve", lambda e: e.tensor_tensor(out=BT.ap, in0=BE.ap, in1=GI.ap, op=ALU.mult), r=[BE.b, GI.b], w=[BT.b])
                S.op("dve", lambda e: e.tensor_tensor(out=KT.ap, in0=KM.ap, in1=GI.ap, op=ALU.mult), r=[KM.b, GI.b], w=[KT.b])
                S.op("dve", lambda e: e.tensor_tensor(out=RB.ap, in0=P.ap[:, :, 0:512], in1=GM.ap, op=ALU.mult), r=[P.b, GM.b], w=[RB.b])
                GC = Tl([64, 16], F32, "GC")
                bk = nb()
                for c in range(2):
                    for h in range(8):
                        S.op("pe", lambda e, c=c, h=h, bk=bk: e.matmul(
                            PS[bk][:64, c * 8 + h:c * 8 + h + 1], lhsT=GM.ap[:, c, h * 64:(h + 1) * 64], rhs=CM.ap[:, 3, 0:1], start=True, stop=True),
                            r=[GM.b, CM.b], w=[PSB[bk]])
                S.op("act", lambda e, bk=bk: e.copy(out=GC.ap, in_=PS[bk][:64, 0:16]), r=[PSB[bk]], w=[GC.b])
                XT = {}
                for nm, src in (("RB", RB), ("AL", AL), ("BT", BT), ("KT", KT)):
                    dst = Tl([64, 2, 8, 64], F32, nm + "T")
                    XT[nm] = dst
                    for c in range(2):
                        bk = nb()
                        for h in range(8):
                            S.op("pe", lambda e, c=c, h=h, bk=bk, src=src: e.transpose(
                                PS[bk][:64, h * 64:(h + 1) * 64], src.ap[:, c, h * 64:(h + 1) * 64], I64),
                                r=[src.b, B_ident], w=[PSB[bk]])
                        eng = "act" if c == 0 else "dve"
                        if eng == "act":
                            S.op("act", lambda e, c=c, bk=bk, dst=dst: e.copy(out=dst.ap[:, c, :, :], in_=PS[bk][:64, :].rearrange("p (h t) -> p h t", h=8)),
                                 r=[PSB[bk]], w=[dst.b])
                        else:
                            S.op("dve", lambda e, c=c, bk=bk, dst=dst: e.tensor_copy(out=dst.ap[:, c, :, :], in_=PS[bk][:64, :].rearrange("p (h t) -> p h t", h=8)),
                                 r=[PSB[bk]], w=[dst.b])
                RBT, ALT, BTT, KTT = XT["RB"], XT["AL"], XT["BT"], XT["KT"]
                Lm = [Tl([64, 2, 8, 64], F32, "L0"), Tl([64, 2, 8, 64], F32, "L1")]
                Nm = [Tl([64, 2, 8, 64], F32, "N0"), Tl([64, 2, 8, 64], F32, "N1")]
                Pm = [Tl([64, 2, 8, 64], F32, "P0"), Tl([64, 2, 8, 64], F32, "P1")]
                AK = Tl([64, 2, 8, 64], F32, "AK")
                RBm = Tl([64, 2, 8, 64], F32, "RBm")
                RKm = Tl([64, 2, 8, 64], F32, "RKm")
                WT = Tl([64, 2, 8, 64], F32, "WT")
                GT = Tl([64, 2, 8, 64], F32, "GT")

                def mm8(c, lh, rh, rb):
                    bk = nb()
                    for h in range(8):
                        S.op("pe", lambda e, h=h, bk=bk: e.matmul(PS[bk][:64, h * 64:(h + 1) * 64], lhsT=lh(h), rhs=rh(h), start=True, stop=True),
                             r=rb, w=[PSB[bk]])
                    return bk

                def psv(bk):
                    return PS[bk][:64, :].rearrange("p (h t) -> p h t", h=8)

                def mk(m):
                    return CM.ap[:, m, :].unsqueeze(1).to_broadcast([64, 8, 64])
                for c in range(2):
                    for (A_, B_, m, dst) in ((ALT, BTT, 0, Lm[0]), (BTT, ALT, 1, Nm[0]), (ALT, KTT, 0, AK), (BTT, RBT, 2, RBm), (KTT, RBT, 2, RKm)):
                        bk = mm8(c, lambda h, A_=A_, c=c: A_.ap[:, c, h, :], lambda h, B_=B_, c=c: B_.ap[:, c, h, :], [A_.b, B_.b])
                        S.op("dve", lambda e, bk=bk, dst=dst, m=m, c=c: e.tensor_tensor(out=dst.ap[:, c, :, :], in0=psv(bk), in1=mk(m), op=ALU.mult),
                             r=[PSB[bk], CM.b], w=[dst.b])
                    S.op("dve", lambda e, c=c: e.tensor_tensor(out=Pm[0].ap[:, c, :, :], in0=Nm[0].ap[:, c, :, :],
                                                               in1=I64.unsqueeze(1).to_broadcast([64, 8, 64]), op=ALU.add),
                         r=[Nm[0].b, B_ident], w=[Pm[0].b])
                for j in range(1, 6):
                    a, b_ = (j - 1) % 2, j % 2
                    for c in range(2):
                        bk = mm8(c, lambda h, c=c, a=a: Nm[a].ap[:, c, h, :], lambda h, c=c, a=a: Lm[a].ap[:, c, h, :], [Nm[a].b, Lm[a].b])
                        S.op("act", lambda e, bk=bk, c=c, b_=b_: e.copy(out=Lm[b_].ap[:, c, :, :], in_=psv(bk)), r=[PSB[bk]], w=[Lm[b_].b])
                        if j < 5:
                            bk = mm8(c, lambda h, c=c, a=a: Lm[a].ap[:, c, h, :], lambda h, c=c, a=a: Nm[a].ap[:, c, h, :], [Nm[a].b, Lm[a].b])
                            S.op("act", lambda e, bk=bk, c=c, b_=b_: e.copy(out=Nm[b_].ap[:, c, :, :], in_=psv(bk)), r=[PSB[bk]], w=[Nm[b_].b])
                        bk = mm8(c, lambda h, c=c, b_=b_: Lm[b_].ap[:, c, h, :], lambda h, c=c, a=a: Pm[a].ap[:, c, h, :], [Lm[b_].b, Pm[a].b])
                        S.op("dve", lambda e, bk=bk, c=c, a=a, b_=b_: e.tensor_tensor(out=Pm[b_].ap[:, c, :, :], in0=psv(bk), in1=Pm[a].ap[:, c, :, :], op=ALU.add),
                             r=[PSB[bk], Pm[a].b], w=[Pm[b_].b])
                PF = Pm[1]
                for c in range(2):
                    bk = mm8(c, lambda h, c=c: AL.ap[:, c, h * 64:(h + 1) * 64], lambda h, c=c: PF.ap[:, c, h, :], [AL.b, PF.b])
                    S.op("act", lambda e, bk=bk, c=c: e.copy(out=WT.ap[:, c, :, :], in_=psv(bk)), r=[PSB[bk]], w=[WT.b])
                    bk = mm8(c, lambda h, c=c: AK.ap[:, c, h, :], lambda h, c=c: PF.ap[:, c, h, :], [AK.b, PF.b])
                    S.op("dve", lambda e, bk=bk, c=c: e.tensor_copy(out=GT.ap[:, c, :, :], in_=psv(bk)), r=[PSB[bk]], w=[GT.b])
                U = Tl([64, 512], F32, "U")
                Y = Tl([64, 2, 512], F32, "Y")
                for c in range(2):
                    Hc, Hn = H[cur], H[1 - cur]

                    def V(h, c=c):
                        return P.ap[:, c, 1024 + h * 64:1024 + (h + 1) * 64]
                    bu = nb()
                    for h in range(8):
                        S.op("pe", lambda e, h=h, c=c, bu=bu, Hc=Hc: e.matmul(PS[bu][:64, h * 64:(h + 1) * 64], lhsT=WT.ap[:, c, h, :], rhs=Hc.ap[:, h, :], start=True, stop=False),
                             r=[WT.b, Hc.b], w=[PSB[bu]])
                        S.op("pe", lambda e, h=h, c=c, bu=bu, V=V: e.matmul(PS[bu][:64, h * 64:(h + 1) * 64], lhsT=GT.ap[:, c, h, :], rhs=V(h), start=False, stop=True),
                             r=[GT.b, P.b], w=[PSB[bu]])
                    S.op("act", lambda e, bu=bu: e.copy(out=U.ap, in_=PS[bu][:64, :]), r=[PSB[bu]], w=[U.b])
                    bh = nb()
                    for h in range(8):
                        S.op("pe", lambda e, h=h, c=c, bh=bh: e.matmul(PS[bh][:64, h * 64:(h + 1) * 64], lhsT=BT.ap[:, c, h * 64:(h + 1) * 64], rhs=U.ap[:, h * 64:(h + 1) * 64], start=True, stop=False),
                             r=[BT.b, U.b], w=[PSB[bh]])
                        S.op("pe", lambda e, h=h, c=c, bh=bh, V=V: e.matmul(PS[bh][:64, h * 64:(h + 1) * 64], lhsT=KT.ap[:, c, h * 64:(h + 1) * 64], rhs=V(h), start=False, stop=True),
                             r=[KT.b, P.b], w=[PSB[bh]])
                    by = nb()
                    for h in range(8):
                        S.op("pe", lambda e, h=h, c=c, by=by, Hc=Hc: e.matmul(PS[by][:64, h * 64:(h + 1) * 64], lhsT=RBT.ap[:, c, h, :], rhs=Hc.ap[:, h, :], start=True, stop=False),
                             r=[RBT.b, Hc.b], w=[PSB[by]])
                        S.op("pe", lambda e, h=h, c=c, by=by: e.matmul(PS[by][:64, h * 64:(h + 1) * 64], lhsT=RBm.ap[:, c, h, :], rhs=U.ap[:, h * 64:(h + 1) * 64], start=False, stop=False),
                             r=[RBm.b, U.b], w=[PSB[by]])
                        S.op("pe", lambda e, h=h, c=c, by=by, V=V: e.matmul(PS[by][:64, h * 64:(h + 1) * 64], lhsT=RKm.ap[:, c, h, :], rhs=V(h), start=False, stop=True),
                             r=[RKm.b, P.b], w=[PSB[by]])
                    S.op("act", lambda e, by=by, c=c: e.copy(out=Y.ap[:, c, :], in_=PS[by][:64, :]), r=[PSB[by]], w=[Y.b])
                    S.op("dve", lambda e, bh=bh, Hc=Hc, Hn=Hn: e.tensor_tensor(out=Hn.ap, in0=Hc.ap, in1=PS[bh][:64, :].rearrange("p (h v) -> p h v", h=8), op=ALU.add),
                         r=[Hc.b, PSB[bh]], w=[Hn.b])
                    S.op("dve", lambda e, c=c, Hn=Hn: e.tensor_tensor(out=Hn.ap, in0=Hn.ap, in1=GC.ap[:, c * 8:(c + 1) * 8].unsqueeze(2).to_broadcast([64, 8, 64]), op=ALU.mult),
                         r=[GC.b], w=[Hn.b])
                    cur = 1 - cur
                d4(T, Y.ap, Y.b, u * 128, 2)
                S.barrier()
            Hf = H[cur]
            HT = Tl([64, 8, 64], F32, "HT")
            bk = nb()
            for h in range(8):
                S.op("pe", lambda e, h=h, bk=bk: e.transpose(PS[bk][:64, h * 64:(h + 1) * 64], Hf.ap[:, h, :], I64), r=[Hf.b, B_ident], w=[PSB[bk]])
            S.op("act", lambda e, bk=bk: e.copy(out=HT.ap, in_=PS[bk][:64, :].rearrange("p (h k) -> p h k", h=8)), r=[PSB[bk]], w=[HT.b])
            S.op("sp", lambda e: e.dma_start(out=wkvp_out.rearrange("h v k -> v h k"), in_=HT.ap), r=[HT.b], dma="HT")
            S.barrier()

        sample_part()
        if os.environ.get("KDEV_NOPROMPT") is None:
            prompt_part()


    LAM_INIT = 0.8 - 0.6 * math.exp(-0.3 * 0)
    SUBLN_EPS = 1e-5
    LF = 383

    def attn_setup():
        A = {}
        LQ = Tl([128, 4, 64], F32, "LQ")
        S.op("sp", lambda e: e.dma_start(out=LQ.ap.rearrange("p a n -> p (a n)"), in_=lamvec.partition_broadcast(128)), w=[LQ.b], dma="LQ")
        LS = Tl([128, 4], F32, "LS")
        LAM = Tl([128, 2], F32, "LAM")
        PRD = Tl([128, 2, 64], F32, "PRD")
        S.op("dve", lambda e: e.tensor_tensor(out=PRD.ap, in0=LQ.ap[:, 0:2, :], in1=LQ.ap[:, 2:4, :], op=ALU.mult), r=[LQ.b], w=[PRD.b])
        S.op("dve", lambda e: e.tensor_reduce(out=LS.ap[:, 0:2], in_=PRD.ap, axis=AX.X, op=ALU.add), r=[PRD.b], w=[LS.b])
        S.op("act", lambda e: e.activation(out=LS.ap[:, 2:4], in_=LS.ap[:, 0:2], func=AF.Exp), r=[LS.b], w=[LS.b])
        S.op("dve", lambda e: e.tensor_tensor(out=LAM.ap[:, 0:1], in0=LS.ap[:, 2:3], in1=LS.ap[:, 3:4], op=ALU.subtract), r=[LS.b], w=[LAM.b])
        S.op("dve", lambda e: e.tensor_scalar(out=LAM.ap[:, 0:1], in0=LAM.ap[:, 0:1], scalar1=LAM_INIT, scalar2=None, op0=ALU.add), r=[LAM.b], w=[LAM.b])
        S.op("dve", lambda e: e.tensor_scalar(out=LAM.ap[:, 1:2], in0=LAM.ap[:, 0:1], scalar1=-1.0, scalar2=None, op0=ALU.mult), r=[LAM.b], w=[LAM.b])
        AS = int(os.environ.get("KDEV_AS", 9))
        if AS < 1:
            return A
        SUBW = Tl([128, 128], F32, "SUBW")
        S.op("sp", lambda e: e.dma_start(out=SUBW.ap, in_=subln.partition_broadcast(128)), w=[SUBW.b], dma="SUBW")
        S.op("dve", lambda e: e.tensor_scalar(out=SUBW.ap, in0=SUBW.ap, scalar1=1.0 - LAM_INIT, scalar2=None, op0=ALU.mult), r=[SUBW.b], w=[SUBW.b])
        if AS < 2:
            return A
        TE = Tl([33, 4], F32, "TE")
        T31 = Tl([33, 4], F32, "T31")
        OH = Tl([33, LF], F32, "OH")
        S.op("sp", lambda e: e.dma_start(out=TE.ap[0:32, :], in_=relb), w=[TE.b], dma="TE")
        S.op("sp", lambda e: e.dma_start(out=T31.ap[0:32, :], in_=relb[31:32, :].partition_broadcast(32)), w=[T31.b], dma="T31")
        S.op("sp", lambda e: e.dma_start(out=OH.ap, in_=onehot), w=[OH.b], dma="OH")
        S.op("dve", lambda e: e.tensor_tensor(out=TE.ap[0:32, :], in0=TE.ap[0:32, :], in1=T31.ap[0:32, :], op=ALU.subtract), r=[T31.b], w=[TE.b])
        S.op("dve", lambda e: e.memset(TE.ap[32:33, :], -30000.0), w=[TE.b])
        if AS < 3:
            return A
        RR = Tl([128, LF], F32, "RR")
        TEB = Tl([33, 4, 128], F32, "TEB")
        S.op("dve", lambda e: e.tensor_copy(out=TEB.ap, in_=TE.ap.unsqueeze(2).to_broadcast([33, 4, 128])), r=[TE.b], w=[TEB.b])
        for h in range(4):
            S.op("pe", lambda e, h=h: e.matmul(PS[1][:, 0:LF], lhsT=TEB.ap[:, h, :], rhs=OH.ap, start=True, stop=True),
                 r=[TEB.b, OH.b], w=[PSB[1]])
            S.op("act", lambda e: e.copy(out=RR.ap, in_=PS[1][:, 0:LF]), r=[PSB[1]], w=[RR.b])
            S.op("sp", lambda e, h=h: e.dma_start(out=r_scr[h], in_=RR.ap), r=[RR.b], dma="RR")
        S.barrier()
        if AS < 4:
            return A
        BF = Tl([128, 4, 2, 128], F32, "BF")
        BB = Tl([128, 4, 2, 128], BF16, "BB")
        for h in range(4):
            for dl in range(2):
                src = bass.AP(tensor=r_scr.tensor, offset=h * 128 * LF + 127 + 128 * dl, ap=[[LF - 1, 128], [1, 128]])
                S.op("sp", lambda e, h=h, dl=dl, src=src: e.dma_start(out=BF.ap[:, h, dl, :], in_=src), w=[BF.b], dma="BF")
        S.op("dve", lambda e: e.tensor_copy(out=BB.ap, in_=BF.ap), r=[BF.b], w=[BB.b])
        if os.environ.get("KDEV_DBG"):
            bfd = dout("bf_dbg", [128, 4 * 2 * 128])
            S.op("sp", lambda e: e.dma_start(out=bfd, in_=BF.ap.rearrange("p h d q -> p (h d q)")), r=[BF.b], dma="bfd")
        A.update(LAM=LAM, SUBW=SUBW, BB=BB)
        A["mark"] = AR.off
        return A

    def subln_tile(OA, rows, G, junk, st, SUBW):
        S.op("dve", lambda e: e.tensor_tensor(out=junk.ap, in0=OA.ap, in1=OA.ap, op=ALU.mult), r=[OA.b], w=[junk.b])
        S.op("dve", lambda e: e.tensor_reduce(out=st.ap, in_=junk.ap, axis=AX.X, op=ALU.add), r=[junk.b], w=[st.b])
        S.op("dve", lambda e: e.tensor_scalar(out=st.ap, in0=st.ap, scalar1=1.0 / 128, scalar2=SUBLN_EPS, op0=ALU.mult, op1=ALU.add), r=[st.b], w=[st.b])
        S.op("act", lambda e: e.sqrt(out=st.ap, in_=st.ap), r=[st.b], w=[st.b])
        S.op("dve", lambda e: e.reciprocal(out=st.ap, in_=st.ap), r=[st.b], w=[st.b])
        S.op("dve", lambda e: e.tensor_tensor(out=OA.ap, in0=OA.ap, in1=st.ap.unsqueeze(2).to_broadcast([rows, G, 128]), op=ALU.mult), r=[st.b], w=[OA.b])
        S.op("dve", lambda e: e.tensor_tensor(out=OA.ap, in0=OA.ap, in1=SUBW.ap[0:rows, :].unsqueeze(1).to_broadcast([rows, G, 128]), op=ALU.mult),
             r=[SUBW.b], w=[OA.b])

    def attn_prompt(A, qT, kT, Vb):
        LAM, SUBW, BB = A["LAM"], A["SUBW"], A["BB"]
        PT = [Tl([128, 2, 256], BF16, "PT%d" % i) for i in range(2)]
        OA = [Tl([128, 4, 128], F32, "OA%d" % i) for i in range(2)]
        RS2 = Tl([128, 2], F32, "RS2")
        junk = Tl([128, 4, 128], F32, "jk")
        st = Tl([128, 4], F32, "st4")
        sb = [(0, 1), (2, 3)]
        it = [0]
        APL = int(os.environ.get("KDEV_AP", 9))
        for G in range(int(os.environ.get("KDEV_APG0", 0)), int(os.environ.get("KDEV_APG", 8))):
            qt0 = 2 * G
            for h in range(4):
                ob = [4 + 2 * (h % 2), 5 + 2 * (h % 2)]
                for j in range(2):
                    S.op("dve", lambda e, j=j, ob=ob: e.memset(PS[ob[j]][:, :], 0.0), w=[PSB[ob[j]]])
                for kb in range(qt0 + 2):
                    i_ = it[0] % 2
                    it[0] += 1
                    far = kb < qt0 - 1
                    j_lo = 0 if kb <= qt0 else 1
                    for m in range(2):
                        bk = sb[i_][m]
                        kop = kT[m * 64:(m + 1) * 64, h, kb * 128:(kb + 1) * 128]
                        if far:
                            for j in range(2):
                                S.op("pe", lambda e, m=m, bk=bk, kop=kop, h=h, qt0=qt0, j=j: e.matmul(
                                    PS[bk][:, j * 128:(j + 1) * 128], lhsT=kop,
                                    rhs=qT[m * 64:(m + 1) * 64, h, (qt0 + j) * 128:(qt0 + j + 1) * 128],
                                    start=True, stop=True), r=[B_qk], w=[PSB[bk]])
                        else:
                            for j in range(j_lo, 2):
                                qt = qt0 + j
                                dl = qt - kb
                                S.op("pe", lambda e, m=m, bk=bk, kop=kop, h=h, qt=qt, j=j, dl=dl: e.matmul(
                                    PS[bk][:, j * 128:(j + 1) * 128], lhsT=kop,
                                    rhs=qT[m * 64:(m + 1) * 64, h, qt * 128:(qt + 1) * 128], start=True, stop=(dl >= 2)),
                                    r=[B_qk], w=[PSB[bk]])
                                if dl < 2:
                                    S.op("pe", lambda e, m=m, bk=bk, h=h, j=j, dl=dl: e.matmul(
                                        PS[bk][:, j * 128:(j + 1) * 128], lhsT=identb[:, :],
                                        rhs=BB.ap[:, h, dl, :], start=False, stop=True),
                                        r=[BB.b, B_ident], w=[PSB[bk]])
                    pt = PT[i_]
                    for m in range(2):
                        bk = sb[i_][m]
                        S.op("act", lambda e, bk=bk, pt=pt, j_lo=j_lo, m=m: e.activation(
                            out=pt.ap[:, m, j_lo * 128:256], in_=PS[bk][:, j_lo * 128:256], func=AF.Exp),
                            r=[PSB[bk]], w=[pt.b])
                    for j in range(j_lo, 2 if APL >= 2 else 0):
                        qt = qt0 + j
                        for m in range(2):
                            S.op("pe", lambda e, j=j, m=m, pt=pt, kb=kb, h=h, qt=qt, ob=ob: e.matmul(
                                PS[ob[j]][:, m * 256:m * 256 + 129], lhsT=pt.ap[:, m, j * 128:(j + 1) * 128],
                                rhs=Vb[:, kb, h, 0:129], start=False, stop=(kb == qt), skip_group_check=True),
                                r=[pt.b, B_vb], w=[PSB[ob[j]]])
                for j in range(2 if APL >= 3 else 0):
                    oa = OA[j]
                    ov = PS[ob[j]][:, :].rearrange("p (m x) -> p m x", m=2)
                    S.op("dve", lambda e, ov=ov: e.reciprocal(out=RS2.ap, in_=ov[:, :, 128]), r=[PSB[ob[j]]], w=[RS2.b])
                    S.op("dve", lambda e: e.tensor_tensor(out=RS2.ap[:, 1:2], in0=RS2.ap[:, 1:2], in1=LAM.ap[:, 1:2], op=ALU.mult), r=[LAM.b], w=[RS2.b])
                    S.op("dve", lambda e, ov=ov, oa=oa, h=h: e.tensor_scalar(out=oa.ap[:, h, :], in0=ov[:, 0, 0:128], scalar1=RS2.ap[:, 0:1], scalar2=None, op0=ALU.mult),
                         r=[PSB[ob[j]], RS2.b], w=[oa.b])
                    S.op("dve", lambda e, ov=ov, oa=oa, h=h: e.scalar_tensor_tensor(out=oa.ap[:, h, :], in0=ov[:, 1, 0:128], scalar=RS2.ap[:, 1:2], in1=oa.ap[:, h, :],
                                                                                  op0=ALU.mult, op1=ALU.add), r=[PSB[ob[j]], RS2.b], w=[oa.b])
            for j in range(2 if APL >= 4 else 0):
                subln_tile(OA[j], 128, 4, junk, st, SUBW)
                S.op("sp", lambda e, j=j, qt0=qt0: e.dma_start(out=oat_scr[(qt0 + j) * 128:(qt0 + j + 1) * 128, :], in_=OA[j].ap.rearrange("p h d -> p (h d)")),
                     r=[OA[j].b], dma="OA%d" % j)
        S.barrier()

    def attn_sample(A, qT, kT):
        LAM, SUBW, BB = A["LAM"], A["SUBW"], A["BB"]
        AR.off = A["mark"]
        PTB = Tl([128, 256], I32, "PTB")
        IOT = Tl([128, 1], I32, "IOT")
        IDX = Tl([128, 256], I32, "IDX")
        S.op("sp", lambda e: e.dma_start(out=PTB.ap, in_=ptab.partition_broadcast(128)), w=[PTB.b], dma="PTB")
        S.op("sp", lambda e: e.dma_start(out=IOT.ap, in_=iotap), w=[IOT.b], dma="IOT")
        S.op("pool", lambda e: e.tensor_scalar(out=IDX.ap, in0=PTB.ap, scalar1=128, scalar2=IOT.ap[:, 0:1], op0=ALU.mult, op1=ALU.add),
             r=[PTB.b, IOT.b], w=[IDX.b])
        QB = Tl([128, 16, 4, 8], BF16, "QB")
        S.op("dve", lambda e: e.memset(QB.ap, 0.0), w=[QB.b])
        for m in range(2):
            S.op("dve", lambda e, m=m: e.tensor_copy(out=QB.ap[m * 64:(m + 1) * 64, :, :, m * 4:(m + 1) * 4],
                                                     in_=qT[m * 64:(m + 1) * 64, :, TP:TT].rearrange("p h (b t) -> p b h t", t=4)),
                 r=[B_qk], w=[QB.b])
        VN = Tl([4, 16, 512], F32, "VN")
        S.op("sp", lambda e: e.dma_start(out=VN.ap, in_=v_out[TP:TT, :].rearrange("(b t) n -> t b n", t=4)), w=[VN.b], dma="VN")
        ONE = Tl([128, 1], F32, "ONE")
        S.op("dve", lambda e: e.memset(ONE.ap, 1.0), w=[ONE.b])
        BS = Tl([128, 4, 2, 4], BF16, "BS")
        BN = Tl([4, 4, 2, 4], BF16, "BN")
        for m in range(2):
            S.op("dve", lambda e, m=m: e.tensor_copy(out=BS.ap[:, :, m, :], in_=BB.ap[:, :, 1, 0:4]), r=[BB.b], w=[BS.b])
            S.op("dve", lambda e, m=m: e.tensor_copy(out=BN.ap[:, :, m, :], in_=BB.ap[0:4, :, 0, 0:4]), r=[BB.b], w=[BN.b])
        OS = Tl([8, 16, 4, 128], F32, "OS")
        SMs = Tl([8, 16, 4], F32, "SMs")
        SEL = Tl([8, 3, 4], F32, "SEL")
        mark2 = AR.off
        NS = 4
        KP = [Tl([128, 512], F32, "KP%d" % i) for i in range(NS)]
        VP = [Tl([128, 512], F32, "VP%d" % i) for i in range(NS)]
        KTb = [Tl([128, 4, 128], BF16, "KTb%d" % i) for i in range(2)]
        PTs = Tl([128, 512], F32, "PTs")
        PTN = Tl([4, 32], F32, "PTN")
        PSJ = Tl([128, 32], F32, "PSJ")
        cnt = [0]
        for b in range(16):
            sbk, obk, mbk = 1, 2, 3
            for j in range(16):
                sl = cnt[0] % NS
                cnt[0] += 1
                col = b * 16 + j
                S.op("pool", lambda e, sl=sl, col=col: e.indirect_dma_start(
                    out=KP[sl].ap, out_offset=None, in_=ck2, in_offset=bass.IndirectOffsetOnAxis(ap=IDX.ap[:, col:col + 1], axis=0)),
                    r=[IDX.b], w=[KP[sl].b], dma="KP%d" % sl)
                tb = 4 + (j % 2)
                for h in range(4):
                    S.op("pe", lambda e, h=h, tb=tb, sl=sl: e.transpose(PS[tb][:, h * 128:(h + 1) * 128], KP[sl].ap[:, h * 128:(h + 1) * 128], identf[:, :]),
                         r=[KP[sl].b, B_ident], w=[PSB[tb]])
                kt = KTb[j % 2]
                if j % 2 == 0:
                    S.op("act", lambda e, tb=tb, kt=kt: e.copy(out=kt.ap, in_=PS[tb][:, :].rearrange("p (h k) -> p h k", h=4)), r=[PSB[tb]], w=[kt.b])
                else:
                    S.op("dve", lambda e, tb=tb, kt=kt: e.tensor_copy(out=kt.ap, in_=PS[tb][:, :].rearrange("p (h k) -> p h k", h=4)), r=[PSB[tb]], w=[kt.b])
                if j == 15:
                    S.op("pe", lambda e: e.matmul(PS[sbk][:, 480:512], lhsT=identb[:, :], rhs=BS.ap.rearrange("p h m t -> p (h m t)"), start=True, stop=False),
                         r=[BS.b, B_ident], w=[PSB[sbk]])
                for h in range(4):
                    S.op("pe", lambda e, h=h, j=j, kt=kt, b=b: e.matmul(PS[sbk][:, j * 32 + h * 8:j * 32 + (h + 1) * 8], lhsT=kt.ap[:, h, :], rhs=QB.ap[:, b, h, :],
                                                                        start=(j != 15), stop=True, skip_group_check=True), r=[kt.b, QB.b], w=[PSB[sbk]])
            S.op("pe", lambda e: e.matmul(PS[mbk][0:4, 0:32], lhsT=identb[0:4, 0:4], rhs=BN.ap.rearrange("p h m t -> p (h m t)"), start=True, stop=False),
                 r=[BN.b, B_ident], w=[PSB[mbk]])
            for h in range(4):
                S.op("pe", lambda e, h=h, b=b: e.matmul(PS[mbk][0:4, h * 8:(h + 1) * 8], lhsT=kT[:, h, TP + 4 * b:TP + 4 * b + 4], rhs=QB.ap[:, b, h, :],
                                                        start=False, stop=True, skip_group_check=True), r=[B_qk, QB.b], w=[PSB[mbk]])
            S.op("act", lambda e: e.activation(out=PTs.ap, in_=PS[sbk][:, :], func=AF.Exp), r=[PSB[sbk]], w=[PTs.b])
            S.op("act", lambda e: e.activation(out=PTN.ap, in_=PS[mbk][0:4, 0:32], func=AF.Exp), r=[PSB[mbk]], w=[PTN.b])
            S.op("dve", lambda e: e.tensor_reduce(out=PSJ.ap, in_=PTs.ap.rearrange("p (j x) -> p x j", j=16), axis=AX.X, op=ALU.add), r=[PTs.b], w=[PSJ.b])
            for h in range(4):
                S.op("pe", lambda e, h=h: e.matmul(PS[mbk][0:8, 64 + h:65 + h], lhsT=PSJ.ap[:, h * 8:(h + 1) * 8], rhs=ONE.ap[:, 0:1], start=True, stop=False),
                     r=[PSJ.b, ONE.b], w=[PSB[mbk]])
                S.op("pe", lambda e, h=h: e.matmul(PS[mbk][0:8, 64 + h:65 + h], lhsT=PTN.ap[:, h * 8:(h + 1) * 8], rhs=ONE.ap[0:4, 0:1], start=False, stop=True),
                     r=[PTN.b, ONE.b], w=[PSB[mbk]])
            S.op("act", lambda e, b=b: e.copy(out=SMs.ap[:, b, :], in_=PS[mbk][0:8, 64:68]), r=[PSB[mbk]], w=[SMs.b])
            S.op("dve", lambda e: e.memset(PS[obk][0:8, :], 0.0), w=[PSB[obk]])
            for j in range(16):
                sl = cnt[0] % NS
                cnt[0] += 1
                col = b * 16 + j
                S.op("pool", lambda e, sl=sl, col=col: e.indirect_dma_start(
                    out=VP[sl].ap, out_offset=None, in_=cv2, in_offset=bass.IndirectOffsetOnAxis(ap=IDX.ap[:, col:col + 1], axis=0)),
                    r=[IDX.b], w=[VP[sl].b], dma="VP%d" % sl)
                for h in range(4):
                    S.op("pe", lambda e, h=h, j=j, sl=sl: e.matmul(PS[obk][0:8, h * 128:(h + 1) * 128], lhsT=PTs.ap[:, j * 32 + h * 8:j * 32 + (h + 1) * 8],
                                                                  rhs=VP[sl].ap[:, h * 128:(h + 1) * 128], start=False, stop=False, skip_group_check=True),
                         r=[PTs.b, VP[sl].b], w=[PSB[obk]])
            for h in range(4):
                S.op("pe", lambda e, h=h, b=b: e.matmul(PS[obk][0:8, h * 128:(h + 1) * 128], lhsT=PTN.ap[:, h * 8:(h + 1) * 8],
                                                        rhs=VN.ap[:, b, h * 128:(h + 1) * 128], start=False, stop=True, skip_group_check=True),
                     r=[PTN.b, VN.b], w=[PSB[obk]])
            S.op("act", lambda e, b=b: e.copy(out=OS.ap[:, b, :, :], in_=PS[obk][0:8, :].rearrange("p (h d) -> p h d", h=4)), r=[PSB[obk]], w=[OS.b])
        S.op("dve", lambda e: e.reciprocal(out=SMs.ap, in_=SMs.ap), r=[SMs.b], w=[SMs.b])
        S.op("dve", lambda e: e.tensor_tensor(out=OS.ap, in0=OS.ap, in1=SMs.ap.unsqueeze(3).to_broadcast([8, 16, 4, 128]), op=ALU.mult), r=[SMs.b], w=[OS.b])
        S.op("sp", lambda e: e.dma_start(out=SEL.ap[:, 0:2, :], in_=selc), w=[SEL.b], dma="SEL")
        S.op("dve", lambda e: e.scalar_tensor_tensor(out=SEL.ap[:, 2, :], in0=SEL.ap[:, 1, :], scalar=LAM.ap[0:8, 0:1], in1=SEL.ap[:, 0, :], op0=ALU.mult, op1=ALU.add),
             r=[LAM.b], w=[SEL.b])
        S.barrier()
        AR.off = mark2
        OC = Tl([4, 64, 128], F32, "OC")
        osf = OS.ap.rearrange("p b h d -> p (b h d)")
        ocf = OC.ap.rearrange("p g d -> p (g d)")
        for ch in range(16):
            S.op("pe", lambda e, ch=ch: e.matmul(PS[5][0:4, :], lhsT=SEL.ap[:, 2, :], rhs=osf[:, ch * 512:(ch + 1) * 512], start=True, stop=True),
                 r=[SEL.b, OS.b], w=[PSB[5]])
            S.op("act", lambda e, ch=ch: e.copy(out=ocf[:, ch * 512:(ch + 1) * 512], in_=PS[5][0:4, :]), r=[PSB[5]], w=[OC.b])
        jk = Tl([4, 64, 128], F32, "jk2")
        st = Tl([4, 64], F32, "st64")
        subln_tile(OC, 4, 64, jk, st, SUBW)
        S.op("sp", lambda e: e.dma_start(out=oat_scr[TP:TT, :].rearrange("(b t) n -> t b n", t=4), in_=OC.ap.rearrange("p (b h) d -> p b (h d)", h=4)),
             r=[OC.b], dma="OC")
        S.barrier()

    def merge_phase():
        AR.off = 0
        Wo = AR.alloc([128, 8, D], BF16)
        B_wo = Buf("wo")
        S.op("pool", lambda e: e.dma_start(out=Wo, in_=w_out.rearrange("(k p) n -> p k n", p=128)), w=[B_wo], dma="wg0")
        GA = AR.alloc([128, 2, D], F32)
        load_gain(GA, 0, 3)
        CAT = [Tl([128, D], F32, "CAT%d" % i) for i in range(2)]
        HR = [Tl([128, D], F32, "HR%d" % i) for i in range(2)]
        CB = Tl([128, D], BF16, "CB")
        CT = Tl([128, 8, 128], BF16, "CT")
        MM = Tl([128, D], F32, "MM")
        junk = Tl([128, D], BF16, "junk")
        st = Tl([128, 4], F32, "st")
        for i, (t0, rows) in enumerate(tiles_all):
            sl = i % 2
            cat, hr = CAT[sl], HR[sl]
            S.op("sp", lambda e, cat=cat, t0=t0, rows=rows: e.dma_start(out=cat.ap[:rows, 0:512], in_=orw_scr[t0:t0 + rows, :]), w=[cat.b], dma="CAT%d" % sl)
            S.op("sp", lambda e, cat=cat, t0=t0, rows=rows: e.dma_start(out=cat.ap[:rows, 512:1024], in_=oat_scr[t0:t0 + rows, :]), w=[cat.b], dma="CAT%d" % sl)
            S.op("sp", lambda e, hr=hr, t0=t0, rows=rows: e.dma_start(out=hr.ap[:rows, :], in_=h_scr[t0:t0 + rows, :]), w=[hr.b], dma="HR%d" % sl)
            S.op("dve", lambda e, cat=cat, rows=rows: e.tensor_copy(out=CB.ap[:rows, :], in_=cat.ap[:rows, :]), r=[cat.b], w=[CB.b])
            for kc in range(8):
                S.op("pe", lambda e, kc=kc, rows=rows: e.transpose(psb16(0)[:, kc * 128:kc * 128 + rows], CB.ap[:rows, kc * 128:(kc + 1) * 128], identb[:rows, :rows]),
                     r=[CB.b, B_ident], w=[PSB[0]])
            S.op("act", lambda e, rows=rows: e.copy(out=CT.ap[:, :, 0:rows], in_=psb16(0).rearrange("p (k t) -> p k t", k=8)[:, :, 0:rows]), r=[PSB[0]], w=[CT.b])
            for nh in range(2):
                pc = 1 + nh
                for kc in range(8):
                    S.op("pe", lambda e, kc=kc, nh=nh, pc=pc, rows=rows: e.matmul(PS[pc][:rows, :], lhsT=CT.ap[:, kc, 0:rows], rhs=Wo[:, kc, nh * 512:(nh + 1) * 512],
                                                                                start=(kc == 0), stop=(kc == 7)), r=[CT.b, B_wo], w=[PSB[pc]])
                S.op("act", lambda e, nh=nh, pc=pc, rows=rows: e.copy(out=MM.ap[:rows, nh * 512:(nh + 1) * 512], in_=PS[pc][:rows, :]), r=[PSB[pc]], w=[MM.b])
            rms_rstd(MM.ap[:rows, :], rows, junk.ap[:rows, :], st.ap[:rows, 0:1], st.ap[:rows, 1:2], [MM.b], junk.b, st.b)
            S.op("dve", lambda e, rows=rows: e.scalar_tensor_tensor(out=MM.ap[:rows, :], in0=MM.ap[:rows, :], scalar=st.ap[:rows, 1:2], in1=GA[:rows, 0, :],
                                                                     op0=ALU.mult, op1=ALU.mult), r=[st.b, B_GA[0]], w=[MM.b])
            S.op("dve", lambda e, rows=rows, hr=hr: e.tensor_tensor(out=hr.ap[:rows, :], in0=hr.ap[:rows, :], in1=MM.ap[:rows, :], op=ALU.add), r=[MM.b], w=[hr.b])
            S.op("sp", lambda e, rows=rows, hr=hr, t0=t0: e.dma_start(out=h2_scr[t0:t0 + rows, :], in_=hr.ap[:rows, :]), r=[hr.b], dma="HR%d" % sl)
        S.barrier()

    B_qk = Buf("qk")
    B_vb = Buf("vb")

    ffn_phase("f1", x_all, h_scr, ff1_in, ff1_out, 0, 1)
    AR.off = 0
    qT = AR.alloc([128, 4, TT], BF16)
    kT = AR.alloc([128, 4, TT], BF16)
    Vb = AR.alloc([128, 16, 4, 132], BF16)
    attn_mark = AR.off
    if os.environ.get("KDEV_PROJ", "1") == "1":
        proj_phase(qT, kT, Vb)

    SKIP = os.environ.get("KDEV_SKIP", "")
    if STAGE >= 3:
        AR.off = attn_mark
        A = attn_setup()
        if "P" not in SKIP:
            attn_prompt(A, qT, kT, Vb)
        if os.environ.get("KDEV_NOSAMP") is None:
            attn_sample(A, qT, kT)
    if STAGE >= 2 and "R" not in SKIP:
        rwkv_phase()
    if STAGE >= 3:
        if "M" not in SKIP:
            merge_phase()
        if "F" not in SKIP:
            ffn_phase("f2", h2_scr, y_all, ff2_in, ff2_out, 4, 5)

    S.barrier()
    S.emit()
    return nc


_CACHE = {}


def _bucket(n):
    me = 16
    nf = np.maximum(n, 1).astype(np.float32)
    large = me + (np.log(nf / np.float32(me)) / np.float32(math.log(128 / me)) * np.float32(32 - me)).astype(np.int32)
    large = np.minimum(large, 31)
    return np.where(n < me, n, large)


def _onehot():
    n = np.arange(-127, 256)
    oh = np.zeros((33, 383), np.float32)
    bk = _bucket(np.maximum(n, 0))
    for j, nn in enumerate(n):
        if nn < 0:
            oh[32, j] = 1.0
        else:
            oh[bk[j], j] = 1.0
    return oh


def _selc():
    m = np.zeros((8, 2, 4), np.float32)
    for t in range(4):
        m[t, 0, t] = 1.0
        m[4 + t, 1, t] = -1.0
    return m


def _cmask():
    t = np.arange(64)
    m = np.zeros((64, 4, 64), np.float32)
    m[:, 0, :] = (t[:, None] > t[None, :])
    m[:, 1, :] = (t[:, None] < t[None, :])
    m[:, 2, :] = (t[:, None] <= t[None, :])
    m[63, 3, 0] = 1.0
    return m


def kernel(**inp):
    f32 = np.float32
    if "nc" not in _CACHE:
        _CACHE["nc"] = build_program()
    nc = _CACHE["nc"]
    xp = np.asarray(inp["x_prompt"], f32)
    xs = np.asarray(inp["x_sample"], f32)
    shared = {
        "gains": np.ascontiguousarray(np.asarray(inp["norm_gains"], f32)[0]),
        "ff1_in": np.ascontiguousarray(np.asarray(inp["ff1_in"], f32)[0]),
        "ff1_out": np.ascontiguousarray(np.asarray(inp["ff1_out"], f32)[0]),
        "ff2_in": np.ascontiguousarray(np.asarray(inp["ff2_in"], f32)[0]),
        "ff2_out": np.ascontiguousarray(np.asarray(inp["ff2_out"], f32)[0]),
        "w_in": np.ascontiguousarray(np.asarray(inp["w_in"], f32)[0]),
        "w_out": np.ascontiguousarray(np.asarray(inp["w_out"], f32)[0]),
        "identf": np.eye(128, dtype=f32),
        "rwvec": np.ascontiguousarray(np.concatenate([np.asarray(inp[k], f32).reshape(-1) for k in
                 ("rw_mu", "rw_w0", "rw_a0", "rw_kk", "rw_ka", "rw_rk", "rw_gn_w", "rw_gn_b")])[None, :]),
        "cmask": _cmask(),
        "lamvec": np.ascontiguousarray(np.concatenate([np.asarray(inp[k], f32).reshape(-1) for k in ("da_lq1", "da_lq2", "da_lk1", "da_lk2")])[None, :]),
        "subln": np.ascontiguousarray(np.asarray(inp["da_subln"], f32).reshape(1, 128)),
        "relb": np.ascontiguousarray(np.asarray(inp["rel_bias_table"], f32)),
        "onehot": _onehot(),
        "selc": _selc(),
        "iotap": np.arange(128, dtype=np.int32).reshape(128, 1),
    }
    if STAGE >= 3 and os.environ.get("KDEV_NOSAMP") is None:
        shared["ck2"] = np.asarray(inp["cache_k"], f32).reshape(NPOOL * 128, 512)
        shared["cv2"] = np.asarray(inp["cache_v"], f32).reshape(NPOOL * 128, 512)
    shared.update({
        "rw_w2": np.ascontiguousarray(np.asarray(inp["rw_w2"], f32)[0]),
        "rw_a2": np.ascontiguousarray(np.asarray(inp["rw_a2"], f32)[0]),
        "rw_g2": np.ascontiguousarray(np.asarray(inp["rw_g2"], f32)[0]),
    })
    ssh = np.asarray(inp["state_shift"], f32)[0]
    swkv = np.asarray(inp["state_wkv"], f32)[0]
    in_maps = []
    for c in range(NCORES):
        m = dict(shared)
        m["ptab"] = np.ascontiguousarray(np.asarray(inp["page_table"], np.int32)[16 * c:16 * c + 16].reshape(1, 256))
        m["shift0"] = np.ascontiguousarray(ssh[16 * c:16 * c + 16])
        m["wkv0"] = np.ascontiguousarray(swkv[16 * c:16 * c + 16].reshape(128, 4096))
        m["x_all"] = np.ascontiguousarray(np.concatenate([xp[c], xs[16 * c:16 * c + 16].reshape(TS, D)], axis=0))
        in_maps.append(m)
    ncr = int(os.environ.get("KDEV_CORES", NCORES))
    t0 = time.time()
    res = run_bass_kernel_spmd(nc, in_maps[:ncr], core_ids=list(range(ncr)))
    if os.environ.get("KDEV_CORES"):
        print("run time", time.time() - t0)
    R = list(res.results) + [res.results[0]] * (NCORES - ncr)
    if os.environ.get("KDEV_DBG"):
        _CACHE["R"] = R
    y_prompt = np.stack([R[c]["y_all"][:TP] for c in range(NCORES)], 0)
    y_sample = np.concatenate([R[c]["y_all"][TP:].reshape(16, 4, D) for c in range(NCORES)], 0)
    k_prompt = np.stack([R[c]["k_out"][:TP].reshape(TP, 4, 2, 64) for c in range(NCORES)], 0)[None]
    v_prompt = np.stack([R[c]["v_out"][:TP].reshape(TP, 4, 128) for c in range(NCORES)], 0)[None]
    k_sample = np.concatenate([R[c]["k_out"][TP:].reshape(16, 4, 4, 2, 64) for c in range(NCORES)], 0)[None]
    v_sample = np.concatenate([R[c]["v_out"][TP:].reshape(16, 4, 4, 128) for c in range(NCORES)], 0)[None]
    wkv_prompt = np.stack([R[c]["wkvp_out"] for c in range(NCORES)], 0)[None]
    wkv_sample = np.concatenate([R[c]["wkvs_out"].reshape(16, 8, 64, 64) for c in range(NCORES)], 0)[None]
    shift_prompt = np.concatenate([R[c]["shp_out"] for c in range(NCORES)], 0)[None]
    shift_sample = np.concatenate([R[c]["shs_out"] for c in range(NCORES)], 0)[None]
    outs = (y_prompt, y_sample, k_prompt, v_prompt, k_sample, v_sample, wkv_prompt, wkv_sample, shift_prompt, shift_sample)
    return tuple(np.ascontiguousarray(o, dtype=f32) for o in outs)
```

```python
import math
import os
import time
import numpy as np
import concourse.bass as bass
import concourse.mybir as mybir
from concourse.bass_utils import run_bass_kernel_spmd

F32 = mybir.dt.float32
BF16 = mybir.dt.bfloat16
I32 = mybir.dt.int32
AF = mybir.ActivationFunctionType
ALU = mybir.AluOpType
AX = mybir.AxisListType

NCORES = 8
D = 1024
TP = 2048
TS = 64
TT = TP + TS
DFF = 2816
NFC = 22
PROJ = 3328
RCOLS = 1792
NPOOL = 2560
EPS = 1e-6

STAGE = int(os.environ.get('KDEV_STAGE', 3))


class Buf:
    __slots__ = ("name", "w", "r", "x")

    def __init__(self, name, excl=False):
        self.name = name
        self.w = None
        self.r = {}
        self.x = excl


class Sched:
    def __init__(self, nc):
        self.nc = nc
        self.prog = {k: [] for k in ("pe", "act", "dve", "pool", "sp")}
        self.csem = {k: nc.alloc_semaphore("c_" + k) for k in ("pe", "act", "dve", "pool")}
        self.cnt = {k: 0 for k in self.csem}
        self.known = {k: {} for k in self.prog}
        self.dsem = {}
        self.sems = {}

    def _dsem(self, name):
        if name not in self.dsem:
            s = self.nc.alloc_semaphore("d_" + name)
            self.dsem[name] = [s, 0]
        return self.dsem[name]

    def op(self, e, fn, r=(), w=(), dma=None):
        deps = []
        for b in r:
            if b.w is not None:
                deps.append(b.w)
            if b.x:
                deps.extend(b.r.values())
        for b in w:
            if b.w is not None:
                deps.append(b.w)
            deps.extend(b.r.values())
        waits = {}
        for (key, sem, val, prod) in deps:
            if prod == "pe" and e == "pe" and dma is None:
                continue
            if self.known[e].get(key, 0) >= val:
                continue
            if key not in waits or waits[key][1] < val:
                waits[key] = (sem, val)
        for key, (sem, val) in waits.items():
            self.known[e][key] = val
            self.prog[e].append(("wait", sem, val))
        if dma is None:
            self.cnt[e] += 1
            tok = ("c_" + e, self.csem[e], self.cnt[e], e)
            self.prog[e].append(("op", fn, self.csem[e], 1))
        else:
            d = self._dsem(dma)
            d[1] += 16
            tok = ("d_" + dma, d[0], d[1], None)
            self.prog[e].append(("op", fn, d[0], 16))
        for b in r:
            old = b.r.get(tok[0])
            if old is None or old[2] < tok[2]:
                b.r[tok[0]] = tok
        for b in w:
            b.w = tok
            b.r = {}
        return tok

    def barrier(self):
        for e in self.prog:
            for k in self.csem:
                if self.cnt[k] > self.known[e].get("c_" + k, 0):
                    self.known[e]["c_" + k] = self.cnt[k]
                    self.prog[e].append(("wait", self.csem[k], self.cnt[k]))
            for name, (sem, val) in self.dsem.items():
                if val > self.known[e].get("d_" + name, 0):
                    self.known[e]["d_" + name] = val
                    self.prog[e].append(("wait", sem, val))

    def emit(self):
        nc = self.nc
        engs = {"pe": "tensor", "act": "scalar", "dve": "vector", "pool": "gpsimd", "sp": "sync"}
        with nc.Block() as block:
            for k, attr in engs.items():
                prog = self.prog[k]

                def body(eng, prog=prog):
                    for it in prog:
                        if it[0] == "wait":
                            eng.wait_ge(it[1], it[2])
                        else:
                            it[1](eng).then_inc(it[2], it[3])

                getattr(block, attr)(body)


def build_program():
    nc = bass.Bass("TRN2", target_bir_lowering=False)
    S = Sched(nc)

    def din(name, shape, dt=F32):
        return nc.dram_tensor(name, list(shape), dt, kind="ExternalInput").ap()

    def dout(name, shape, dt=F32):
        return nc.dram_tensor(name, list(shape), dt, kind="ExternalOutput").ap()

    def dscr(name, shape, dt=F32):
        return nc.dram_tensor(name, list(shape), dt, kind="Internal").ap()

    x_all = din("x_all", [TT, D])
    gains = din("gains", [6, D])
    ff1_in = din("ff1_in", [D, 2 * DFF])
    ff1_out = din("ff1_out", [DFF, D])
    ff2_in = din("ff2_in", [D, 2 * DFF])
    ff2_out = din("ff2_out", [DFF, D])
    w_in = din("w_in", [D, PROJ])
    w_out = din("w_out", [D, D])
    identf_d = din("identf", [128, 128])

    y_all = dout("y_all", [TT, D])
    k_out = dout("k_out", [TT, 512])
    v_out = dout("v_out", [TT, 512])
    shp_out = dout("shp_out", [1, RCOLS])
    shs_out = dout("shs_out", [16, RCOLS])
    wkvp_out = dout("wkvp_out", [8, 64, 64])
    wkvs_out = dout("wkvs_out", [128, 4096])

    rwvec = din("rwvec", [1, 5376])
    cmask = din("cmask", [64, 4, 64])
    rw_w2 = din("rw_w2", [64, 512])
    rw_a2 = din("rw_a2", [64, 512])
    rw_g2 = din("rw_g2", [128, 512])
    shift0 = din("shift0", [16, RCOLS])
    wkv0 = din("wkv0", [128, 4096])
    prev_scr = dscr("prev_scr", [TT, RCOLS])
    lamvec = din("lamvec", [1, 256])
    subln = din("subln", [1, 128])
    relb = din("relb", [32, 4])
    onehot = din("onehot", [33, 383])
    selc = din("selc", [8, 2, 4])
    ptab = din("ptab", [1, 256], I32)
    iotap = din("iotap", [128, 1], I32)
    HAVE_SAMP = STAGE >= 3 and os.environ.get("KDEV_NOSAMP") is None
    ck2 = din("ck2", [NPOOL * 128, 512]) if HAVE_SAMP else None
    cv2 = din("cv2", [NPOOL * 128, 512]) if HAVE_SAMP else None
    r_scr = dscr("r_scr", [4, 128, 383])
    oat_scr = (dout if os.environ.get("KDEV_DBG") else dscr)("oat_scr", [TT, 512])
    h2_scr = (dout if os.environ.get("KDEV_DBG") else dscr)("h2_scr", [TT, D])
    rs_scr = dscr("rs_scr", [64, 3072])
    ys_scr = dscr("ys_scr", [128, 256])
    orw_scr = (dout if os.environ.get("KDEV_DBG") else dscr)("orw_scr", [TT, 512])
    h_scr = dout("h_scr", [TT, D]) if os.environ.get("KDEV_DBG") else dscr("h_scr", [TT, D])
    pr_scr = dscr("pr_scr", [TT, RCOLS])

    NW = 52000
    BIG = nc.alloc_sbuf_tensor("BIG", [128, NW], F32)
    identf = nc.alloc_sbuf_tensor("identf_sb", [128, 128], F32)
    identb = nc.alloc_sbuf_tensor("identb_sb", [128, 128], BF16)

    class Arena:
        def __init__(self):
            self.off = 0

        def alloc(self, shape, dt=F32):
            n = 1
            for d_ in shape[1:]:
                n *= d_
            words = n if dt in (F32, I32) else (n + 1) // 2
            assert self.off + words <= NW, ("arena overflow", self.off, words)
            v = BIG[:, self.off:self.off + words]
            self.off += words
            if dt != F32:
                v = v.bitcast(dt)[:, 0:n]
            if len(shape) > 2:
                names = " ".join("a%d" % i for i in range(len(shape) - 1))
                kw = {"a%d" % i: shape[i + 1] for i in range(len(shape) - 1)}
                v = v.rearrange("p (%s) -> p %s" % (names, names), **kw)
            if shape[0] < 128:
                v = v[0:shape[0]]
            return v

    AR = Arena()
    PS = [nc.alloc_psum_tensor("ps%d" % i, [128, 512], F32) for i in range(8)]
    PSB = [Buf("ps%d" % i, True) for i in range(8)]
    B_ident = Buf("ident")
    B_GA = [Buf("ga0"), Buf("ga1")]
    GAh = [None]

    def psb16(i):
        return PS[i][:].bitcast(BF16)

    S.op("sp", lambda e: e.dma_start(out=identf[:], in_=identf_d), w=[B_ident], dma="ident")
    S.op("dve", lambda e: e.tensor_copy(out=identb[:], in_=identf[:]), r=[B_ident], w=[B_ident])

    def load_gain(GA, slot, idx):
        S.op("sp", lambda e: e.dma_start(out=GA[:, slot, :], in_=gains[idx:idx + 1, :].partition_broadcast(128)),
             w=[B_GA[slot]], dma="ga%d" % slot)

    tiles_all = [(t * 128, 128) for t in range(16)] + [(TP, TS)]
    groups = [tiles_all[0:4], tiles_all[4:8], tiles_all[8:12], tiles_all[12:16], tiles_all[16:17]]
    if os.environ.get("KDEV_NG"):
        groups = groups[:int(os.environ["KDEV_NG"])]

    def rms_rstd(src_ap, rows, junk_ap, ss_ap, rstd_ap, rbufs, junk_b, st_b, eps=EPS, n=D):
        S.op("act", lambda e: e.activation(out=junk_ap, in_=src_ap, func=AF.Square, accum_out=ss_ap),
             r=rbufs, w=[junk_b, st_b])
        S.op("dve", lambda e: e.tensor_scalar(out=rstd_ap, in0=ss_ap, scalar1=1.0 / n, scalar2=eps,
                                              op0=ALU.mult, op1=ALU.add), r=[st_b], w=[st_b])
        S.op("act", lambda e: e.sqrt(out=rstd_ap, in_=rstd_ap), r=[st_b], w=[st_b])
        S.op("dve", lambda e: e.reciprocal(out=rstd_ap, in_=rstd_ap), r=[st_b], w=[st_b])

    def ffn_phase(tag, src_d, dst_d, w_in_d, w_out_d, gi, go):
        AR.off = 0
        Win = AR.alloc([128, 8, 2 * DFF], BF16)
        Wout = AR.alloc([128, NFC, D], BF16)
        GA = AR.alloc([128, 2, D], F32)
        B_wg = [Buf("wg%d" % i) for i in range(11)]
        B_wu = [Buf("wu%d" % i) for i in range(11)]
        B_wo = [Buf("wo%d" % i) for i in range(11)]
        w_in_v = w_in_d.rearrange("(k p) n -> p k n", p=128)
        w_out_v = w_out_d.rearrange("(f p) n -> p f n", p=128)
        for i in range(11):
            S.op("pool", lambda e, i=i: e.dma_start(out=Win[:, :, i * 256:(i + 1) * 256],
                                                     in_=w_in_v[:, :, i * 256:(i + 1) * 256]),
                 w=[B_wg[i]], dma="wg%d" % i)
            S.op("pool", lambda e, i=i: e.dma_start(out=Win[:, :, DFF + i * 256:DFF + (i + 1) * 256],
                                                     in_=w_in_v[:, :, DFF + i * 256:DFF + (i + 1) * 256]),
                 w=[B_wu[i]], dma="wu%d" % i)
        for i in range(11):
            S.op("pool", lambda e, i=i: e.dma_start(out=Wout[:, 2 * i:2 * i + 2, :], in_=w_out_v[:, 2 * i:2 * i + 2, :]),
                 w=[B_wo[i]], dma="wo%d" % i)
        load_gain(GA, 0, gi)
        load_gain(GA, 1, go)

        xld = [AR.alloc([128, D], F32) for i in range(2)]
        B_xld = [Buf("xld0"), Buf("xld1")]
        xres = [AR.alloc([128, D], F32)] * 2
        B_xres = [Buf("xres0")] * 2
        junk = AR.alloc([128, D], BF16)
        B_junk = Buf("junk")
        xn = AR.alloc([128, D], BF16)
        B_xn = Buf("xn")
        xnT = AR.alloc([128, 8, 512], BF16)
        B_xnT = [Buf("xnT%d" % i) for i in range(4)]
        hT = AR.alloc([128, NFC, 512], BF16)
        B_hT = [Buf("hT%d" % i) for i in range(NFC)]
        sg = [AR.alloc([128, 512], F32) for i in range(2)]
        B_sg = [Buf("sg0"), Buf("sg1")]
        yb = AR.alloc([128, D], F32)
        B_y = Buf("y")
        hb = [AR.alloc([128, D], F32) for i in range(2)]
        B_hb = [Buf("hb0"), Buf("hb1")]
        st = AR.alloc([128, 8], F32)
        B_st = [Buf("st0"), Buf("st1")]

        cnt = {'x': 0, 'o': 0}

        def do_group(grp):
            ntok = sum(r for _, r in grp)
            for ti, (t0, rows) in enumerate(grp):
                sl = cnt['x'] % 2
                cnt['x'] += 1
                xs = xld[sl]
                S.op("sp", lambda e, xs=xs, t0=t0, rows=rows: e.dma_start(out=xs[:rows, :], in_=src_d[t0:t0 + rows, :]),
                     w=[B_xld[sl]], dma="xld%d" % sl)
                rms_rstd(xs[:rows, :], rows, junk[:rows, :], st[:rows, 0:1], st[:rows, 1:2], [B_xld[sl]], B_junk, B_st[0])
                S.op("dve", lambda e, xs=xs, rows=rows: e.scalar_tensor_tensor(
                    out=xn[:rows, :], in0=xs[:rows, :], scalar=st[:rows, 1:2], in1=GA[:rows, 0, :],
                    op0=ALU.mult, op1=ALU.mult), r=[B_xld[sl], B_st[0], B_GA[0]], w=[B_xn])
                for kc in range(8):
                    S.op("pe", lambda e, kc=kc, rows=rows: e.transpose(
                        psb16(0)[:, kc * 128:kc * 128 + rows], xn[:rows, kc * 128:(kc + 1) * 128], identb[:rows, :rows]),
                        r=[B_xn, B_ident], w=[PSB[0]])
                S.op("act", lambda e, ti=ti, rows=rows: e.copy(
                    out=xnT[:, :, ti * 128:ti * 128 + rows],
                    in_=psb16(0).rearrange("p (k t) -> p k t", k=8)[:, :, 0:rows]),
                    r=[PSB[0]], w=[B_xnT[ti]])
            for fc in range(NFC):
                pa = 1 + (fc % 2) * 2
                pb = pa + 1
                for kc in range(8):
                    S.op("pe", lambda e, fc=fc, kc=kc, pa=pa: e.matmul(
                        PS[pa][:, 0:ntok], lhsT=Win[:, kc, fc * 128:(fc + 1) * 128], rhs=xnT[:, kc, 0:ntok],
                        start=(kc == 0), stop=(kc == 7)),
                        r=[B_wg[fc // 2]] + B_xnT[:len(grp)], w=[PSB[pa]])
                for kc in range(8):
                    S.op("pe", lambda e, fc=fc, kc=kc, pb=pb: e.matmul(
                        PS[pb][:, 0:ntok], lhsT=Win[:, kc, DFF + fc * 128:DFF + (fc + 1) * 128], rhs=xnT[:, kc, 0:ntok],
                        start=(kc == 0), stop=(kc == 7)),
                        r=[B_wu[fc // 2]] + B_xnT[:len(grp)], w=[PSB[pb]])
                sgi = fc % 2
                S.op("act", lambda e, pa=pa, sgi=sgi: e.activation(out=sg[sgi][:, 0:ntok], in_=PS[pa][:, 0:ntok], func=AF.Silu),
                     r=[PSB[pa]], w=[B_sg[sgi]])
                S.op("dve", lambda e, fc=fc, pb=pb, sgi=sgi: e.tensor_tensor(
                    out=hT[:, fc, 0:ntok], in0=sg[sgi][:, 0:ntok], in1=PS[pb][:, 0:ntok], op=ALU.mult),
                    r=[B_sg[sgi], PSB[pb]], w=[B_hT[fc]])
            for ti, (t0, rows) in enumerate(grp):
                rs = cnt['o'] % 2
                cnt['o'] += 1
                xr_ = xres[rs]
                S.op("sp", lambda e, xr_=xr_, t0=t0, rows=rows: e.dma_start(out=xr_[:rows, :], in_=src_d[t0:t0 + rows, :]),
                     w=[B_xres[rs]], dma="xres0")
                for nh in range(2):
                    pc = 5 + nh
                    for fc in range(NFC):
                        S.op("pe", lambda e, fc=fc, nh=nh, pc=pc, ti=ti, rows=rows: e.matmul(
                            PS[pc][:rows, :], lhsT=hT[:, fc, ti * 128:ti * 128 + rows], rhs=Wout[:, fc, nh * 512:(nh + 1) * 512],
                            start=(fc == 0), stop=(fc == NFC - 1)),
                            r=[B_hT[fc], B_wo[fc // 2]], w=[PSB[pc]])
                    S.op("act", lambda e, nh=nh, pc=pc, rows=rows: e.copy(out=yb[:rows, nh * 512:(nh + 1) * 512], in_=PS[pc][:rows, :]),
                         r=[PSB[pc]], w=[B_y])
                rms_rstd(yb[:rows, :], rows, junk[:rows, :], st[:rows, 2:3], st[:rows, 3:4], [B_y], B_junk, B_st[1])
                ho = hb[rs]
                S.op("dve", lambda e, rows=rows: e.scalar_tensor_tensor(
                    out=yb[:rows, :], in0=yb[:rows, :], scalar=st[:rows, 3:4], in1=GA[:rows, 1, :],
                    op0=ALU.mult, op1=ALU.mult), r=[B_st[1], B_GA[1]], w=[B_y])
                S.op("dve", lambda e, rows=rows, ho=ho, xr_=xr_: e.scalar_tensor_tensor(
                    out=ho[:rows, :], in0=yb[:rows, :], scalar=0.5, in1=xr_[:rows, :],
                    op0=ALU.mult, op1=ALU.add), r=[B_y, B_xres[rs]], w=[B_hb[rs]])
                S.op("sp", lambda e, rows=rows, ho=ho, t0=t0: e.dma_start(out=dst_d[t0:t0 + rows, :], in_=ho[:rows, :]),
                     r=[B_hb[rs]], dma="hb%d" % rs)

        for grp in groups:
            do_group(grp)
        S.barrier()

    def proj_phase(qT, kT, Vb):
        Wi = AR.alloc([128, 8, PROJ], BF16)
        GA = AR.alloc([128, 2, D], F32)
        blocks = [(0, 512), (512, 1024), (1024, 1536), (1536, 2048), (2048, 2560), (2560, 3072), (3072, 3328)]
        B_wi = [Buf("wi%d" % i) for i in range(7)]
        w_in_v = w_in.rearrange("(k p) n -> p k n", p=128)
        for i, (c0, c1) in enumerate(blocks):
            S.op("pool", lambda e, c0=c0, c1=c1: e.dma_start(out=Wi[:, :, c0:c1], in_=w_in_v[:, :, c0:c1]),
                 w=[B_wi[i]], dma="wg%d" % i)
        load_gain(GA, 0, 2)
        S.op("dve", lambda e: e.memset(Vb[:, :, :, 128:129], 1.0), w=[B_vb])
        xld = [AR.alloc([128, D], F32) for i in range(2)]
        B_xld = [Buf("xld0"), Buf("xld1")]
        junk = AR.alloc([128, D], BF16)
        B_junk = Buf("junk")
        xn = AR.alloc([128, D], BF16)
        B_xn = Buf("xn")
        xnT = AR.alloc([128, 8, 512], BF16)
        B_xnT = [Buf("xnT%d" % i) for i in range(4)]
        ob = [AR.alloc([128, 512], F32) for i in range(3)]
        B_ob = [Buf("hb0"), Buf("hb1"), Buf("y")]
        st = AR.alloc([128, 8], F32)
        B_st = Buf("st0")
        cnt = {'x': 0, 'o': 0}

        def do_group(grp):
            ntok = sum(r for _, r in grp)
            g0 = grp[0][0]
            for ti, (t0, rows) in enumerate(grp):
                sl = cnt['x'] % 2
                cnt['x'] += 1
                xs = xld[sl]
                S.op("sp", lambda e, xs=xs, t0=t0, rows=rows: e.dma_start(out=xs[:rows, :], in_=h_scr[t0:t0 + rows, :]),
                     w=[B_xld[sl]], dma="xld%d" % sl)
                rms_rstd(xs[:rows, :], rows, junk[:rows, :], st[:rows, 0:1], st[:rows, 1:2], [B_xld[sl]], B_junk, B_st)
                S.op("dve", lambda e, xs=xs, rows=rows: e.scalar_tensor_tensor(
                    out=xn[:rows, :], in0=xs[:rows, :], scalar=st[:rows, 1:2], in1=GA[:rows, 0, :],
                    op0=ALU.mult, op1=ALU.mult), r=[B_xld[sl], B_st, B_GA[0]], w=[B_xn])
                for kc in range(8):
                    S.op("pe", lambda e, kc=kc, rows=rows: e.transpose(
                        psb16(0)[:, kc * 128:kc * 128 + rows], xn[:rows, kc * 128:(kc + 1) * 128], identb[:rows, :rows]),
                        r=[B_xn, B_ident], w=[PSB[0]])
                S.op("act", lambda e, ti=ti, rows=rows: e.copy(
                    out=xnT[:, :, ti * 128:ti * 128 + rows],
                    in_=psb16(0).rearrange("p (k t) -> p k t", k=8)[:, :, 0:rows]),
                    r=[PSB[0]], w=[B_xnT[ti]])
            for which in range(2):
                for h in range(4):
                    pa = 1 + (h % 2)
                    c0 = which * 512 + h * 128
                    for kc in range(8):
                        S.op("pe", lambda e, kc=kc, pa=pa, c0=c0: e.matmul(
                            PS[pa][:, 0:ntok], lhsT=Wi[:, kc, c0:c0 + 128], rhs=xnT[:, kc, 0:ntok],
                            start=(kc == 0), stop=(kc == 7)),
                            r=[B_wi[which]] + B_xnT[:len(grp)], w=[PSB[pa]])
                    dstT = qT if which == 0 else kT
                    sc = 0.125 if which == 0 else 1.0
                    S.op("act", lambda e, pa=pa, dstT=dstT, h=h, sc=sc: e.activation(
                        out=dstT[:, h, g0:g0 + ntok], in_=PS[pa][:, 0:ntok], func=AF.Copy, scale=sc),
                        r=[PSB[pa]], w=[B_qk])
            for ti, (t0, rows) in enumerate(grp):
                for bi in range(1, 7):
                    c0, c1 = blocks[bi]
                    wdt = c1 - c0
                    pc = 3 + (cnt['o'] % 3)
                    oi = cnt['o'] % 3
                    cnt['o'] += 1
                    for kc in range(8):
                        S.op("pe", lambda e, kc=kc, pc=pc, c0=c0, c1=c1, wdt=wdt, ti=ti, rows=rows: e.matmul(
                            PS[pc][:rows, 0:wdt], lhsT=xnT[:, kc, ti * 128:ti * 128 + rows], rhs=Wi[:, kc, c0:c1],
                            start=(kc == 0), stop=(kc == 7)),
                            r=[B_wi[bi], B_xnT[ti]], w=[PSB[pc]])
                    o_ = ob[oi]
                    S.op("act", lambda e, pc=pc, o_=o_, wdt=wdt, rows=rows: e.copy(out=o_[:rows, 0:wdt], in_=PS[pc][:rows, 0:wdt]),
                         r=[PSB[pc]], w=[B_ob[oi]])
                    if bi == 1:
                        dst = k_out[t0:t0 + rows, :]
                    elif bi == 2:
                        dst = v_out[t0:t0 + rows, :]
                        if t0 < TP:
                            tl = t0 // 128
                            S.op("dve", lambda e, pc=pc, tl=tl: e.tensor_copy(
                                out=Vb[:, tl, :, 0:128], in_=PS[pc][:, :].rearrange("p (h d) -> p h d", h=4)),
                                r=[PSB[pc]], w=[B_vb])
                    else:
                        dst = pr_scr[t0:t0 + rows, c0 - 1536:c1 - 1536]
                    S.op("sp", lambda e, o_=o_, dst=dst, wdt=wdt, rows=rows: e.dma_start(out=dst, in_=o_[:rows, 0:wdt]),
                         r=[B_ob[oi]], dma="ob%d" % oi)

        for grp in groups:
            do_group(grp)
        S.barrier()
        if os.environ.get("KDEV_NOSH"):
            return
        S.op("sp", lambda e: e.dma_start(out=shp_out, in_=pr_scr[TP - 1:TP, :]), dma="sh0")
        S.op("sp", lambda e: e.dma_start(out=shs_out, in_=pr_scr[TP:TT, :].rearrange("(b t) n -> b t n", t=4)[:, 3, :]), dma="sh1")


    C0 = math.exp(-0.5)

    class Tl:
        def __init__(self, shape, dt=F32, name="t"):
            self.ap = AR.alloc(shape, dt)
            self.b = Buf(name)

    def rwkv_phase():
        AR.off = 0
        RV = Tl([64, 5376], F32, "RV")
        LW = Tl([128, 2, 512], BF16, "LW")
        CM = Tl([64, 4, 64], F32, "CM")
        S.op("sp", lambda e: e.dma_start(out=RV.ap, in_=rwvec.partition_broadcast(64)), w=[RV.b], dma="RV")
        S.op("sp", lambda e: e.dma_start(out=CM.ap, in_=cmask), w=[CM.b], dma="CM")
        S.op("pool", lambda e: e.dma_start(out=LW.ap[0:64, 0, :], in_=rw_w2), w=[LW.b], dma="LW")
        S.op("pool", lambda e: e.dma_start(out=LW.ap[64:128, 0, :], in_=rw_a2), w=[LW.b], dma="LW")
        S.op("pool", lambda e: e.dma_start(out=LW.ap[:, 1, :], in_=rw_g2), w=[LW.b], dma="LW")
        ZR = Tl([1, RCOLS], F32, "ZR")
        S.op("dve", lambda e: e.memset(ZR.ap, 0.0), w=[ZR.b])
        S.op("sp", lambda e: e.dma_start(out=prev_scr[0:1, :], in_=ZR.ap), r=[ZR.b], dma="pv0")
        S.op("sp", lambda e: e.dma_start(out=prev_scr[1:TP, :], in_=pr_scr[0:TP - 1, :]), dma="pv1")
        S.op("sp", lambda e: e.dma_start(
            out=prev_scr[TP:TT, :].rearrange("(b t) n -> b t n", t=4)[:, 1:4, :],
            in_=pr_scr[TP:TT, :].rearrange("(b t) n -> b t n", t=4)[:, 0:3, :]), dma="pv2")
        S.op("sp", lambda e: e.dma_start(
            out=prev_scr[TP:TT, :].rearrange("(b t) n -> b t n", t=4)[:, 0, :], in_=shift0), dma="pv3")
        S.barrier()
        mark = AR.off

        def vec(i0, n, nch):
            return RV.ap[:, i0:i0 + n].unsqueeze(1).to_broadcast([64, nch, n])

        def d1(tok0, nch):
            T = {}
            P = Tl([64, nch, RCOLS], F32, "P")
            PV = Tl([64, nch, RCOLS], F32, "PV")
            S.op("sp", lambda e: e.dma_start(out=P.ap, in_=pr_scr[tok0:tok0 + 64 * nch, :].rearrange("(c t) n -> t c n", t=64)),
                 w=[P.b], dma="P")
            S.op("sp", lambda e: e.dma_start(out=PV.ap, in_=prev_scr[tok0:tok0 + 64 * nch, :].rearrange("(c t) n -> t c n", t=64)),
                 w=[PV.b], dma="PV")
            S.op("dve", lambda e: e.tensor_tensor(out=PV.ap, in0=PV.ap, in1=P.ap, op=ALU.subtract), r=[P.b], w=[PV.b])
            S.op("dve", lambda e: e.tensor_tensor(out=PV.ap, in0=PV.ap, in1=vec(0, RCOLS, nch), op=ALU.mult), r=[RV.b], w=[PV.b])
            S.op("dve", lambda e: e.tensor_tensor(out=P.ap, in0=P.ap, in1=PV.ap, op=ALU.add), r=[PV.b], w=[P.b])
            LO = Tl([64, nch, 256], BF16, "LO")
            S.op("act", lambda e: e.activation(out=LO.ap[:, :, 0:64], in_=P.ap[:, :, 1536:1600], func=AF.Tanh), r=[P.b], w=[LO.b])
            S.op("act", lambda e: e.copy(out=LO.ap[:, :, 64:128], in_=P.ap[:, :, 1600:1664]), r=[P.b], w=[LO.b])
            S.op("act", lambda e: e.activation(out=LO.ap[:, :, 128:256], in_=P.ap[:, :, 1664:1792], func=AF.Sigmoid), r=[P.b], w=[LO.b])
            LOT = Tl([128, nch, 2, 64], BF16, "LOT")
            for c in range(nch):
                for j in range(2):
                    S.op("pe", lambda e, c=c, j=j: e.transpose(
                        psb16(0)[:, (c * 2 + j) * 64:(c * 2 + j + 1) * 64], LO.ap[:, c, j * 128:(j + 1) * 128], identb[:64, :64]),
                        r=[LO.b, B_ident], w=[PSB[0]])
            S.op("act", lambda e: e.copy(out=LOT.ap, in_=psb16(0)[:, 0:nch * 128].rearrange("p (c j t) -> p c j t", c=nch, j=2)),
                 r=[PSB[0]], w=[LOT.b])
            SG = Tl([64, nch, 512], F32, "SG")
            AA = Tl([64, nch, 512], F32, "AA")
            GG = Tl([64, nch, 512], F32, "GG")
            for c in range(nch):
                S.op("pe", lambda e, c=c: e.matmul(PS[1][:64, :], lhsT=LOT.ap[0:64, c, 0, :], rhs=LW.ap[0:64, 0, :], start=True, stop=True),
                     r=[LOT.b, LW.b], w=[PSB[1]])
                S.op("pe", lambda e, c=c: e.matmul(PS[2][:64, :], lhsT=LOT.ap[64:128, c, 0, :], rhs=LW.ap[64:128, 0, :], start=True, stop=True),
                     r=[LOT.b, LW.b], w=[PSB[2]])
                S.op("pe", lambda e, c=c: e.matmul(PS[3][:64, :], lhsT=LOT.ap[:, c, 1, :], rhs=LW.ap[:, 1, :], start=True, stop=True),
                     r=[LOT.b, LW.b], w=[PSB[3]])
                S.op("dve", lambda e, c=c: e.tensor_tensor(out=SG.ap[:, c, :], in0=PS[1][:64, :], in1=RV.ap[:, 1792:2304], op=ALU.add),
                     r=[PSB[1], RV.b], w=[SG.b])
                S.op("dve", lambda e, c=c: e.tensor_tensor(out=AA.ap[:, c, :], in0=PS[2][:64, :], in1=RV.ap[:, 2304:2816], op=ALU.add),
                     r=[PSB[2], RV.b], w=[AA.b])
                S.op("act", lambda e, c=c: e.copy(out=GG.ap[:, c, :], in_=PS[3][:64, :]), r=[PSB[3]], w=[GG.b])
            S.op("act", lambda e: e.activation(out=SG.ap, in_=SG.ap, func=AF.Sigmoid), r=[SG.b], w=[SG.b])
            S.op("act", lambda e: e.activation(out=AA.ap, in_=AA.ap, func=AF.Sigmoid), r=[AA.b], w=[AA.b])
            KK = Tl([64, nch, 512], F32, "KK")
            T1 = Tl([64, nch, 512], F32, "T1")
            SS = Tl([64, nch * 8], F32, "SS")
            kr = P.ap[:, :, 512:1024]
            S.op("dve", lambda e: e.tensor_tensor(out=KK.ap, in0=kr, in1=vec(2816, 512, nch), op=ALU.mult), r=[P.b, RV.b], w=[KK.b])
            S.op("dve", lambda e: e.tensor_tensor(out=T1.ap, in0=KK.ap, in1=KK.ap, op=ALU.mult), r=[KK.b], w=[T1.b])
            S.op("dve", lambda e: e.tensor_reduce(out=SS.ap, in_=T1.ap.rearrange("p c (h n) -> p (c h) n", n=64), axis=AX.X, op=ALU.add),
                 r=[T1.b], w=[SS.b])
            S.op("dve", lambda e: e.tensor_scalar(out=SS.ap, in0=SS.ap, scalar1=1e-24, scalar2=None, op0=ALU.add), r=[SS.b], w=[SS.b])
            S.op("act", lambda e: e.sqrt(out=SS.ap, in_=SS.ap), r=[SS.b], w=[SS.b])
            S.op("dve", lambda e: e.reciprocal(out=SS.ap, in_=SS.ap), r=[SS.b], w=[SS.b])
            S.op("dve", lambda e: e.tensor_tensor(
                out=KK.ap.rearrange("p c (h n) -> p (c h) n", n=64), in0=KK.ap.rearrange("p c (h n) -> p (c h) n", n=64),
                in1=SS.ap.unsqueeze(2).to_broadcast([64, nch * 8, 64]), op=ALU.mult), r=[SS.b], w=[KK.b])
            KM = Tl([64, nch, 512], F32, "KM")
            S.op("dve", lambda e: e.scalar_tensor_tensor(out=T1.ap, in0=AA.ap, scalar=-1.0, in1=vec(3328, 512, nch), op0=ALU.add, op1=ALU.mult),
                 r=[AA.b, RV.b], w=[T1.b])
            S.op("dve", lambda e: e.scalar_tensor_tensor(out=KM.ap, in0=T1.ap, scalar=1.0, in1=kr, op0=ALU.add, op1=ALU.mult),
                 r=[T1.b, P.b], w=[KM.b])
            BE = Tl([64, nch, 512], F32, "BE")
            S.op("dve", lambda e: e.tensor_tensor(out=BE.ap, in0=KK.ap, in1=AA.ap, op=ALU.mult), r=[KK.b, AA.b], w=[BE.b])
            T.update(P=P, SG=SG, AA=AA, GG=GG, KK=KK, KM=KM, BE=BE, T1=T1)
            return T

        def sample_part():
            AR.off = mark
            T = d1(TP, 1)
            P, SG, KK, KM, BE = T["P"], T["SG"], T["KK"], T["KM"], T["BE"]
            RS = Tl([64, 8, 6, 64], F32, "RS")

            def hv(ap):
                return ap.rearrange("p c (h n) -> p (c h) n", n=64)
            S.op("act", lambda e: e.activation(out=RS.ap[:, :, 0, :], in_=hv(SG.ap), func=AF.Exp, scale=-C0), r=[SG.b], w=[RS.b])
            S.op("dve", lambda e: e.tensor_copy(out=RS.ap[:, :, 1, :], in_=hv(KM.ap)), r=[KM.b], w=[RS.b])
            S.op("dve", lambda e: e.tensor_copy(out=RS.ap[:, :, 2, :], in_=hv(P.ap[:, :, 1024:1536])), r=[P.b], w=[RS.b])
            S.op("dve", lambda e: e.tensor_scalar(out=RS.ap[:, :, 3, :], in0=hv(KK.ap), scalar1=-1.0, scalar2=None, op0=ALU.mult), r=[KK.b], w=[RS.b])
            S.op("dve", lambda e: e.tensor_copy(out=RS.ap[:, :, 4, :], in_=hv(BE.ap)), r=[BE.b], w=[RS.b])
            S.op("dve", lambda e: e.tensor_copy(out=RS.ap[:, :, 5, :], in_=hv(P.ap[:, :, 0:512])), r=[P.b], w=[RS.b])
            S.op("sp", lambda e: e.dma_start(out=rs_scr, in_=RS.ap.rearrange("p h q n -> p (h q n)")), r=[RS.b], dma="RSo")
            X = Tl([128, 4, 6, 64], F32, "X")
            St = Tl([128, 64, 64], F32, "St")
            TM = Tl([128, 64, 64], F32, "TM")
            SA = Tl([128, 64], F32, "SA")
            YS = Tl([128, 4, 64], F32, "YS")
            S.op("sp", lambda e: e.dma_start(out=St.ap.rearrange("p v k -> p (v k)"), in_=wkv0), w=[St.b], dma="St")
            S.barrier()
            for b in range(16):
                S.op("sp", lambda e, b=b: e.dma_start(
                    out=X.ap[8 * b:8 * b + 8, :, :, :].rearrange("p t q n -> p t (q n)"),
                    in_=rs_scr[4 * b:4 * b + 4, :].rearrange("t (h x) -> h t x", h=8)), w=[X.b], dma="X")
            for t in range(4):
                def bk(q, t=t):
                    return X.ap[:, t, q, :].unsqueeze(1).to_broadcast([128, 64, 64])
                S.op("dve", lambda e, bk=bk: e.tensor_tensor(out=TM.ap, in0=St.ap, in1=bk(3), op=ALU.mult), r=[St.b, X.b], w=[TM.b])
                S.op("dve", lambda e: e.tensor_reduce(out=SA.ap, in_=TM.ap, axis=AX.X, op=ALU.add), r=[TM.b], w=[SA.b])
                S.op("dve", lambda e, bk=bk: e.tensor_tensor(out=St.ap, in0=St.ap, in1=bk(0), op=ALU.mult), r=[X.b], w=[St.b])
                S.op("dve", lambda e, bk=bk: e.tensor_tensor(out=TM.ap, in0=SA.ap.unsqueeze(2).to_broadcast([128, 64, 64]), in1=bk(4), op=ALU.mult),
                     r=[SA.b, X.b], w=[TM.b])
                S.op("dve", lambda e: e.tensor_tensor(out=St.ap, in0=St.ap, in1=TM.ap, op=ALU.add), r=[TM.b], w=[St.b])
                S.op("dve", lambda e, bk=bk, t=t: e.tensor_tensor(out=TM.ap, in0=X.ap[:, t, 2, :].unsqueeze(2).to_broadcast([128, 64, 64]), in1=bk(1), op=ALU.mult),
                     r=[X.b], w=[TM.b])
                S.op("dve", lambda e: e.tensor_tensor(out=St.ap, in0=St.ap, in1=TM.ap, op=ALU.add), r=[TM.b], w=[St.b])
                S.op("dve", lambda e, bk=bk: e.tensor_tensor(out=TM.ap, in0=St.ap, in1=bk(5), op=ALU.mult), r=[St.b, X.b], w=[TM.b])
                S.op("dve", lambda e, t=t: e.tensor_reduce(out=YS.ap[:, t, :], in_=TM.ap, axis=AX.X, op=ALU.add), r=[TM.b], w=[YS.b])
            S.op("sp", lambda e: e.dma_start(out=wkvs_out, in_=St.ap.rearrange("p v k -> p (v k)")), r=[St.b], dma="St")
            S.op("sp", lambda e: e.dma_start(out=ys_scr, in_=YS.ap.rearrange("p t n -> p (t n)")), r=[YS.b], dma="YSo")
            S.barrier()
            YT = Tl([64, 1, 512], F32, "YT")
            for b in range(16):
                S.op("sp", lambda e, b=b: e.dma_start(
                    out=YT.ap[4 * b:4 * b + 4, 0, :].rearrange("t (h n) -> t h n", h=8),
                    in_=ys_scr[8 * b:8 * b + 8, :].rearrange("h (t n) -> t h n", t=4)), w=[YT.b], dma="YT")
            d4(T, YT.ap, YT.b, TP, 1)
            S.barrier()

        GN_EPS = 64e-5

        def hv(ap):
            return ap.rearrange("p c (h n) -> p (c h) n", n=64)

        def d4(T, Yap, Yb, tok0, nch):
            P, AA, GG, KM, T1 = T["P"], T["AA"], T["GG"], T["KM"], T["T1"]
            G = nch * 8
            MU = Tl([64, G], F32, "MU")
            VR = Tl([64, G], F32, "VR")
            YC = Tl([64, nch, 512], F32, "YC")

            def hv(ap):
                return ap.rearrange("p c (h n) -> p c h n", n=64)

            def g3(t_):
                return t_.ap.rearrange("p (c h) -> p c h", h=8)

            def bl(t_):
                return g3(t_).unsqueeze(3).to_broadcast([64, nch, 8, 64])
            S.op("dve", lambda e: e.tensor_reduce(out=g3(MU), in_=hv(Yap), axis=AX.X, op=ALU.add), r=[Yb], w=[MU.b])
            S.op("dve", lambda e: e.tensor_scalar(out=MU.ap, in0=MU.ap, scalar1=1.0 / 64, scalar2=None, op0=ALU.mult), r=[MU.b], w=[MU.b])
            S.op("dve", lambda e: e.tensor_tensor(out=hv(YC.ap), in0=hv(Yap), in1=bl(MU), op=ALU.subtract), r=[Yb, MU.b], w=[YC.b])
            S.op("dve", lambda e: e.tensor_tensor(out=T1.ap, in0=YC.ap, in1=YC.ap, op=ALU.mult), r=[YC.b], w=[T1.b])
            S.op("dve", lambda e: e.tensor_reduce(out=g3(VR), in_=hv(T1.ap), axis=AX.X, op=ALU.add), r=[T1.b], w=[VR.b])
            S.op("dve", lambda e: e.tensor_scalar(out=VR.ap, in0=VR.ap, scalar1=1.0 / 64, scalar2=GN_EPS, op0=ALU.mult, op1=ALU.add), r=[VR.b], w=[VR.b])
            S.op("act", lambda e: e.sqrt(out=VR.ap, in_=VR.ap), r=[VR.b], w=[VR.b])
            S.op("dve", lambda e: e.reciprocal(out=VR.ap, in_=VR.ap), r=[VR.b], w=[VR.b])
            S.op("dve", lambda e: e.tensor_tensor(out=hv(YC.ap), in0=hv(YC.ap), in1=bl(VR), op=ALU.mult), r=[VR.b], w=[YC.b])
            S.op("dve", lambda e: e.tensor_tensor(out=YC.ap, in0=YC.ap, in1=vec(4352, 512, nch), op=ALU.mult), r=[RV.b], w=[YC.b])
            S.op("dve", lambda e: e.tensor_tensor(out=YC.ap, in0=YC.ap, in1=vec(4864, 512, nch), op=ALU.add), r=[RV.b], w=[YC.b])
            S.op("dve", lambda e: e.tensor_tensor(out=T1.ap, in0=P.ap[:, :, 0:512], in1=KM.ap, op=ALU.mult), r=[P.b, KM.b], w=[T1.b])
            S.op("dve", lambda e: e.tensor_tensor(out=T1.ap, in0=T1.ap, in1=vec(3840, 512, nch), op=ALU.mult), r=[RV.b], w=[T1.b])
            S.op("dve", lambda e: e.tensor_reduce(out=g3(MU), in_=hv(T1.ap), axis=AX.X, op=ALU.add), r=[T1.b], w=[MU.b])
            S.op("dve", lambda e: e.tensor_tensor(out=hv(T1.ap), in0=hv(P.ap[:, :, 1024:1536]), in1=bl(MU), op=ALU.mult), r=[P.b, MU.b], w=[T1.b])
            S.op("dve", lambda e: e.tensor_tensor(out=YC.ap, in0=YC.ap, in1=T1.ap, op=ALU.add), r=[T1.b], w=[YC.b])
            S.op("dve", lambda e: e.tensor_tensor(out=YC.ap, in0=YC.ap, in1=GG.ap, op=ALU.mult), r=[GG.b], w=[YC.b])
            S.op("sp", lambda e: e.dma_start(out=orw_scr[tok0:tok0 + 64 * nch, :].rearrange("(c t) n -> t c n", t=64), in_=YC.ap),
                 r=[YC.b], dma="YC")

        bank = [0]

        def nb():
            bank[0] = (bank[0] % 7) + 1
            return bank[0]

        def prompt_part():
            AR.off = mark
            H = [Tl([64, 8, 64], F32, "H0"), Tl([64, 8, 64], F32, "H1")]
            S.op("dve", lambda e: e.memset(H[0].ap, 0.0), w=[H[0].b])
            umark = AR.off
            cur = 0
            I64 = identf[:64, :64]
            F32R = mybir.dt.float32r
            USE_R = False

            def rr(ap):
                return ap.bitcast(F32R) if (USE_R and ap.dtype == F32) else ap
            for u in range(16):
                AR.off = umark
                T = d1(u * 128, 2)
                P, SG, AA, KK, KM, BE, T1 = T["P"], T["SG"], T["AA"], T["KK"], T["KM"], T["BE"], T["T1"]
                GM = Tl([64, 2, 512], F32, "GM")
                GI = Tl([64, 2, 512], F32, "GI")
                GP = Tl([64, 2, 512], F32, "GP")
                for c in range(2):
                    bk = nb()
                    S.op("pe", lambda e, c=c, bk=bk: e.matmul(PS[bk][:64, :], lhsT=CM.ap[:, 2, :], rhs=SG.ap[:, c, :], start=True, stop=True),
                         r=[CM.b, SG.b], w=[PSB[bk]])
                    S.op("act", lambda e, c=c, bk=bk: e.activation(out=GM.ap[:, c, :], in_=PS[bk][:64, :], func=AF.Exp, scale=-C0), r=[PSB[bk]], w=[GM.b])
                    S.op("act", lambda e, c=c, bk=bk: e.activation(out=GI.ap[:, c, :], in_=PS[bk][:64, :], func=AF.Exp, scale=C0), r=[PSB[bk]], w=[GI.b])
                    S.op("dve", lambda e, c=c, bk=bk: e.tensor_tensor(out=T1.ap[:, c, :], in0=PS[bk][:64, :], in1=SG.ap[:, c, :], op=ALU.subtract),
                         r=[PSB[bk], SG.b], w=[T1.b])
                S.op("act", lambda e: e.activation(out=GP.ap, in_=T1.ap, func=AF.Exp, scale=-C0), r=[T1.b], w=[GP.b])
                AL = Tl([64, 2, 512], F32, "AL")
                BT = Tl([64, 2, 512], F32, "BT")
                KT = Tl([64, 2, 512], F32, "KT")
                RB = Tl([64, 2, 512], F32, "RB")
                S.op("dve", lambda e: e.scalar_tensor_tensor(out=AL.ap, in0=KK.ap, scalar=-1.0, in1=GP.ap, op0=ALU.mult, op1=ALU.mult), r=[KK.b, GP.b], w=[AL.b])
                S.op("dve", lambda e: e.tensor_tensor(out=BT.ap, in0=BE.ap, in1=GI.ap, op=ALU.mult), r=[BE.b, GI.b], w=[BT.b])
                S.op("dve", lambda e: e.tensor_tensor(out=KT.ap, in0=KM.ap, in1=GI.ap, op=ALU.mult), r=[KM.b, GI.b], w=[KT.b])
                S.op("dve", lambda e: e.tensor_tensor(out=RB.ap, in0=P.ap[:, :, 0:512], in1=GM.ap, op=ALU.mult), r=[P.b, GM.b], w=[RB.b])
                GC = Tl([64, 16], F32, "GC")
                bk = nb()
                for c in range(2):
                    for h in range(8):
                        S.op("pe", lambda e, c=c, h=h, bk=bk: e.matmul(
                            PS[bk][:64, c * 8 + h:c * 8 + h + 1], lhsT=GM.ap[:, c, h * 64:(h + 1) * 64], rhs=CM.ap[:, 3, 0:1], start=True, stop=True),
                            r=[GM.b, CM.b], w=[PSB[bk]])
                S.op("act", lambda e, bk=bk: e.copy(out=GC.ap, in_=PS[bk][:64, 0:16]), r=[PSB[bk]], w=[GC.b])
                XT = {}
                for nm, src in (("RB", RB), ("AL", AL), ("BT", BT), ("KT", KT)):
                    dst = Tl([64, 2, 8, 64], F32, nm + "T")
                    XT[nm] = dst
                    for c in range(2):
                        bk = nb()
                        for h in range(8):
                            S.op("pe", lambda e, c=c, h=h, bk=bk, src=src: e.transpose(
                                PS[bk][:64, h * 64:(h + 1) * 64], src.ap[:, c, h * 64:(h + 1) * 64], I64),
                                r=[src.b, B_ident], w=[PSB[bk]])
                        eng = "act" if c == 0 else "dve"
                        if eng == "act":
                            S.op("act", lambda e, c=c, bk=bk, dst=dst: e.copy(out=dst.ap[:, c, :, :], in_=PS[bk][:64, :].rearrange("p (h t) -> p h t", h=8)),
                                 r=[PSB[bk]], w=[dst.b])
                        else:
                            S.op("dve", lambda e, c=c, bk=bk, dst=dst: e.tensor_copy(out=dst.ap[:, c, :, :], in_=PS[bk][:64, :].rearrange("p (h t) -> p h t", h=8)),
                                 r=[PSB[bk]], w=[dst.b])
                RBT, ALT, BTT, KTT = XT["RB"], XT["AL"], XT["BT"], XT["KT"]
                DT2 = BF16 if os.environ.get("KDEV_D2F32") is None else F32
                Lm = [Tl([64, 2, 8, 64], DT2, "L0"), Tl([64, 2, 8, 64], DT2, "L1")]
                Nm = [Tl([64, 2, 8, 64], DT2, "N0"), Tl([64, 2, 8, 64], DT2, "N1")]
                Pm = [Tl([64, 2, 8, 64], DT2, "P0"), Tl([64, 2, 8, 64], F32, "P1")]
                Pb = Tl([64, 2, 8, 64], DT2, "Pb")
                I64b = identb[:64, :64] if DT2 == BF16 else identf[:64, :64]
                AK = Tl([64, 2, 8, 64], F32, "AK")
                RBm = Tl([64, 2, 8, 64], F32, "RBm")
                RKm = Tl([64, 2, 8, 64], F32, "RKm")
                WT = Tl([64, 2, 8, 64], F32, "WT")
                GT = Tl([64, 2, 8, 64], F32, "GT")

                def mm8(c, lh, rh, rb):
                    bk = nb()
                    for h in range(8):
                        S.op("pe", lambda e, h=h, bk=bk: e.matmul(PS[bk][:64, h * 64:(h + 1) * 64], lhsT=rr(lh(h)), rhs=rr(rh(h)), start=True, stop=True),
                             r=rb, w=[PSB[bk]])
                    return bk

                def psv(bk):
                    return PS[bk][:64, :].rearrange("p (h t) -> p h t", h=8)

                def mk(m):
                    return CM.ap[:, m, :].unsqueeze(1).to_broadcast([64, 8, 64])
                for c in range(2):
                    for (A_, B_, m, dst) in ((ALT, BTT, 0, Lm[0]), (BTT, ALT, 1, Nm[0]), (ALT, KTT, 0, AK), (BTT, RBT, 2, RBm), (KTT, RBT, 2, RKm)):
                        bk = mm8(c, lambda h, A_=A_, c=c: A_.ap[:, c, h, :], lambda h, B_=B_, c=c: B_.ap[:, c, h, :], [A_.b, B_.b])
                        S.op("dve", lambda e, bk=bk, dst=dst, m=m, c=c: e.tensor_tensor(out=dst.ap[:, c, :, :], in0=psv(bk), in1=mk(m), op=ALU.mult),
                             r=[PSB[bk], CM.b], w=[dst.b])
                    S.op("dve", lambda e, c=c: e.tensor_tensor(out=Pm[0].ap[:, c, :, :], in0=Nm[0].ap[:, c, :, :],
                                                               in1=I64b.unsqueeze(1).to_broadcast([64, 8, 64]), op=ALU.add),
                         r=[Nm[0].b, B_ident], w=[Pm[0].b])
                for j in range(1, 6):
                    a, b_ = (j - 1) % 2, j % 2
                    for c in range(2):
                        bk = mm8(c, lambda h, c=c, a=a: Nm[a].ap[:, c, h, :], lambda h, c=c, a=a: Lm[a].ap[:, c, h, :], [Nm[a].b, Lm[a].b])
                        S.op("act", lambda e, bk=bk, c=c, b_=b_: e.copy(out=Lm[b_].ap[:, c, :, :], in_=psv(bk)), r=[PSB[bk]], w=[Lm[b_].b])
                        if j < 5:
                            bk = mm8(c, lambda h, c=c, a=a: Lm[a].ap[:, c, h, :], lambda h, c=c, a=a: Nm[a].ap[:, c, h, :], [Nm[a].b, Lm[a].b])
                            S.op("act", lambda e, bk=bk, c=c, b_=b_: e.copy(out=Nm[b_].ap[:, c, :, :], in_=psv(bk)), r=[PSB[bk]], w=[Nm[b_].b])
                        pin = Pm[0] if a == 0 else Pb
                        pout = Pm[1] if j == 5 else (Pb if a == 0 else Pm[0])
                        bk = mm8(c, lambda h, c=c, b_=b_: Lm[b_].ap[:, c, h, :], lambda h, c=c, pin=pin: pin.ap[:, c, h, :], [Lm[b_].b, pin.b])
                        S.op("dve", lambda e, bk=bk, c=c, pin=pin, pout=pout: e.tensor_tensor(out=pout.ap[:, c, :, :], in0=psv(bk), in1=pin.ap[:, c, :, :], op=ALU.add),
                             r=[PSB[bk], pin.b], w=[pout.b])
                PF = Pm[1]
                for c in range(2):
                    bk = mm8(c, lambda h, c=c: AL.ap[:, c, h * 64:(h + 1) * 64], lambda h, c=c: PF.ap[:, c, h, :], [AL.b, PF.b])
                    S.op("act", lambda e, bk=bk, c=c: e.copy(out=WT.ap[:, c, :, :], in_=psv(bk)), r=[PSB[bk]], w=[WT.b])
                    bk = mm8(c, lambda h, c=c: AK.ap[:, c, h, :], lambda h, c=c: PF.ap[:, c, h, :], [AK.b, PF.b])
                    S.op("dve", lambda e, bk=bk, c=c: e.tensor_copy(out=GT.ap[:, c, :, :], in_=psv(bk)), r=[PSB[bk]], w=[GT.b])
                U = Tl([64, 512], F32, "U")
                Y = Tl([64, 2, 512], F32, "Y")
                for c in range(2):
                    Hc, Hn = H[cur], H[1 - cur]

                    def V(h, c=c):
                        return P.ap[:, c, 1024 + h * 64:1024 + (h + 1) * 64]
                    bu = nb()
                    for h in range(8):
                        S.op("pe", lambda e, h=h, c=c, bu=bu, Hc=Hc: e.matmul(PS[bu][:64, h * 64:(h + 1) * 64], lhsT=rr(WT.ap[:, c, h, :]), rhs=rr(Hc.ap[:, h, :]), start=True, stop=False),
                             r=[WT.b, Hc.b], w=[PSB[bu]])
                        S.op("pe", lambda e, h=h, c=c, bu=bu, V=V: e.matmul(PS[bu][:64, h * 64:(h + 1) * 64], lhsT=rr(GT.ap[:, c, h, :]), rhs=rr(V(h)), start=False, stop=True),
                             r=[GT.b, P.b], w=[PSB[bu]])
                    S.op("act", lambda e, bu=bu: e.copy(out=U.ap, in_=PS[bu][:64, :]), r=[PSB[bu]], w=[U.b])
                    bh = nb()
                    for h in range(8):
                        S.op("pe", lambda e, h=h, c=c, bh=bh: e.matmul(PS[bh][:64, h * 64:(h + 1) * 64], lhsT=rr(BT.ap[:, c, h * 64:(h + 1) * 64]), rhs=rr(U.ap[:, h * 64:(h + 1) * 64]), start=True, stop=False),
                             r=[BT.b, U.b], w=[PSB[bh]])
                        S.op("pe", lambda e, h=h, c=c, bh=bh, V=V: e.matmul(PS[bh][:64, h * 64:(h + 1) * 64], lhsT=rr(KT.ap[:, c, h * 64:(h + 1) * 64]), rhs=rr(V(h)), start=False, stop=True),
                             r=[KT.b, P.b], w=[PSB[bh]])
                    by = nb()
                    for h in range(8):
                        S.op("pe", lambda e, h=h, c=c, by=by, Hc=Hc: e.matmul(PS[by][:64, h * 64:(h + 1) * 64], lhsT=rr(RBT.ap[:, c, h, :]), rhs=rr(Hc.ap[:, h, :]), start=True, stop=False),
                             r=[RBT.b, Hc.b], w=[PSB[by]])
                        S.op("pe", lambda e, h=h, c=c, by=by: e.matmul(PS[by][:64, h * 64:(h + 1) * 64], lhsT=rr(RBm.ap[:, c, h, :]), rhs=rr(U.ap[:, h * 64:(h + 1) * 64]), start=False, stop=False),
                             r=[RBm.b, U.b], w=[PSB[by]])
                        S.op("pe", lambda e, h=h, c=c, by=by, V=V: e.matmul(PS[by][:64, h * 64:(h + 1) * 64], lhsT=rr(RKm.ap[:, c, h, :]), rhs=rr(V(h)), start=False, stop=True),
                             r=[RKm.b, P.b], w=[PSB[by]])
                    S.op("act", lambda e, by=by, c=c: e.copy(out=Y.ap[:, c, :], in_=PS[by][:64, :]), r=[PSB[by]], w=[Y.b])
                    S.op("dve", lambda e, bh=bh, Hc=Hc, Hn=Hn: e.tensor_tensor(out=Hn.ap, in0=Hc.ap, in1=PS[bh][:64, :].rearrange("p (h v) -> p h v", h=8), op=ALU.add),
                         r=[Hc.b, PSB[bh]], w=[Hn.b])
                    S.op("dve", lambda e, c=c, Hn=Hn: e.tensor_tensor(out=Hn.ap, in0=Hn.ap, in1=GC.ap[:, c * 8:(c + 1) * 8].unsqueeze(2).to_broadcast([64, 8, 64]), op=ALU.mult),
                         r=[GC.b], w=[Hn.b])
                    cur = 1 - cur
                d4(T, Y.ap, Y.b, u * 128, 2)
                S.barrier()
            Hf = H[cur]
            HT = Tl([64, 8, 64], F32, "HT")
            bk = nb()
            for h in range(8):
                S.op("pe", lambda e, h=h, bk=bk: e.transpose(PS[bk][:64, h * 64:(h + 1) * 64], Hf.ap[:, h, :], I64), r=[Hf.b, B_ident], w=[PSB[bk]])
            S.op("act", lambda e, bk=bk: e.copy(out=HT.ap, in_=PS[bk][:64, :].rearrange("p (h k) -> p h k", h=8)), r=[PSB[bk]], w=[HT.b])
            S.op("sp", lambda e: e.dma_start(out=wkvp_out.rearrange("h v k -> v h k"), in_=HT.ap), r=[HT.b], dma="HT")
            S.barrier()

        sample_part()
        if os.environ.get("KDEV_NOPROMPT") is None:
            prompt_part()


    LAM_INIT = 0.8 - 0.6 * math.exp(-0.3 * 0)
    SUBLN_EPS = 1e-5
    LF = 383

    def attn_setup():
        A = {}
        LQ = Tl([128, 4, 64], F32, "LQ")
        S.op("sp", lambda e: e.dma_start(out=LQ.ap.rearrange("p a n -> p (a n)"), in_=lamvec.partition_broadcast(128)), w=[LQ.b], dma="LQ")
        LS = Tl([128, 4], F32, "LS")
        LAM = Tl([128, 2], F32, "LAM")
        PRD = Tl([128, 2, 64], F32, "PRD")
        S.op("dve", lambda e: e.tensor_tensor(out=PRD.ap, in0=LQ.ap[:, 0:2, :], in1=LQ.ap[:, 2:4, :], op=ALU.mult), r=[LQ.b], w=[PRD.b])
        S.op("dve", lambda e: e.tensor_reduce(out=LS.ap[:, 0:2], in_=PRD.ap, axis=AX.X, op=ALU.add), r=[PRD.b], w=[LS.b])
        S.op("act", lambda e: e.activation(out=LS.ap[:, 2:4], in_=LS.ap[:, 0:2], func=AF.Exp), r=[LS.b], w=[LS.b])
        S.op("dve", lambda e: e.tensor_tensor(out=LAM.ap[:, 0:1], in0=LS.ap[:, 2:3], in1=LS.ap[:, 3:4], op=ALU.subtract), r=[LS.b], w=[LAM.b])
        S.op("dve", lambda e: e.tensor_scalar(out=LAM.ap[:, 0:1], in0=LAM.ap[:, 0:1], scalar1=LAM_INIT, scalar2=None, op0=ALU.add), r=[LAM.b], w=[LAM.b])
        S.op("dve", lambda e: e.tensor_scalar(out=LAM.ap[:, 1:2], in0=LAM.ap[:, 0:1], scalar1=-1.0, scalar2=None, op0=ALU.mult), r=[LAM.b], w=[LAM.b])
        AS = int(os.environ.get("KDEV_AS", 9))
        if AS < 1:
            return A
        SUBW = Tl([128, 128], F32, "SUBW")
        S.op("sp", lambda e: e.dma_start(out=SUBW.ap, in_=subln.partition_broadcast(128)), w=[SUBW.b], dma="SUBW")
        S.op("dve", lambda e: e.tensor_scalar(out=SUBW.ap, in0=SUBW.ap, scalar1=1.0 - LAM_INIT, scalar2=None, op0=ALU.mult), r=[SUBW.b], w=[SUBW.b])
        if AS < 2:
            return A
        TE = Tl([33, 4], F32, "TE")
        T31 = Tl([33, 4], F32, "T31")
        OH = Tl([33, LF], F32, "OH")
        S.op("sp", lambda e: e.dma_start(out=TE.ap[0:32, :], in_=relb), w=[TE.b], dma="TE")
        S.op("sp", lambda e: e.dma_start(out=T31.ap[0:32, :], in_=relb[31:32, :].partition_broadcast(32)), w=[T31.b], dma="T31")
        S.op("sp", lambda e: e.dma_start(out=OH.ap, in_=onehot), w=[OH.b], dma="OH")
        S.op("dve", lambda e: e.tensor_tensor(out=TE.ap[0:32, :], in0=TE.ap[0:32, :], in1=T31.ap[0:32, :], op=ALU.subtract), r=[T31.b], w=[TE.b])
        S.op("dve", lambda e: e.memset(TE.ap[32:33, :], -30000.0), w=[TE.b])
        if AS < 3:
            return A
        RR = Tl([128, LF], F32, "RR")
        TEB = Tl([33, 4, 128], F32, "TEB")
        S.op("dve", lambda e: e.tensor_copy(out=TEB.ap, in_=TE.ap.unsqueeze(2).to_broadcast([33, 4, 128])), r=[TE.b], w=[TEB.b])
        for h in range(4):
            S.op("pe", lambda e, h=h: e.matmul(PS[1][:, 0:LF], lhsT=TEB.ap[:, h, :], rhs=OH.ap, start=True, stop=True),
                 r=[TEB.b, OH.b], w=[PSB[1]])
            S.op("act", lambda e: e.copy(out=RR.ap, in_=PS[1][:, 0:LF]), r=[PSB[1]], w=[RR.b])
            S.op("sp", lambda e, h=h: e.dma_start(out=r_scr[h], in_=RR.ap), r=[RR.b], dma="RR")
        S.barrier()
        if AS < 4:
            return A
        BF = Tl([128, 4, 2, 128], F32, "BF")
        BB = Tl([128, 4, 2, 128], BF16, "BB")
        for h in range(4):
            for dl in range(2):
                src = bass.AP(tensor=r_scr.tensor, offset=h * 128 * LF + 127 + 128 * dl, ap=[[LF - 1, 128], [1, 128]])
                S.op("sp", lambda e, h=h, dl=dl, src=src: e.dma_start(out=BF.ap[:, h, dl, :], in_=src), w=[BF.b], dma="BF")
        S.op("dve", lambda e: e.tensor_copy(out=BB.ap, in_=BF.ap), r=[BF.b], w=[BB.b])
        if os.environ.get("KDEV_DBG"):
            bfd = dout("bf_dbg", [128, 4 * 2 * 128])
            S.op("sp", lambda e: e.dma_start(out=bfd, in_=BF.ap.rearrange("p h d q -> p (h d q)")), r=[BF.b], dma="bfd")
        A.update(LAM=LAM, SUBW=SUBW, BB=BB)
        A["mark"] = AR.off
        return A

    def subln_tile(OA, rows, G, junk, st, SUBW):
        S.op("dve", lambda e: e.tensor_tensor(out=junk.ap, in0=OA.ap, in1=OA.ap, op=ALU.mult), r=[OA.b], w=[junk.b])
        S.op("dve", lambda e: e.tensor_reduce(out=st.ap, in_=junk.ap, axis=AX.X, op=ALU.add), r=[junk.b], w=[st.b])
        S.op("dve", lambda e: e.tensor_scalar(out=st.ap, in0=st.ap, scalar1=1.0 / 128, scalar2=SUBLN_EPS, op0=ALU.mult, op1=ALU.add), r=[st.b], w=[st.b])
        S.op("act", lambda e: e.sqrt(out=st.ap, in_=st.ap), r=[st.b], w=[st.b])
        S.op("dve", lambda e: e.reciprocal(out=st.ap, in_=st.ap), r=[st.b], w=[st.b])
        S.op("dve", lambda e: e.tensor_tensor(out=OA.ap, in0=OA.ap, in1=st.ap.unsqueeze(2).to_broadcast([rows, G, 128]), op=ALU.mult), r=[st.b], w=[OA.b])
        S.op("dve", lambda e: e.tensor_tensor(out=OA.ap, in0=OA.ap, in1=SUBW.ap[0:rows, :].unsqueeze(1).to_broadcast([rows, G, 128]), op=ALU.mult),
             r=[SUBW.b], w=[OA.b])

    def attn_prompt(A, qT, kT, Vb):
        LAM, SUBW, BB = A["LAM"], A["SUBW"], A["BB"]
        PT = [Tl([128, 2, 256], BF16, "PT%d" % i) for i in range(2)]
        OA = [Tl([128, 4, 128], F32, "OA%d" % i) for i in range(2)]
        RS2 = Tl([128, 2], F32, "RS2")
        junk = Tl([128, 4, 128], F32, "jk")
        st = Tl([128, 4], F32, "st4")
        sb = [(0, 1), (2, 3)]
        it = [0]
        APL = int(os.environ.get("KDEV_AP", 9))
        for G in range(int(os.environ.get("KDEV_APG0", 0)), int(os.environ.get("KDEV_APG", 8))):
            qt0 = 2 * G
            for h in range(4):
                ob = [4 + 2 * (h % 2), 5 + 2 * (h % 2)]
                for j in range(2):
                    S.op("dve", lambda e, j=j, ob=ob: e.memset(PS[ob[j]][:, :], 0.0), w=[PSB[ob[j]]])
                for kb in range(qt0 + 2):
                    i_ = it[0] % 2
                    it[0] += 1
                    far = kb < qt0 - 1
                    j_lo = 0 if kb <= qt0 else 1
                    for m in range(2):
                        bk = sb[i_][m]
                        kop = kT[m * 64:(m + 1) * 64, h, kb * 128:(kb + 1) * 128]
                        if far:
                            for j in range(2):
                                S.op("pe", lambda e, m=m, bk=bk, kop=kop, h=h, qt0=qt0, j=j: e.matmul(
                                    PS[bk][:, j * 128:(j + 1) * 128], lhsT=kop,
                                    rhs=qT[m * 64:(m + 1) * 64, h, (qt0 + j) * 128:(qt0 + j + 1) * 128],
                                    start=True, stop=True), r=[B_qk], w=[PSB[bk]])
                        else:
                            for j in range(j_lo, 2):
                                qt = qt0 + j
                                dl = qt - kb
                                S.op("pe", lambda e, m=m, bk=bk, kop=kop, h=h, qt=qt, j=j, dl=dl: e.matmul(
                                    PS[bk][:, j * 128:(j + 1) * 128], lhsT=kop,
                                    rhs=qT[m * 64:(m + 1) * 64, h, qt * 128:(qt + 1) * 128], start=True, stop=(dl >= 2)),
                                    r=[B_qk], w=[PSB[bk]])
                                if dl < 2:
                                    S.op("pe", lambda e, m=m, bk=bk, h=h, j=j, dl=dl: e.matmul(
                                        PS[bk][:, j * 128:(j + 1) * 128], lhsT=identb[:, :],
                                        rhs=BB.ap[:, h, dl, :], start=False, stop=True),
                                        r=[BB.b, B_ident], w=[PSB[bk]])
                    pt = PT[i_]
                    for m in range(2):
                        bk = sb[i_][m]
                        S.op("act", lambda e, bk=bk, pt=pt, j_lo=j_lo, m=m: e.activation(
                            out=pt.ap[:, m, j_lo * 128:256], in_=PS[bk][:, j_lo * 128:256], func=AF.Exp),
                            r=[PSB[bk]], w=[pt.b])
                    for j in range(j_lo, 2 if APL >= 2 else 0):
                        qt = qt0 + j
                        for m in range(2):
                            S.op("pe", lambda e, j=j, m=m, pt=pt, kb=kb, h=h, qt=qt, ob=ob: e.matmul(
                                PS[ob[j]][:, m * 256:m * 256 + 129], lhsT=pt.ap[:, m, j * 128:(j + 1) * 128],
                                rhs=Vb[:, kb, h, 0:129], start=False, stop=(kb == qt), skip_group_check=True),
                                r=[pt.b, B_vb], w=[PSB[ob[j]]])
                for j in range(2 if APL >= 3 else 0):
                    oa = OA[j]
                    ov = PS[ob[j]][:, :].rearrange("p (m x) -> p m x", m=2)
                    S.op("dve", lambda e, ov=ov: e.reciprocal(out=RS2.ap, in_=ov[:, :, 128]), r=[PSB[ob[j]]], w=[RS2.b])
                    S.op("dve", lambda e: e.tensor_tensor(out=RS2.ap[:, 1:2], in0=RS2.ap[:, 1:2], in1=LAM.ap[:, 1:2], op=ALU.mult), r=[LAM.b], w=[RS2.b])
                    S.op("dve", lambda e, ov=ov, oa=oa, h=h: e.tensor_scalar(out=oa.ap[:, h, :], in0=ov[:, 0, 0:128], scalar1=RS2.ap[:, 0:1], scalar2=None, op0=ALU.mult),
                         r=[PSB[ob[j]], RS2.b], w=[oa.b])
                    S.op("dve", lambda e, ov=ov, oa=oa, h=h: e.scalar_tensor_tensor(out=oa.ap[:, h, :], in0=ov[:, 1, 0:128], scalar=RS2.ap[:, 1:2], in1=oa.ap[:, h, :],
                                                                                  op0=ALU.mult, op1=ALU.add), r=[PSB[ob[j]], RS2.b], w=[oa.b])
            for j in range(2 if APL >= 4 else 0):
                subln_tile(OA[j], 128, 4, junk, st, SUBW)
                S.op("sp", lambda e, j=j, qt0=qt0: e.dma_start(out=oat_scr[(qt0 + j) * 128:(qt0 + j + 1) * 128, :], in_=OA[j].ap.rearrange("p h d -> p (h d)")),
                     r=[OA[j].b], dma="OA%d" % j)
        S.barrier()

    def attn_sample(A, qT, kT):
        LAM, SUBW, BB = A["LAM"], A["SUBW"], A["BB"]
        AR.off = A["mark"]
        PTB = Tl([128, 256], I32, "PTB")
        IOT = Tl([128, 1], I32, "IOT")
        IDX = Tl([128, 256], I32, "IDX")
        S.op("sp", lambda e: e.dma_start(out=PTB.ap, in_=ptab.partition_broadcast(128)), w=[PTB.b], dma="PTB")
        S.op("sp", lambda e: e.dma_start(out=IOT.ap, in_=iotap), w=[IOT.b], dma="IOT")
        S.op("pool", lambda e: e.tensor_scalar(out=IDX.ap, in0=PTB.ap, scalar1=128, scalar2=IOT.ap[:, 0:1], op0=ALU.mult, op1=ALU.add),
             r=[PTB.b, IOT.b], w=[IDX.b])
        QB = Tl([128, 16, 4, 8], BF16, "QB")
        S.op("dve", lambda e: e.memset(QB.ap, 0.0), w=[QB.b])
        for m in range(2):
            S.op("dve", lambda e, m=m: e.tensor_copy(out=QB.ap[m * 64:(m + 1) * 64, :, :, m * 4:(m + 1) * 4],
                                                     in_=qT[m * 64:(m + 1) * 64, :, TP:TT].rearrange("p h (b t) -> p b h t", t=4)),
                 r=[B_qk], w=[QB.b])
        VN = Tl([4, 16, 512], F32, "VN")
        S.op("sp", lambda e: e.dma_start(out=VN.ap, in_=v_out[TP:TT, :].rearrange("(b t) n -> t b n", t=4)), w=[VN.b], dma="VN")
        ONE = Tl([128, 1], F32, "ONE")
        S.op("dve", lambda e: e.memset(ONE.ap, 1.0), w=[ONE.b])
        BS = Tl([128, 4, 2, 4], BF16, "BS")
        BN = Tl([4, 4, 2, 4], BF16, "BN")
        for m in range(2):
            S.op("dve", lambda e, m=m: e.tensor_copy(out=BS.ap[:, :, m, :], in_=BB.ap[:, :, 1, 0:4]), r=[BB.b], w=[BS.b])
            S.op("dve", lambda e, m=m: e.tensor_copy(out=BN.ap[:, :, m, :], in_=BB.ap[0:4, :, 0, 0:4]), r=[BB.b], w=[BN.b])
        OS = Tl([8, 16, 4, 128], F32, "OS")
        SMs = Tl([8, 16, 4], F32, "SMs")
        SEL = Tl([8, 3, 4], F32, "SEL")
        mark2 = AR.off
        NS = 4
        KP = [Tl([128, 512], F32, "KP%d" % i) for i in range(NS)]
        VP = [Tl([128, 512], F32, "VP%d" % i) for i in range(16)]
        B_vpg = [Buf("vpg%d" % i) for i in range(4)]
        KTb = [Tl([128, 4, 128], BF16, "KTb%d" % i) for i in range(2)]
        PTs = Tl([128, 512], F32, "PTs")
        PTN = Tl([4, 32], F32, "PTN")
        PSJ = Tl([128, 32], F32, "PSJ")
        cnt = [0]
        for b in range(16):
            sbk, obk, mbk = 1, 2, 3
            for j in range(16):
                sl = cnt[0] % NS
                cnt[0] += 1
                col = b * 16 + j
                S.op("pool", lambda e, sl=sl, col=col: e.indirect_dma_start(
                    out=KP[sl].ap, out_offset=None, in_=ck2, in_offset=bass.IndirectOffsetOnAxis(ap=IDX.ap[:, col:col + 1], axis=0)),
                    r=[IDX.b], w=[KP[sl].b], dma="KP%d" % sl)
                S.op("pool", lambda e, j=j, col=col: e.indirect_dma_start(
                    out=VP[j].ap, out_offset=None, in_=cv2, in_offset=bass.IndirectOffsetOnAxis(ap=IDX.ap[:, col:col + 1], axis=0)),
                    r=[IDX.b], w=[B_vpg[j % 4]], dma="VP%d" % (j % 4))
                tb = 4 + (j % 2)
                for h in range(4):
                    S.op("pe", lambda e, h=h, tb=tb, sl=sl: e.transpose(PS[tb][:, h * 128:(h + 1) * 128], KP[sl].ap[:, h * 128:(h + 1) * 128], identf[:, :]),
                         r=[KP[sl].b, B_ident], w=[PSB[tb]])
                kt = KTb[j % 2]
                if j % 2 == 0:
                    S.op("act", lambda e, tb=tb, kt=kt: e.copy(out=kt.ap, in_=PS[tb][:, :].rearrange("p (h k) -> p h k", h=4)), r=[PSB[tb]], w=[kt.b])
                else:
                    S.op("dve", lambda e, tb=tb, kt=kt: e.tensor_copy(out=kt.ap, in_=PS[tb][:, :].rearrange("p (h k) -> p h k", h=4)), r=[PSB[tb]], w=[kt.b])
                if j == 15:
                    S.op("pe", lambda e: e.matmul(PS[sbk][:, 480:512], lhsT=identb[:, :], rhs=BS.ap.rearrange("p h m t -> p (h m t)"), start=True, stop=False),
                         r=[BS.b, B_ident], w=[PSB[sbk]])
                for h in range(4):
                    S.op("pe", lambda e, h=h, j=j, kt=kt, b=b: e.matmul(PS[sbk][:, j * 32 + h * 8:j * 32 + (h + 1) * 8], lhsT=kt.ap[:, h, :], rhs=QB.ap[:, b, h, :],
                                                                        start=(j != 15), stop=True, skip_group_check=True), r=[kt.b, QB.b], w=[PSB[sbk]])
            S.op("pe", lambda e: e.matmul(PS[mbk][0:4, 0:32], lhsT=identb[0:4, 0:4], rhs=BN.ap.rearrange("p h m t -> p (h m t)"), start=True, stop=False),
                 r=[BN.b, B_ident], w=[PSB[mbk]])
            for h in range(4):
                S.op("pe", lambda e, h=h, b=b: e.matmul(PS[mbk][0:4, h * 8:(h + 1) * 8], lhsT=kT[:, h, TP + 4 * b:TP + 4 * b + 4], rhs=QB.ap[:, b, h, :],
                                                        start=False, stop=True, skip_group_check=True), r=[B_qk, QB.b], w=[PSB[mbk]])
            S.op("act", lambda e: e.activation(out=PTs.ap, in_=PS[sbk][:, :], func=AF.Exp), r=[PSB[sbk]], w=[PTs.b])
            S.op("act", lambda e: e.activation(out=PTN.ap, in_=PS[mbk][0:4, 0:32], func=AF.Exp), r=[PSB[mbk]], w=[PTN.b])
            S.op("dve", lambda e: e.tensor_reduce(out=PSJ.ap, in_=PTs.ap.rearrange("p (j x) -> p x j", j=16), axis=AX.X, op=ALU.add), r=[PTs.b], w=[PSJ.b])
            for h in range(4):
                S.op("pe", lambda e, h=h: e.matmul(PS[mbk][0:8, 64 + h:65 + h], lhsT=PSJ.ap[:, h * 8:(h + 1) * 8], rhs=ONE.ap[:, 0:1], start=True, stop=False),
                     r=[PSJ.b, ONE.b], w=[PSB[mbk]])
                S.op("pe", lambda e, h=h: e.matmul(PS[mbk][0:8, 64 + h:65 + h], lhsT=PTN.ap[:, h * 8:(h + 1) * 8], rhs=ONE.ap[0:4, 0:1], start=False, stop=True),
                     r=[PTN.b, ONE.b], w=[PSB[mbk]])
            S.op("act", lambda e, b=b: e.copy(out=SMs.ap[:, b, :], in_=PS[mbk][0:8, 64:68]), r=[PSB[mbk]], w=[SMs.b])
            S.op("dve", lambda e: e.memset(PS[obk][0:8, :], 0.0), w=[PSB[obk]])
            for j in range(16):
                sl = j
                for h in range(4):
                    S.op("pe", lambda e, h=h, j=j, sl=sl: e.matmul(PS[obk][0:8, h * 128:(h + 1) * 128], lhsT=PTs.ap[:, j * 32 + h * 8:j * 32 + (h + 1) * 8],
                                                                  rhs=VP[sl].ap[:, h * 128:(h + 1) * 128], start=False, stop=False, skip_group_check=True),
                         r=[PTs.b, B_vpg[sl % 4]], w=[PSB[obk]])
            for h in range(4):
                S.op("pe", lambda e, h=h, b=b: e.matmul(PS[obk][0:8, h * 128:(h + 1) * 128], lhsT=PTN.ap[:, h * 8:(h + 1) * 8],
                                                        rhs=VN.ap[:, b, h * 128:(h + 1) * 128], start=False, stop=True, skip_group_check=True),
                     r=[PTN.b, VN.b], w=[PSB[obk]])
            S.op("act", lambda e, b=b: e.copy(out=OS.ap[:, b, :, :], in_=PS[obk][0:8, :].rearrange("p (h d) -> p h d", h=4)), r=[PSB[obk]], w=[OS.b])
        S.op("dve", lambda e: e.reciprocal(out=SMs.ap, in_=SMs.ap), r=[SMs.b], w=[SMs.b])
        S.op("dve", lambda e: e.tensor_tensor(out=OS.ap, in0=OS.ap, in1=SMs.ap.unsqueeze(3).to_broadcast([8, 16, 4, 128]), op=ALU.mult), r=[SMs.b], w=[OS.b])
        S.op("sp", lambda e: e.dma_start(out=SEL.ap[:, 0:2, :], in_=selc), w=[SEL.b], dma="SEL")
        S.op("dve", lambda e: e.scalar_tensor_tensor(out=SEL.ap[:, 2, :], in0=SEL.ap[:, 1, :], scalar=LAM.ap[0:8, 0:1], in1=SEL.ap[:, 0, :], op0=ALU.mult, op1=ALU.add),
             r=[LAM.b], w=[SEL.b])
        S.barrier()
        AR.off = mark2
        OC = Tl([4, 64, 128], F32, "OC")
        osf = OS.ap.rearrange("p b h d -> p (b h d)")
        ocf = OC.ap.rearrange("p g d -> p (g d)")
        for ch in range(16):
            S.op("pe", lambda e, ch=ch: e.matmul(PS[5][0:4, :], lhsT=SEL.ap[:, 2, :], rhs=osf[:, ch * 512:(ch + 1) * 512], start=True, stop=True),
                 r=[SEL.b, OS.b], w=[PSB[5]])
            S.op("act", lambda e, ch=ch: e.copy(out=ocf[:, ch * 512:(ch + 1) * 512], in_=PS[5][0:4, :]), r=[PSB[5]], w=[OC.b])
        jk = Tl([4, 64, 128], F32, "jk2")
        st = Tl([4, 64], F32, "st64")
        subln_tile(OC, 4, 64, jk, st, SUBW)
        S.op("sp", lambda e: e.dma_start(out=oat_scr[TP:TT, :].rearrange("(b t) n -> t b n", t=4), in_=OC.ap.rearrange("p (b h) d -> p b (h d)", h=4)),
             r=[OC.b], dma="OC")
        S.barrier()

    def merge_phase():
        AR.off = 0
        Wo = AR.alloc([128, 8, D], BF16)
        B_wo = Buf("wo")
        S.op("pool", lambda e: e.dma_start(out=Wo, in_=w_out.rearrange("(k p) n -> p k n", p=128)), w=[B_wo], dma="wg0")
        GA = AR.alloc([128, 2, D], F32)
        load_gain(GA, 0, 3)
        CAT = [Tl([128, D], F32, "CAT%d" % i) for i in range(2)]
        HR = [Tl([128, D], F32, "HR%d" % i) for i in range(2)]
        CB = Tl([128, D], BF16, "CB")
        CT = Tl([128, 8, 128], BF16, "CT")
        MM = Tl([128, D], F32, "MM")
        junk = Tl([128, D], BF16, "junk")
        st = Tl([128, 4], F32, "st")
        for i, (t0, rows) in enumerate(tiles_all):
            sl = i % 2
            cat, hr = CAT[sl], HR[sl]
            S.op("sp", lambda e, cat=cat, t0=t0, rows=rows: e.dma_start(out=cat.ap[:rows, 0:512], in_=orw_scr[t0:t0 + rows, :]), w=[cat.b], dma="CAT%d" % sl)
            S.op("sp", lambda e, cat=cat, t0=t0, rows=rows: e.dma_start(out=cat.ap[:rows, 512:1024], in_=oat_scr[t0:t0 + rows, :]), w=[cat.b], dma="CAT%d" % sl)
            S.op("sp", lambda e, hr=hr, t0=t0, rows=rows: e.dma_start(out=hr.ap[:rows, :], in_=h_scr[t0:t0 + rows, :]), w=[hr.b], dma="HR%d" % sl)
            S.op("dve", lambda e, cat=cat, rows=rows: e.tensor_copy(out=CB.ap[:rows, :], in_=cat.ap[:rows, :]), r=[cat.b], w=[CB.b])
            for kc in range(8):
                S.op("pe", lambda e, kc=kc, rows=rows: e.transpose(psb16(0)[:, kc * 128:kc * 128 + rows], CB.ap[:rows, kc * 128:(kc + 1) * 128], identb[:rows, :rows]),
                     r=[CB.b, B_ident], w=[PSB[0]])
            S.op("act", lambda e, rows=rows: e.copy(out=CT.ap[:, :, 0:rows], in_=psb16(0).rearrange("p (k t) -> p k t", k=8)[:, :, 0:rows]), r=[PSB[0]], w=[CT.b])
            for nh in range(2):
                pc = 1 + nh
                for kc in range(8):
                    S.op("pe", lambda e, kc=kc, nh=nh, pc=pc, rows=rows: e.matmul(PS[pc][:rows, :], lhsT=CT.ap[:, kc, 0:rows], rhs=Wo[:, kc, nh * 512:(nh + 1) * 512],
                                                                                start=(kc == 0), stop=(kc == 7)), r=[CT.b, B_wo], w=[PSB[pc]])
                S.op("act", lambda e, nh=nh, pc=pc, rows=rows: e.copy(out=MM.ap[:rows, nh * 512:(nh + 1) * 512], in_=PS[pc][:rows, :]), r=[PSB[pc]], w=[MM.b])
            rms_rstd(MM.ap[:rows, :], rows, junk.ap[:rows, :], st.ap[:rows, 0:1], st.ap[:rows, 1:2], [MM.b], junk.b, st.b)
            S.op("dve", lambda e, rows=rows: e.scalar_tensor_tensor(out=MM.ap[:rows, :], in0=MM.ap[:rows, :], scalar=st.ap[:rows, 1:2], in1=GA[:rows, 0, :],
                                                                     op0=ALU.mult, op1=ALU.mult), r=[st.b, B_GA[0]], w=[MM.b])
            S.op("dve", lambda e, rows=rows, hr=hr: e.tensor_tensor(out=hr.ap[:rows, :], in0=hr.ap[:rows, :], in1=MM.ap[:rows, :], op=ALU.add), r=[MM.b], w=[hr.b])
            S.op("sp", lambda e, rows=rows, hr=hr, t0=t0: e.dma_start(out=h2_scr[t0:t0 + rows, :], in_=hr.ap[:rows, :]), r=[hr.b], dma="HR%d" % sl)
        S.barrier()

    B_qk = Buf("qk")
    B_vb = Buf("vb")

    ffn_phase("f1", x_all, h_scr, ff1_in, ff1_out, 0, 1)
    AR.off = 0
    qT = AR.alloc([128, 4, TT], BF16)
    kT = AR.alloc([128, 4, TT], BF16)
    Vb = AR.alloc([128, 16, 4, 132], BF16)
    attn_mark = AR.off
    if os.environ.get("KDEV_PROJ", "1") == "1":
        proj_phase(qT, kT, Vb)

    SKIP = os.environ.get("KDEV_SKIP", "")
    if STAGE >= 3:
        AR.off = attn_mark
        A = attn_setup()
        if "P" not in SKIP:
            attn_prompt(A, qT, kT, Vb)
        if os.environ.get("KDEV_NOSAMP") is None:
            attn_sample(A, qT, kT)
    if STAGE >= 2 and "R" not in SKIP:
        rwkv_phase()
    if STAGE >= 3:
        if "M" not in SKIP:
            merge_phase()
        if "F" not in SKIP:
            ffn_phase("f2", h2_scr, y_all, ff2_in, ff2_out, 4, 5)

    S.barrier()
    S.emit()
    return nc


_CACHE = {}


def _bucket(n):
    me = 16
    nf = np.maximum(n, 1).astype(np.float32)
    large = me + (np.log(nf / np.float32(me)) / np.float32(math.log(128 / me)) * np.float32(32 - me)).astype(np.int32)
    large = np.minimum(large, 31)
    return np.where(n < me, n, large)


def _onehot():
    n = np.arange(-127, 256)
    oh = np.zeros((33, 383), np.float32)
    bk = _bucket(np.maximum(n, 0))
    for j, nn in enumerate(n):
        if nn < 0:
            oh[32, j] = 1.0
        else:
            oh[bk[j], j] = 1.0
    return oh


def _selc():
    m = np.zeros((8, 2, 4), np.float32)
    for t in range(4):
        m[t, 0, t] = 1.0
        m[4 + t, 1, t] = -1.0
    return m


def _cmask():
    t = np.arange(64)
    m = np.zeros((64, 4, 64), np.float32)
    m[:, 0, :] = (t[:, None] > t[None, :])
    m[:, 1, :] = (t[:, None] < t[None, :])
    m[:, 2, :] = (t[:, None] <= t[None, :])
    m[63, 3, 0] = 1.0
    return m


def kernel(**inp):
    f32 = np.float32
    if "nc" not in _CACHE:
        _CACHE["nc"] = build_program()
    nc = _CACHE["nc"]
    xp = np.asarray(inp["x_prompt"], f32)
    xs = np.asarray(inp["x_sample"], f32)
    shared = {
        "gains": np.ascontiguousarray(np.asarray(inp["norm_gains"], f32)[0]),
        "ff1_in": np.ascontiguousarray(np.asarray(inp["ff1_in"], f32)[0]),
        "ff1_out": np.ascontiguousarray(np.asarray(inp["ff1_out"], f32)[0]),
        "ff2_in": np.ascontiguousarray(np.asarray(inp["ff2_in"], f32)[0]),
        "ff2_out": np.ascontiguousarray(np.asarray(inp["ff2_out"], f32)[0]),
        "w_in": np.ascontiguousarray(np.asarray(inp["w_in"], f32)[0]),
        "w_out": np.ascontiguousarray(np.asarray(inp["w_out"], f32)[0]),
        "identf": np.eye(128, dtype=f32),
        "rwvec": np.ascontiguousarray(np.concatenate([np.asarray(inp[k], f32).reshape(-1) for k in
                 ("rw_mu", "rw_w0", "rw_a0", "rw_kk", "rw_ka", "rw_rk", "rw_gn_w", "rw_gn_b")])[None, :]),
        "cmask": _cmask(),
        "lamvec": np.ascontiguousarray(np.concatenate([np.asarray(inp[k], f32).reshape(-1) for k in ("da_lq1", "da_lq2", "da_lk1", "da_lk2")])[None, :]),
        "subln": np.ascontiguousarray(np.asarray(inp["da_subln"], f32).reshape(1, 128)),
        "relb": np.ascontiguousarray(np.asarray(inp["rel_bias_table"], f32)),
        "onehot": _onehot(),
        "selc": _selc(),
        "iotap": np.arange(128, dtype=np.int32).reshape(128, 1),
    }
    if STAGE >= 3 and os.environ.get("KDEV_NOSAMP") is None:
        shared["ck2"] = np.asarray(inp["cache_k"], f32).reshape(NPOOL * 128, 512)
        shared["cv2"] = np.asarray(inp["cache_v"], f32).reshape(NPOOL * 128, 512)
    shared.update({
        "rw_w2": np.ascontiguousarray(np.asarray(inp["rw_w2"], f32)[0]),
        "rw_a2": np.ascontiguousarray(np.asarray(inp["rw_a2"], f32)[0]),
        "rw_g2": np.ascontiguousarray(np.asarray(inp["rw_g2"], f32)[0]),
    })
    ssh = np.asarray(inp["state_shift"], f32)[0]
    swkv = np.asarray(inp["state_wkv"], f32)[0]
    in_maps = []
    for c in range(NCORES):
        m = dict(shared)
        m["ptab"] = np.ascontiguousarray(np.asarray(inp["page_table"], np.int32)[16 * c:16 * c + 16].reshape(1, 256))
        m["shift0"] = np.ascontiguousarray(ssh[16 * c:16 * c + 16])
        m["wkv0"] = np.ascontiguousarray(swkv[16 * c:16 * c + 16].reshape(128, 4096))
        m["x_all"] = np.ascontiguousarray(np.concatenate([xp[c], xs[16 * c:16 * c + 16].reshape(TS, D)], axis=0))
        in_maps.append(m)
    ncr = int(os.environ.get("KDEV_CORES", NCORES))
    t0 = time.time()
    if os.environ.get("KDEV_TRACE"):
        res = run_bass_kernel_spmd(nc, in_maps[:ncr], core_ids=list(range(ncr)), trace=True)
        print("EXEC_NS", res.exec_time_ns)
    else:
        res = run_bass_kernel_spmd(nc, in_maps[:ncr], core_ids=list(range(ncr)))
    if os.environ.get("KDEV_CORES"):
        print("run time", time.time() - t0)
    R = list(res.results) + [res.results[0]] * (NCORES - ncr)
    if os.environ.get("KDEV_DBG"):
        _CACHE["R"] = R
    y_prompt = np.stack([R[c]["y_all"][:TP] for c in range(NCORES)], 0)
    y_sample = np.concatenate([R[c]["y_all"][TP:].reshape(16, 4, D) for c in range(NCORES)], 0)
    k_prompt = np.stack([R[c]["k_out"][:TP].reshape(TP, 4, 2, 64) for c in range(NCORES)], 0)[None]
    v_prompt = np.stack([R[c]["v_out"][:TP].reshape(TP, 4, 128) for c in range(NCORES)], 0)[None]
    k_sample = np.concatenate([R[c]["k_out"][TP:].reshape(16, 4, 4, 2, 64) for c in range(NCORES)], 0)[None]
    v_sample = np.concatenate([R[c]["v_out"][TP:].reshape(16, 4, 4, 128) for c in range(NCORES)], 0)[None]
    wkv_prompt = np.stack([R[c]["wkvp_out"] for c in range(NCORES)], 0)[None]
    wkv_sample = np.concatenate([R[c]["wkvs_out"].reshape(16, 8, 64, 64) for c in range(NCORES)], 0)[None]
    shift_prompt = np.concatenate([R[c]["shp_out"] for c in range(NCORES)], 0)[None]
    shift_sample = np.concatenate([R[c]["shs_out"] for c in range(NCORES)], 0)[None]
    outs = (y_prompt, y_sample, k_prompt, v_prompt, k_sample, v_sample, wkv_prompt, wkv_sample, shift_prompt, shift_sample)
    return tuple(np.ascontiguousarray(o, dtype=f32) for o in outs)
```

```python
import math
import os
import time
import numpy as np
import concourse.bass as bass
import concourse.mybir as mybir
from concourse.bass_utils import run_bass_kernel_spmd

F32 = mybir.dt.float32
BF16 = mybir.dt.bfloat16
I32 = mybir.dt.int32
AF = mybir.ActivationFunctionType
ALU = mybir.AluOpType
AX = mybir.AxisListType

NCORES = 8
D = 1024
TP = 2048
TS = 64
TT = TP + TS
DFF = 2816
NFC = 22
PROJ = 3328
RCOLS = 1792
NPOOL = 2560
EPS = 1e-6

STAGE = int(os.environ.get('KDEV_STAGE', 3))


class Buf:
    __slots__ = ("name", "w", "r", "x")

    def __init__(self, name, excl=False):
        self.name = name
        self.w = None
        self.r = {}
        self.x = excl


class Sched:
    def __init__(self, nc):
        self.nc = nc
        self.prog = {k: [] for k in ("pe", "act", "dve", "pool", "sp")}
        self.csem = {k: nc.alloc_semaphore("c_" + k) for k in ("pe", "act", "dve", "pool")}
        self.cnt = {k: 0 for k in self.csem}
        self.known = {k: {} for k in self.prog}
        self.dsem = {}
        self.sems = {}

    def _dsem(self, name):
        if name not in self.dsem:
            s = self.nc.alloc_semaphore("d_" + name)
            self.dsem[name] = [s, 0]
        return self.dsem[name]

    def op(self, e, fn, r=(), w=(), dma=None):
        deps = []
        for b in r:
            if b.w is not None:
                deps.append(b.w)
            if b.x:
                deps.extend(b.r.values())
        for b in w:
            if b.w is not None:
                deps.append(b.w)
            deps.extend(b.r.values())
        waits = {}
        for (key, sem, val, prod) in deps:
            if prod == "pe" and e == "pe" and dma is None:
                continue
            if self.known[e].get(key, 0) >= val:
                continue
            if key not in waits or waits[key][1] < val:
                waits[key] = (sem, val)
        for key, (sem, val) in waits.items():
            self.known[e][key] = val
            self.prog[e].append(("wait", sem, val))
        if dma is None:
            self.cnt[e] += 1
            tok = ("c_" + e, self.csem[e], self.cnt[e], e)
            self.prog[e].append(("op", fn, self.csem[e], 1))
        else:
            d = self._dsem(dma)
            d[1] += 16
            tok = ("d_" + dma, d[0], d[1], None)
            self.prog[e].append(("op", fn, d[0], 16))
        for b in r:
            old = b.r.get(tok[0])
            if old is None or old[2] < tok[2]:
                b.r[tok[0]] = tok
        for b in w:
            b.w = tok
            b.r = {}
        return tok

    def barrier(self):
        for e in self.prog:
            for k in self.csem:
                if self.cnt[k] > self.known[e].get("c_" + k, 0):
                    self.known[e]["c_" + k] = self.cnt[k]
                    self.prog[e].append(("wait", self.csem[k], self.cnt[k]))
            for name, (sem, val) in self.dsem.items():
                if val > self.known[e].get("d_" + name, 0):
                    self.known[e]["d_" + name] = val
                    self.prog[e].append(("wait", sem, val))

    def emit(self):
        nc = self.nc
        engs = {"pe": "tensor", "act": "scalar", "dve": "vector", "pool": "gpsimd", "sp": "sync"}
        with nc.Block() as block:
            for k, attr in engs.items():
                prog = self.prog[k]

                def body(eng, prog=prog):
                    for it in prog:
                        if it[0] == "wait":
                            eng.wait_ge(it[1], it[2])
                        else:
                            it[1](eng).then_inc(it[2], it[3])

                getattr(block, attr)(body)


def build_program():
    nc = bass.Bass("TRN2", target_bir_lowering=False)
    S = Sched(nc)

    def din(name, shape, dt=F32):
        return nc.dram_tensor(name, list(shape), dt, kind="ExternalInput").ap()

    def dout(name, shape, dt=F32):
        return nc.dram_tensor(name, list(shape), dt, kind="ExternalOutput").ap()

    def dscr(name, shape, dt=F32):
        return nc.dram_tensor(name, list(shape), dt, kind="Internal").ap()

    x_all = din("x_all", [TT, D])
    gains = din("gains", [6, D])
    ff1_in = din("ff1_in", [D, 2 * DFF])
    ff1_out = din("ff1_out", [DFF, D])
    ff2_in = din("ff2_in", [D, 2 * DFF])
    ff2_out = din("ff2_out", [DFF, D])
    w_in = din("w_in", [D, PROJ])
    w_out = din("w_out", [D, D])
    identf_d = din("identf", [128, 128])

    y_all = dout("y_all", [TT, D])
    k_out = dout("k_out", [TT, 512])
    v_out = dout("v_out", [TT, 512])
    shp_out = dout("shp_out", [1, RCOLS])
    shs_out = dout("shs_out", [16, RCOLS])
    wkvp_out = dout("wkvp_out", [8, 64, 64])
    wkvs_out = dout("wkvs_out", [128, 4096])

    rwvec = din("rwvec", [1, 5376])
    cmask = din("cmask", [64, 4, 64])
    rw_w2 = din("rw_w2", [64, 512])
    rw_a2 = din("rw_a2", [64, 512])
    rw_g2 = din("rw_g2", [128, 512])
    shift0 = din("shift0", [16, RCOLS])
    wkv0 = din("wkv0", [128, 4096])
    prev_scr = dscr("prev_scr", [TT, RCOLS])
    lamvec = din("lamvec", [1, 256])
    subln = din("subln", [1, 128])
    relb = din("relb", [32, 4])
    onehot = din("onehot", [33, 383])
    selc = din("selc", [8, 2, 4])
    ptab = din("ptab", [1, 256], I32)
    iotap = din("iotap", [128, 1], I32)
    HAVE_SAMP = STAGE >= 3 and os.environ.get("KDEV_NOSAMP") is None
    ck2 = din("ck2", [NPOOL * 128, 512]) if HAVE_SAMP else None
    cv2 = din("cv2", [NPOOL * 128, 512]) if HAVE_SAMP else None
    r_scr = dscr("r_scr", [4, 128, 383])
    oat_scr = (dout if os.environ.get("KDEV_DBG") else dscr)("oat_scr", [TT, 512])
    h2_scr = (dout if os.environ.get("KDEV_DBG") else dscr)("h2_scr", [TT, D])
    rs_scr = dscr("rs_scr", [64, 3072])
    ys_scr = dscr("ys_scr", [128, 256])
    orw_scr = (dout if os.environ.get("KDEV_DBG") else dscr)("orw_scr", [TT, 512])
    h_scr = dout("h_scr", [TT, D]) if os.environ.get("KDEV_DBG") else dscr("h_scr", [TT, D])
    pr_scr = dscr("pr_scr", [TT, RCOLS])

    NW = 52000
    BIG = nc.alloc_sbuf_tensor("BIG", [128, NW], F32)
    identf = nc.alloc_sbuf_tensor("identf_sb", [128, 128], F32)
    identb = nc.alloc_sbuf_tensor("identb_sb", [128, 128], BF16)

    class Arena:
        def __init__(self):
            self.off = 0

        def alloc(self, shape, dt=F32):
            n = 1
            for d_ in shape[1:]:
                n *= d_
            words = n if dt in (F32, I32) else (n + 1) // 2
            assert self.off + words <= NW, ("arena overflow", self.off, words)
            v = BIG[:, self.off:self.off + words]
            self.off += words
            if dt != F32:
                v = v.bitcast(dt)[:, 0:n]
            if len(shape) > 2:
                names = " ".join("a%d" % i for i in range(len(shape) - 1))
                kw = {"a%d" % i: shape[i + 1] for i in range(len(shape) - 1)}
                v = v.rearrange("p (%s) -> p %s" % (names, names), **kw)
            if shape[0] < 128:
                v = v[0:shape[0]]
            return v

    AR = Arena()
    PS = [nc.alloc_psum_tensor("ps%d" % i, [128, 512], F32) for i in range(8)]
    PSB = [Buf("ps%d" % i, True) for i in range(8)]
    B_ident = Buf("ident")
    B_GA = [Buf("ga0"), Buf("ga1")]
    GAh = [None]

    def psb16(i):
        return PS[i][:].bitcast(BF16)

    S.op("sp", lambda e: e.dma_start(out=identf[:], in_=identf_d), w=[B_ident], dma="ident")
    S.op("dve", lambda e: e.tensor_copy(out=identb[:], in_=identf[:]), r=[B_ident], w=[B_ident])

    def load_gain(GA, slot, idx):
        S.op("sp", lambda e: e.dma_start(out=GA[:, slot, :], in_=gains[idx:idx + 1, :].partition_broadcast(128)),
             w=[B_GA[slot]], dma="ga%d" % slot)

    tiles_all = [(t * 128, 128) for t in range(16)] + [(TP, TS)]
    groups = [tiles_all[0:4], tiles_all[4:8], tiles_all[8:12], tiles_all[12:16], tiles_all[16:17]]
    if os.environ.get("KDEV_NG"):
        groups = groups[:int(os.environ["KDEV_NG"])]

    def rms_rstd(src_ap, rows, junk_ap, ss_ap, rstd_ap, rbufs, junk_b, st_b, eps=EPS, n=D):
        S.op("act", lambda e: e.activation(out=junk_ap, in_=src_ap, func=AF.Square, accum_out=ss_ap),
             r=rbufs, w=[junk_b, st_b])
        S.op("dve", lambda e: e.tensor_scalar(out=rstd_ap, in0=ss_ap, scalar1=1.0 / n, scalar2=eps,
                                              op0=ALU.mult, op1=ALU.add), r=[st_b], w=[st_b])
        S.op("act", lambda e: e.sqrt(out=rstd_ap, in_=rstd_ap), r=[st_b], w=[st_b])
        S.op("dve", lambda e: e.reciprocal(out=rstd_ap, in_=rstd_ap), r=[st_b], w=[st_b])

    def ffn_phase(tag, src_d, dst_d, w_in_d, w_out_d, gi, go):
        AR.off = 0
        Win = AR.alloc([128, 8, 2 * DFF], BF16)
        Wout = AR.alloc([128, NFC, D], BF16)
        GA = AR.alloc([128, 2, D], F32)
        B_wg = [Buf("wg%d" % i) for i in range(11)]
        B_wu = [Buf("wu%d" % i) for i in range(11)]
        B_wo = [Buf("wo%d" % i) for i in range(11)]
        w_in_v = w_in_d.rearrange("(k p) n -> p k n", p=128)
        w_out_v = w_out_d.rearrange("(f p) n -> p f n", p=128)
        for i in range(11):
            S.op("pool", lambda e, i=i: e.dma_start(out=Win[:, :, i * 256:(i + 1) * 256],
                                                     in_=w_in_v[:, :, i * 256:(i + 1) * 256]),
                 w=[B_wg[i]], dma="wg%d" % i)
            S.op("pool", lambda e, i=i: e.dma_start(out=Win[:, :, DFF + i * 256:DFF + (i + 1) * 256],
                                                     in_=w_in_v[:, :, DFF + i * 256:DFF + (i + 1) * 256]),
                 w=[B_wu[i]], dma="wu%d" % i)
        for i in range(11):
            S.op("pool", lambda e, i=i: e.dma_start(out=Wout[:, 2 * i:2 * i + 2, :], in_=w_out_v[:, 2 * i:2 * i + 2, :]),
                 w=[B_wo[i]], dma="wo%d" % i)
        load_gain(GA, 0, gi)
        load_gain(GA, 1, go)

        xld = [AR.alloc([128, D], F32) for i in range(2)]
        B_xld = [Buf("xld0"), Buf("xld1")]
        xres = [AR.alloc([128, D], F32)] * 2
        B_xres = [Buf("xres0")] * 2
        junk = AR.alloc([128, D], BF16)
        B_junk = Buf("junk")
        xn = AR.alloc([128, D], BF16)
        B_xn = Buf("xn")
        xnT = AR.alloc([128, 8, 512], BF16)
        B_xnT = [Buf("xnT%d" % i) for i in range(4)]
        hT = AR.alloc([128, NFC, 512], BF16)
        B_hT = [Buf("hT%d" % i) for i in range(NFC)]
        sg = [AR.alloc([128, 512], F32) for i in range(2)]
        B_sg = [Buf("sg0"), Buf("sg1")]
        yb = AR.alloc([128, D], F32)
        B_y = Buf("y")
        hb = [AR.alloc([128, D], F32) for i in range(2)]
        B_hb = [Buf("hb0"), Buf("hb1")]
        st = AR.alloc([128, 8], F32)
        B_st = [Buf("st0"), Buf("st1")]

        cnt = {'x': 0, 'o': 0}

        def do_group(grp):
            ntok = sum(r for _, r in grp)
            for ti, (t0, rows) in enumerate(grp):
                sl = cnt['x'] % 2
                cnt['x'] += 1
                xs = xld[sl]
                S.op("sp", lambda e, xs=xs, t0=t0, rows=rows: e.dma_start(out=xs[:rows, :], in_=src_d[t0:t0 + rows, :]),
                     w=[B_xld[sl]], dma="xld%d" % sl)
                rms_rstd(xs[:rows, :], rows, junk[:rows, :], st[:rows, 0:1], st[:rows, 1:2], [B_xld[sl]], B_junk, B_st[0])
                S.op("dve", lambda e, xs=xs, rows=rows: e.scalar_tensor_tensor(
                    out=xn[:rows, :], in0=xs[:rows, :], scalar=st[:rows, 1:2], in1=GA[:rows, 0, :],
                    op0=ALU.mult, op1=ALU.mult), r=[B_xld[sl], B_st[0], B_GA[0]], w=[B_xn])
                for kc in range(8):
                    S.op("pe", lambda e, kc=kc, rows=rows: e.transpose(
                        psb16(0)[:, kc * 128:kc * 128 + rows], xn[:rows, kc * 128:(kc + 1) * 128], identb[:rows, :rows]),
                        r=[B_xn, B_ident], w=[PSB[0]])
                S.op("act", lambda e, ti=ti, rows=rows: e.copy(
                    out=xnT[:, :, ti * 128:ti * 128 + rows],
                    in_=psb16(0).rearrange("p (k t) -> p k t", k=8)[:, :, 0:rows]),
                    r=[PSB[0]], w=[B_xnT[ti]])
            for fc in range(NFC):
                pa = 1 + (fc % 2) * 2
                pb = pa + 1
                for kc in range(8):
                    S.op("pe", lambda e, fc=fc, kc=kc, pa=pa: e.matmul(
                        PS[pa][:, 0:ntok], lhsT=Win[:, kc, fc * 128:(fc + 1) * 128], rhs=xnT[:, kc, 0:ntok],
                        start=(kc == 0), stop=(kc == 7)),
                        r=[B_wg[fc // 2]] + B_xnT[:len(grp)], w=[PSB[pa]])
                for kc in range(8):
                    S.op("pe", lambda e, fc=fc, kc=kc, pb=pb: e.matmul(
                        PS[pb][:, 0:ntok], lhsT=Win[:, kc, DFF + fc * 128:DFF + (fc + 1) * 128], rhs=xnT[:, kc, 0:ntok],
                        start=(kc == 0), stop=(kc == 7)),
                        r=[B_wu[fc // 2]] + B_xnT[:len(grp)], w=[PSB[pb]])
                sgi = fc % 2
                S.op("act", lambda e, pa=pa, sgi=sgi: e.activation(out=sg[sgi][:, 0:ntok], in_=PS[pa][:, 0:ntok], func=AF.Silu),
                     r=[PSB[pa]], w=[B_sg[sgi]])
                S.op("dve", lambda e, fc=fc, pb=pb, sgi=sgi: e.tensor_tensor(
                    out=hT[:, fc, 0:ntok], in0=sg[sgi][:, 0:ntok], in1=PS[pb][:, 0:ntok], op=ALU.mult),
                    r=[B_sg[sgi], PSB[pb]], w=[B_hT[fc]])
            for ti, (t0, rows) in enumerate(grp):
                rs = cnt['o'] % 2
                cnt['o'] += 1
                xr_ = xres[rs]
                S.op("sp", lambda e, xr_=xr_, t0=t0, rows=rows: e.dma_start(out=xr_[:rows, :], in_=src_d[t0:t0 + rows, :]),
                     w=[B_xres[rs]], dma="xres0")
                for nh in range(2):
                    pc = 5 + nh
                    for fc in range(NFC):
                        S.op("pe", lambda e, fc=fc, nh=nh, pc=pc, ti=ti, rows=rows: e.matmul(
                            PS[pc][:rows, :], lhsT=hT[:, fc, ti * 128:ti * 128 + rows], rhs=Wout[:, fc, nh * 512:(nh + 1) * 512],
                            start=(fc == 0), stop=(fc == NFC - 1)),
                            r=[B_hT[fc], B_wo[fc // 2]], w=[PSB[pc]])
                    S.op("act", lambda e, nh=nh, pc=pc, rows=rows: e.copy(out=yb[:rows, nh * 512:(nh + 1) * 512], in_=PS[pc][:rows, :]),
                         r=[PSB[pc]], w=[B_y])
                rms_rstd(yb[:rows, :], rows, junk[:rows, :], st[:rows, 2:3], st[:rows, 3:4], [B_y], B_junk, B_st[1])
                ho = hb[rs]
                S.op("dve", lambda e, rows=rows: e.scalar_tensor_tensor(
                    out=yb[:rows, :], in0=yb[:rows, :], scalar=st[:rows, 3:4], in1=GA[:rows, 1, :],
                    op0=ALU.mult, op1=ALU.mult), r=[B_st[1], B_GA[1]], w=[B_y])
                S.op("dve", lambda e, rows=rows, ho=ho, xr_=xr_: e.scalar_tensor_tensor(
                    out=ho[:rows, :], in0=yb[:rows, :], scalar=0.5, in1=xr_[:rows, :],
                    op0=ALU.mult, op1=ALU.add), r=[B_y, B_xres[rs]], w=[B_hb[rs]])
                S.op("sp", lambda e, rows=rows, ho=ho, t0=t0: e.dma_start(out=dst_d[t0:t0 + rows, :], in_=ho[:rows, :]),
                     r=[B_hb[rs]], dma="hb%d" % rs)

        for grp in groups:
            do_group(grp)
        S.barrier()

    def proj_phase(qT, kT, Vb):
        Wi = AR.alloc([128, 8, PROJ], BF16)
        GA = AR.alloc([128, 2, D], F32)
        blocks = [(0, 512), (512, 1024), (1024, 1536), (1536, 2048), (2048, 2560), (2560, 3072), (3072, 3328)]
        B_wi = [Buf("wi%d" % i) for i in range(7)]
        w_in_v = w_in.rearrange("(k p) n -> p k n", p=128)
        for i, (c0, c1) in enumerate(blocks):
            S.op("pool", lambda e, c0=c0, c1=c1: e.dma_start(out=Wi[:, :, c0:c1], in_=w_in_v[:, :, c0:c1]),
                 w=[B_wi[i]], dma="wg%d" % i)
        load_gain(GA, 0, 2)
        S.op("dve", lambda e: e.memset(Vb[:, :, :, 128:129], 1.0), w=[B_vb])
        xld = [AR.alloc([128, D], F32) for i in range(2)]
        B_xld = [Buf("xld0"), Buf("xld1")]
        junk = AR.alloc([128, D], BF16)
        B_junk = Buf("junk")
        xn = AR.alloc([128, D], BF16)
        B_xn = Buf("xn")
        xnT = AR.alloc([128, 8, 512], BF16)
        B_xnT = [Buf("xnT%d" % i) for i in range(4)]
        ob = [AR.alloc([128, 512], F32) for i in range(3)]
        B_ob = [Buf("hb0"), Buf("hb1"), Buf("y")]
        st = AR.alloc([128, 8], F32)
        B_st = Buf("st0")
        cnt = {'x': 0, 'o': 0}

        def do_group(grp):
            ntok = sum(r for _, r in grp)
            g0 = grp[0][0]
            for ti, (t0, rows) in enumerate(grp):
                sl = cnt['x'] % 2
                cnt['x'] += 1
                xs = xld[sl]
                S.op("sp", lambda e, xs=xs, t0=t0, rows=rows: e.dma_start(out=xs[:rows, :], in_=h_scr[t0:t0 + rows, :]),
                     w=[B_xld[sl]], dma="xld%d" % sl)
                rms_rstd(xs[:rows, :], rows, junk[:rows, :], st[:rows, 0:1], st[:rows, 1:2], [B_xld[sl]], B_junk, B_st)
                S.op("dve", lambda e, xs=xs, rows=rows: e.scalar_tensor_tensor(
                    out=xn[:rows, :], in0=xs[:rows, :], scalar=st[:rows, 1:2], in1=GA[:rows, 0, :],
                    op0=ALU.mult, op1=ALU.mult), r=[B_xld[sl], B_st, B_GA[0]], w=[B_xn])
                for kc in range(8):
                    S.op("pe", lambda e, kc=kc, rows=rows: e.transpose(
                        psb16(0)[:, kc * 128:kc * 128 + rows], xn[:rows, kc * 128:(kc + 1) * 128], identb[:rows, :rows]),
                        r=[B_xn, B_ident], w=[PSB[0]])
                S.op("act", lambda e, ti=ti, rows=rows: e.copy(
                    out=xnT[:, :, ti * 128:ti * 128 + rows],
                    in_=psb16(0).rearrange("p (k t) -> p k t", k=8)[:, :, 0:rows]),
                    r=[PSB[0]], w=[B_xnT[ti]])
            for which in range(2):
                for h in range(4):
                    pa = 1 + (h % 2)
                    c0 = which * 512 + h * 128
                    for kc in range(8):
                        S.op("pe", lambda e, kc=kc, pa=pa, c0=c0: e.matmul(
                            PS[pa][:, 0:ntok], lhsT=Wi[:, kc, c0:c0 + 128], rhs=xnT[:, kc, 0:ntok],
                            start=(kc == 0), stop=(kc == 7)),
                            r=[B_wi[which]] + B_xnT[:len(grp)], w=[PSB[pa]])
                    dstT = qT if which == 0 else kT
                    sc = 0.125 if which == 0 else 1.0
                    S.op("act", lambda e, pa=pa, dstT=dstT, h=h, sc=sc: e.activation(
                        out=dstT[:, h, g0:g0 + ntok], in_=PS[pa][:, 0:ntok], func=AF.Copy, scale=sc),
                        r=[PSB[pa]], w=[B_qk])
            for ti, (t0, rows) in enumerate(grp):
                for bi in range(1, 7):
                    c0, c1 = blocks[bi]
                    wdt = c1 - c0
                    pc = 3 + (cnt['o'] % 3)
                    oi = cnt['o'] % 3
                    cnt['o'] += 1
                    for kc in range(8):
                        S.op("pe", lambda e, kc=kc, pc=pc, c0=c0, c1=c1, wdt=wdt, ti=ti, rows=rows: e.matmul(
                            PS[pc][:rows, 0:wdt], lhsT=xnT[:, kc, ti * 128:ti * 128 + rows], rhs=Wi[:, kc, c0:c1],
                            start=(kc == 0), stop=(kc == 7)),
                            r=[B_wi[bi], B_xnT[ti]], w=[PSB[pc]])
                    o_ = ob[oi]
                    S.op("act", lambda e, pc=pc, o_=o_, wdt=wdt, rows=rows: e.copy(out=o_[:rows, 0:wdt], in_=PS[pc][:rows, 0:wdt]),
                         r=[PSB[pc]], w=[B_ob[oi]])
                    if bi == 1:
                        dst = k_out[t0:t0 + rows, :]
                    elif bi == 2:
                        dst = v_out[t0:t0 + rows, :]
                        if t0 < TP:
                            tl = t0 // 128
                            S.op("dve", lambda e, pc=pc, tl=tl: e.tensor_copy(
                                out=Vb[:, tl, :, 0:128], in_=PS[pc][:, :].rearrange("p (h d) -> p h d", h=4)),
                                r=[PSB[pc]], w=[B_vb])
                    else:
                        dst = pr_scr[t0:t0 + rows, c0 - 1536:c1 - 1536]
                    S.op("sp", lambda e, o_=o_, dst=dst, wdt=wdt, rows=rows: e.dma_start(out=dst, in_=o_[:rows, 0:wdt]),
                         r=[B_ob[oi]], dma="ob%d" % oi)

        for grp in groups:
            do_group(grp)
        S.barrier()
        if os.environ.get("KDEV_NOSH"):
            return
        S.op("sp", lambda e: e.dma_start(out=shp_out, in_=pr_scr[TP - 1:TP, :]), dma="sh0")
        S.op("sp", lambda e: e.dma_start(out=shs_out, in_=pr_scr[TP:TT, :].rearrange("(b t) n -> b t n", t=4)[:, 3, :]), dma="sh1")


    C0 = math.exp(-0.5)

    class Tl:
        def __init__(self, shape, dt=F32, name="t"):
            self.ap = AR.alloc(shape, dt)
            self.b = Buf(name)

    def rwkv_phase():
        AR.off = 0
        RV = Tl([64, 5376], F32, "RV")
        LW = Tl([128, 2, 512], BF16, "LW")
        CM = Tl([64, 4, 64], F32, "CM")
        S.op("sp", lambda e: e.dma_start(out=RV.ap, in_=rwvec.partition_broadcast(64)), w=[RV.b], dma="RV")
        S.op("sp", lambda e: e.dma_start(out=CM.ap, in_=cmask), w=[CM.b], dma="CM")
        S.op("pool", lambda e: e.dma_start(out=LW.ap[0:64, 0, :], in_=rw_w2), w=[LW.b], dma="LW")
        S.op("pool", lambda e: e.dma_start(out=LW.ap[64:128, 0, :], in_=rw_a2), w=[LW.b], dma="LW")
        S.op("pool", lambda e: e.dma_start(out=LW.ap[:, 1, :], in_=rw_g2), w=[LW.b], dma="LW")
        ZJ = Tl([1, 8], F32, "ZJ")
        ZR = Tl([1, RCOLS], F32, "ZR")
        S.op("dve", lambda e: e.memset(ZR.ap, 0.0), w=[ZR.b])
        S.op("sp", lambda e: e.dma_start(out=prev_scr[0:1, :], in_=ZR.ap), r=[ZR.b], dma="pv0")
        S.op("sp", lambda e: e.dma_start(out=prev_scr[1:TP, :], in_=pr_scr[0:TP - 1, :]), dma="pv1")
        S.op("sp", lambda e: e.dma_start(
            out=prev_scr[TP:TT, :].rearrange("(b t) n -> b t n", t=4)[:, 1:4, :],
            in_=pr_scr[TP:TT, :].rearrange("(b t) n -> b t n", t=4)[:, 0:3, :]), dma="pv2")
        S.op("sp", lambda e: e.dma_start(
            out=prev_scr[TP:TT, :].rearrange("(b t) n -> b t n", t=4)[:, 0, :], in_=shift0), dma="pv3")
        S.barrier()
        mark = AR.off

        def vec(i0, n, nch):
            return RV.ap[:, i0:i0 + n].unsqueeze(1).to_broadcast([64, nch, n])

        def d1a(tok0, nch, TF=Tl, pname="P"):
            P = TF([64, nch, RCOLS], F32, pname)
            PV = TF([64, nch, RCOLS], F32, "PV")
            S.op("sp", lambda e: e.dma_start(out=P.ap, in_=pr_scr[tok0:tok0 + 64 * nch, :].rearrange("(c t) n -> t c n", t=64)),
                 w=[P.b], dma=pname)
            S.op("sp", lambda e: e.dma_start(out=PV.ap, in_=prev_scr[tok0:tok0 + 64 * nch, :].rearrange("(c t) n -> t c n", t=64)),
                 w=[PV.b], dma="PV")
            if False:
                P2b, PV2b = Buf("P2"), Buf("PV2")
                P2b.w, PV2b.w = P.b.w, PV.b.w
                CS = 1152
                for (eng, c0, c1, pb, pvb) in (("dve", 0, CS, P.b, PV.b), ("pool", CS, RCOLS, P2b, PV2b)):
                    S.op(eng, lambda e, c0=c0, c1=c1: e.tensor_tensor(out=PV.ap[:, :, c0:c1], in0=PV.ap[:, :, c0:c1], in1=P.ap[:, :, c0:c1], op=ALU.subtract), r=[pb], w=[pvb])
                    S.op(eng, lambda e, c0=c0, c1=c1: e.tensor_tensor(out=PV.ap[:, :, c0:c1], in0=PV.ap[:, :, c0:c1],
                                                                      in1=RV.ap[:, c0:c1].unsqueeze(1).to_broadcast([64, nch, c1 - c0]), op=ALU.mult), r=[RV.b], w=[pvb])
                    S.op(eng, lambda e, c0=c0, c1=c1: e.tensor_tensor(out=P.ap[:, :, c0:c1], in0=P.ap[:, :, c0:c1], in1=PV.ap[:, :, c0:c1], op=ALU.add), r=[pvb], w=[pb])
                S.op("pool", lambda e: e.memset(ZJ.ap, 0.0), r=[P2b], w=[P.b, ZJ.b])
            else:
                S.op("dve", lambda e: e.tensor_tensor(out=PV.ap, in0=PV.ap, in1=P.ap, op=ALU.subtract), r=[P.b], w=[PV.b])
                S.op("dve", lambda e: e.tensor_tensor(out=PV.ap, in0=PV.ap, in1=vec(0, RCOLS, nch), op=ALU.mult), r=[RV.b], w=[PV.b])
                S.op("dve", lambda e: e.tensor_tensor(out=P.ap, in0=P.ap, in1=PV.ap, op=ALU.add), r=[PV.b], w=[P.b])
            return P

        def d1(tok0, nch, TF=Tl, P=None):
            T = {}
            if P is None:
                P = d1a(tok0, nch)
            LO = TF([64, nch, 256], BF16, "LO")
            S.op("act", lambda e: e.activation(out=LO.ap[:, :, 0:64], in_=P.ap[:, :, 1536:1600], func=AF.Tanh), r=[P.b], w=[LO.b])
            S.op("act", lambda e: e.copy(out=LO.ap[:, :, 64:128], in_=P.ap[:, :, 1600:1664]), r=[P.b], w=[LO.b])
            S.op("act", lambda e: e.activation(out=LO.ap[:, :, 128:256], in_=P.ap[:, :, 1664:1792], func=AF.Sigmoid), r=[P.b], w=[LO.b])
            LOT = TF([128, nch, 2, 64], BF16, "LOT")
            for c in range(nch):
                for j in range(2):
                    S.op("pe", lambda e, c=c, j=j: e.transpose(
                        psb16(0)[:, (c * 2 + j) * 64:(c * 2 + j + 1) * 64], LO.ap[:, c, j * 128:(j + 1) * 128], identb[:64, :64]),
                        r=[LO.b, B_ident], w=[PSB[0]])
            S.op("act", lambda e: e.copy(out=LOT.ap, in_=psb16(0)[:, 0:nch * 128].rearrange("p (c j t) -> p c j t", c=nch, j=2)),
                 r=[PSB[0]], w=[LOT.b])
            SG = TF([64, nch, 512], F32, "SG")
            AA = TF([64, nch, 512], F32, "AA")
            GG = TF([64, nch, 512], F32, "GG")
            for c in range(nch):
                S.op("pe", lambda e, c=c: e.matmul(PS[1][:64, :], lhsT=LOT.ap[0:64, c, 0, :], rhs=LW.ap[0:64, 0, :], start=True, stop=True),
                     r=[LOT.b, LW.b], w=[PSB[1]])
                S.op("pe", lambda e, c=c: e.matmul(PS[2][:64, :], lhsT=LOT.ap[64:128, c, 0, :], rhs=LW.ap[64:128, 0, :], start=True, stop=True),
                     r=[LOT.b, LW.b], w=[PSB[2]])
                S.op("pe", lambda e, c=c: e.matmul(PS[3][:64, :], lhsT=LOT.ap[:, c, 1, :], rhs=LW.ap[:, 1, :], start=True, stop=True),
                     r=[LOT.b, LW.b], w=[PSB[3]])
                S.op("dve", lambda e, c=c: e.tensor_tensor(out=SG.ap[:, c, :], in0=PS[1][:64, :], in1=RV.ap[:, 1792:2304], op=ALU.add),
                     r=[PSB[1], RV.b], w=[SG.b])
                S.op("dve", lambda e, c=c: e.tensor_tensor(out=AA.ap[:, c, :], in0=PS[2][:64, :], in1=RV.ap[:, 2304:2816], op=ALU.add),
                     r=[PSB[2], RV.b], w=[AA.b])
                S.op("act", lambda e, c=c: e.copy(out=GG.ap[:, c, :], in_=PS[3][:64, :]), r=[PSB[3]], w=[GG.b])
            S.op("act", lambda e: e.activation(out=SG.ap, in_=SG.ap, func=AF.Sigmoid), r=[SG.b], w=[SG.b])
            S.op("act", lambda e: e.activation(out=AA.ap, in_=AA.ap, func=AF.Sigmoid), r=[AA.b], w=[AA.b])
            KK = TF([64, nch, 512], F32, "KK")
            T1 = TF([64, nch, 512], F32, "T1")
            SS = TF([64, nch * 8], F32, "SS")
            kr = P.ap[:, :, 512:1024]
            PL = "pool" if os.environ.get("KDEV_NOPOOL") is None else "dve"
            S.op(PL, lambda e: e.tensor_tensor(out=KK.ap, in0=kr, in1=vec(2816, 512, nch), op=ALU.mult), r=[P.b, RV.b], w=[KK.b])
            S.op(PL, lambda e: e.tensor_tensor(out=T1.ap, in0=KK.ap, in1=KK.ap, op=ALU.mult), r=[KK.b], w=[T1.b])
            S.op("dve", lambda e: e.tensor_reduce(out=SS.ap, in_=T1.ap.rearrange("p c (h n) -> p (c h) n", n=64), axis=AX.X, op=ALU.add),
                 r=[T1.b], w=[SS.b])
            S.op("dve", lambda e: e.tensor_scalar(out=SS.ap, in0=SS.ap, scalar1=1e-24, scalar2=None, op0=ALU.add), r=[SS.b], w=[SS.b])
            S.op("act", lambda e: e.sqrt(out=SS.ap, in_=SS.ap), r=[SS.b], w=[SS.b])
            S.op("dve", lambda e: e.reciprocal(out=SS.ap, in_=SS.ap), r=[SS.b], w=[SS.b])
            S.op(PL, lambda e: e.tensor_tensor(
                out=KK.ap.rearrange("p c (h n) -> p (c h) n", n=64), in0=KK.ap.rearrange("p c (h n) -> p (c h) n", n=64),
                in1=SS.ap.unsqueeze(2).to_broadcast([64, nch * 8, 64]), op=ALU.mult), r=[SS.b], w=[KK.b])
            KM = TF([64, nch, 512], F32, "KM")
            S.op("dve", lambda e: e.scalar_tensor_tensor(out=T1.ap, in0=AA.ap, scalar=-1.0, in1=vec(3328, 512, nch), op0=ALU.add, op1=ALU.mult),
                 r=[AA.b, RV.b], w=[T1.b])
            S.op("dve", lambda e: e.scalar_tensor_tensor(out=KM.ap, in0=T1.ap, scalar=1.0, in1=kr, op0=ALU.add, op1=ALU.mult),
                 r=[T1.b, P.b], w=[KM.b])
            BE = TF([64, nch, 512], F32, "BE")
            S.op("dve", lambda e: e.tensor_tensor(out=BE.ap, in0=KK.ap, in1=AA.ap, op=ALU.mult), r=[KK.b, AA.b], w=[BE.b])
            T.update(P=P, SG=SG, AA=AA, GG=GG, KK=KK, KM=KM, BE=BE, T1=T1)
            return T

        def sample_part():
            AR.off = mark
            T = d1(TP, 1)
            P, SG, KK, KM, BE = T["P"], T["SG"], T["KK"], T["KM"], T["BE"]
            RS = Tl([64, 8, 6, 64], F32, "RS")

            def hv(ap):
                return ap.rearrange("p c (h n) -> p (c h) n", n=64)
            S.op("act", lambda e: e.activation(out=RS.ap[:, :, 0, :], in_=hv(SG.ap), func=AF.Exp, scale=-C0), r=[SG.b], w=[RS.b])
            S.op("dve", lambda e: e.tensor_copy(out=RS.ap[:, :, 1, :], in_=hv(KM.ap)), r=[KM.b], w=[RS.b])
            S.op("dve", lambda e: e.tensor_copy(out=RS.ap[:, :, 2, :], in_=hv(P.ap[:, :, 1024:1536])), r=[P.b], w=[RS.b])
            S.op("dve", lambda e: e.tensor_scalar(out=RS.ap[:, :, 3, :], in0=hv(KK.ap), scalar1=-1.0, scalar2=None, op0=ALU.mult), r=[KK.b], w=[RS.b])
            S.op("dve", lambda e: e.tensor_copy(out=RS.ap[:, :, 4, :], in_=hv(BE.ap)), r=[BE.b], w=[RS.b])
            S.op("dve", lambda e: e.tensor_copy(out=RS.ap[:, :, 5, :], in_=hv(P.ap[:, :, 0:512])), r=[P.b], w=[RS.b])
            S.op("sp", lambda e: e.dma_start(out=rs_scr, in_=RS.ap.rearrange("p h q n -> p (h q n)")), r=[RS.b], dma="RSo")
            X = Tl([128, 4, 6, 64], F32, "X")
            St = Tl([128, 64, 64], F32, "St")
            TM = Tl([128, 64, 64], F32, "TM")
            SA = Tl([128, 64], F32, "SA")
            YS = Tl([128, 4, 64], F32, "YS")
            S.op("sp", lambda e: e.dma_start(out=St.ap.rearrange("p v k -> p (v k)"), in_=wkv0), w=[St.b], dma="St")
            S.barrier()
            for b in range(16):
                S.op("sp", lambda e, b=b: e.dma_start(
                    out=X.ap[8 * b:8 * b + 8, :, :, :].rearrange("p t q n -> p t (q n)"),
                    in_=rs_scr[4 * b:4 * b + 4, :].rearrange("t (h x) -> h t x", h=8)), w=[X.b], dma="X")
            for t in range(4):
                def bk(q, t=t):
                    return X.ap[:, t, q, :].unsqueeze(1).to_broadcast([128, 64, 64])
                S.op("dve", lambda e, bk=bk: e.tensor_tensor(out=TM.ap, in0=St.ap, in1=bk(3), op=ALU.mult), r=[St.b, X.b], w=[TM.b])
                S.op("dve", lambda e: e.tensor_reduce(out=SA.ap, in_=TM.ap, axis=AX.X, op=ALU.add), r=[TM.b], w=[SA.b])
                S.op("dve", lambda e, bk=bk: e.tensor_tensor(out=St.ap, in0=St.ap, in1=bk(0), op=ALU.mult), r=[X.b], w=[St.b])
                S.op("dve", lambda e, bk=bk: e.tensor_tensor(out=TM.ap, in0=SA.ap.unsqueeze(2).to_broadcast([128, 64, 64]), in1=bk(4), op=ALU.mult),
                     r=[SA.b, X.b], w=[TM.b])
                S.op("dve", lambda e: e.tensor_tensor(out=St.ap, in0=St.ap, in1=TM.ap, op=ALU.add), r=[TM.b], w=[St.b])
                S.op("dve", lambda e, bk=bk, t=t: e.tensor_tensor(out=TM.ap, in0=X.ap[:, t, 2, :].unsqueeze(2).to_broadcast([128, 64, 64]), in1=bk(1), op=ALU.mult),
                     r=[X.b], w=[TM.b])
                S.op("dve", lambda e: e.tensor_tensor(out=St.ap, in0=St.ap, in1=TM.ap, op=ALU.add), r=[TM.b], w=[St.b])
                S.op("dve", lambda e, bk=bk: e.tensor_tensor(out=TM.ap, in0=St.ap, in1=bk(5), op=ALU.mult), r=[St.b, X.b], w=[TM.b])
                S.op("dve", lambda e, t=t: e.tensor_reduce(out=YS.ap[:, t, :], in_=TM.ap, axis=AX.X, op=ALU.add), r=[TM.b], w=[YS.b])
            S.op("sp", lambda e: e.dma_start(out=wkvs_out, in_=St.ap.rearrange("p v k -> p (v k)")), r=[St.b], dma="St")
            S.op("sp", lambda e: e.dma_start(out=ys_scr, in_=YS.ap.rearrange("p t n -> p (t n)")), r=[YS.b], dma="YSo")
            S.barrier()
            YT = Tl([64, 1, 512], F32, "YT")
            for b in range(16):
                S.op("sp", lambda e, b=b: e.dma_start(
                    out=YT.ap[4 * b:4 * b + 4, 0, :].rearrange("t (h n) -> t h n", h=8),
                    in_=ys_scr[8 * b:8 * b + 8, :].rearrange("h (t n) -> t h n", t=4)), w=[YT.b], dma="YT")
            d4(T, YT.ap, YT.b, TP, 1)
            S.barrier()

        GN_EPS = 64e-5

        def hv(ap):
            return ap.rearrange("p c (h n) -> p (c h) n", n=64)

        def d4(T, Yap, Yb, tok0, nch, TF=Tl):
            P, AA, GG, KM, T1 = T["P"], T["AA"], T["GG"], T["KM"], T["T1"]
            G = nch * 8
            MU = TF([64, G], F32, "MU")
            VR = TF([64, G], F32, "VR")
            YC = TF([64, nch, 512], F32, "YC")

            def hv(ap):
                return ap.rearrange("p c (h n) -> p c h n", n=64)

            def g3(t_):
                return t_.ap.rearrange("p (c h) -> p c h", h=8)

            def bl(t_):
                return g3(t_).unsqueeze(3).to_broadcast([64, nch, 8, 64])
            S.op("dve", lambda e: e.tensor_reduce(out=g3(MU), in_=hv(Yap), axis=AX.X, op=ALU.add), r=[Yb], w=[MU.b])
            S.op("dve", lambda e: e.tensor_scalar(out=MU.ap, in0=MU.ap, scalar1=1.0 / 64, scalar2=None, op0=ALU.mult), r=[MU.b], w=[MU.b])
            S.op("dve", lambda e: e.tensor_tensor(out=hv(YC.ap), in0=hv(Yap), in1=bl(MU), op=ALU.subtract), r=[Yb, MU.b], w=[YC.b])
            S.op("dve", lambda e: e.tensor_tensor(out=T1.ap, in0=YC.ap, in1=YC.ap, op=ALU.mult), r=[YC.b], w=[T1.b])
            S.op("dve", lambda e: e.tensor_reduce(out=g3(VR), in_=hv(T1.ap), axis=AX.X, op=ALU.add), r=[T1.b], w=[VR.b])
            S.op("dve", lambda e: e.tensor_scalar(out=VR.ap, in0=VR.ap, scalar1=1.0 / 64, scalar2=GN_EPS, op0=ALU.mult, op1=ALU.add), r=[VR.b], w=[VR.b])
            S.op("act", lambda e: e.sqrt(out=VR.ap, in_=VR.ap), r=[VR.b], w=[VR.b])
            S.op("dve", lambda e: e.reciprocal(out=VR.ap, in_=VR.ap), r=[VR.b], w=[VR.b])
            S.op("dve", lambda e: e.tensor_tensor(out=hv(YC.ap), in0=hv(YC.ap), in1=bl(VR), op=ALU.mult), r=[VR.b], w=[YC.b])
            S.op("dve", lambda e: e.tensor_tensor(out=YC.ap, in0=YC.ap, in1=vec(4352, 512, nch), op=ALU.mult), r=[RV.b], w=[YC.b])
            S.op("dve", lambda e: e.tensor_tensor(out=YC.ap, in0=YC.ap, in1=vec(4864, 512, nch), op=ALU.add), r=[RV.b], w=[YC.b])
            S.op("dve", lambda e: e.tensor_tensor(out=T1.ap, in0=P.ap[:, :, 0:512], in1=KM.ap, op=ALU.mult), r=[P.b, KM.b], w=[T1.b])
            S.op("dve", lambda e: e.tensor_tensor(out=T1.ap, in0=T1.ap, in1=vec(3840, 512, nch), op=ALU.mult), r=[RV.b], w=[T1.b])
            S.op("dve", lambda e: e.tensor_reduce(out=g3(MU), in_=hv(T1.ap), axis=AX.X, op=ALU.add), r=[T1.b], w=[MU.b])
            S.op("dve", lambda e: e.tensor_tensor(out=hv(T1.ap), in0=hv(P.ap[:, :, 1024:1536]), in1=bl(MU), op=ALU.mult), r=[P.b, MU.b], w=[T1.b])
            S.op("dve", lambda e: e.tensor_tensor(out=YC.ap, in0=YC.ap, in1=T1.ap, op=ALU.add), r=[T1.b], w=[YC.b])
            S.op("dve", lambda e: e.tensor_tensor(out=YC.ap, in0=YC.ap, in1=GG.ap, op=ALU.mult), r=[GG.b], w=[YC.b])
            S.op("sp", lambda e: e.dma_start(out=orw_scr[tok0:tok0 + 64 * nch, :].rearrange("(c t) n -> t c n", t=64), in_=YC.ap),
                 r=[YC.b], dma="YC")

        bank = [0]

        def nb():
            bank[0] = (bank[0] % 7) + 1
            return bank[0]

        def prompt_part():
            AR.off = mark
            H = [Tl([64, 8, 64], F32, "H0"), Tl([64, 8, 64], F32, "H1")]
            S.op("dve", lambda e: e.memset(H[0].ap, 0.0), w=[H[0].b])
            umark = AR.off
            ucache = {}

            def UT(shape, dt=F32, name="t"):
                if name not in ucache:
                    ucache[name] = Tl(shape, dt, name)
                return ucache[name]
            Pn = d1a(0, 2, UT, "Pp0")
            cur = 0
            I64 = identf[:64, :64]
            F32R = mybir.dt.float32r
            USE_R = False

            def rr(ap):
                return ap.bitcast(F32R) if (USE_R and ap.dtype == F32) else ap
            def do_unit(u, Pin, cur):
                Pnext = None
                T = d1(u * 128, 2, UT, Pin)
                P, SG, AA, KK, KM, BE, T1 = T["P"], T["SG"], T["AA"], T["KK"], T["KM"], T["BE"], T["T1"]
                RWL = int(os.environ.get("KDEV_RW", 9))
                if RWL < 1:
                    S.barrier()
                    return cur, Pnext
                GM = UT([64, 2, 512], F32, "GM")
                GI = UT([64, 2, 512], F32, "GI")
                GP = UT([64, 2, 512], F32, "GP")
                for c in range(2):
                    bk = nb()
                    S.op("pe", lambda e, c=c, bk=bk: e.matmul(PS[bk][:64, :], lhsT=CM.ap[:, 2, :], rhs=SG.ap[:, c, :], start=True, stop=True),
                         r=[CM.b, SG.b], w=[PSB[bk]])
                    S.op("act", lambda e, c=c, bk=bk: e.activation(out=GM.ap[:, c, :], in_=PS[bk][:64, :], func=AF.Exp, scale=-C0), r=[PSB[bk]], w=[GM.b])
                    S.op("act", lambda e, c=c, bk=bk: e.activation(out=GI.ap[:, c, :], in_=PS[bk][:64, :], func=AF.Exp, scale=C0), r=[PSB[bk]], w=[GI.b])
                    S.op("dve", lambda e, c=c, bk=bk: e.tensor_tensor(out=T1.ap[:, c, :], in0=PS[bk][:64, :], in1=SG.ap[:, c, :], op=ALU.subtract),
                         r=[PSB[bk], SG.b], w=[T1.b])
                S.op("act", lambda e: e.activation(out=GP.ap, in_=T1.ap, func=AF.Exp, scale=-C0), r=[T1.b], w=[GP.b])
                AL = UT([64, 2, 512], F32, "AL")
                BT = UT([64, 2, 512], F32, "BT")
                KT = UT([64, 2, 512], F32, "KT")
                RB = UT([64, 2, 512], F32, "RB")
                S.op("dve", lambda e: e.scalar_tensor_tensor(out=AL.ap, in0=KK.ap, scalar=-1.0, in1=GP.ap, op0=ALU.mult, op1=ALU.mult), r=[KK.b, GP.b], w=[AL.b])
                PL = "pool" if os.environ.get("KDEV_NOPOOL") is None else "dve"
                S.op(PL, lambda e: e.tensor_tensor(out=BT.ap, in0=BE.ap, in1=GI.ap, op=ALU.mult), r=[BE.b, GI.b], w=[BT.b])
                S.op("dve", lambda e: e.tensor_tensor(out=KT.ap, in0=KM.ap, in1=GI.ap, op=ALU.mult), r=[KM.b, GI.b], w=[KT.b])
                S.op(PL, lambda e: e.tensor_tensor(out=RB.ap, in0=P.ap[:, :, 0:512], in1=GM.ap, op=ALU.mult), r=[P.b, GM.b], w=[RB.b])
                GC = UT([64, 16], F32, "GC")
                bk = nb()
                for c in range(2):
                    for h in range(8):
                        S.op("pe", lambda e, c=c, h=h, bk=bk: e.matmul(
                            PS[bk][:64, c * 8 + h:c * 8 + h + 1], lhsT=GM.ap[:, c, h * 64:(h + 1) * 64], rhs=CM.ap[:, 3, 0:1], start=True, stop=True),
                            r=[GM.b, CM.b], w=[PSB[bk]])
                S.op("act", lambda e, bk=bk: e.copy(out=GC.ap, in_=PS[bk][:64, 0:16]), r=[PSB[bk]], w=[GC.b])
                if RWL < 2:
                    S.barrier()
                    return cur, Pnext
                XT = {}
                for nm, src in (("RB", RB), ("AL", AL), ("BT", BT), ("KT", KT)):
                    dst = UT([64, 2, 8, 64], F32, nm + "T")
                    XT[nm] = dst
                    for c in range(2):
                        bk = nb()
                        for h in range(8):
                            S.op("pe", lambda e, c=c, h=h, bk=bk, src=src: e.transpose(
                                PS[bk][:64, h * 64:(h + 1) * 64], src.ap[:, c, h * 64:(h + 1) * 64], I64),
                                r=[src.b, B_ident], w=[PSB[bk]])
                        eng = "act" if c == 0 else "dve"
                        if eng == "act":
                            S.op("act", lambda e, c=c, bk=bk, dst=dst: e.copy(out=dst.ap[:, c, :, :], in_=PS[bk][:64, :].rearrange("p (h t) -> p h t", h=8)),
                                 r=[PSB[bk]], w=[dst.b])
                        else:
                            S.op("dve", lambda e, c=c, bk=bk, dst=dst: e.tensor_copy(out=dst.ap[:, c, :, :], in_=PS[bk][:64, :].rearrange("p (h t) -> p h t", h=8)),
                                 r=[PSB[bk]], w=[dst.b])
                RBT, ALT, BTT, KTT = XT["RB"], XT["AL"], XT["BT"], XT["KT"]
                if u + 1 < 16:
                    Pnext = d1a((u + 1) * 128, 2, UT, "Pp%d" % ((u + 1) % 2))
                DT2 = BF16 if os.environ.get("KDEV_D2F32") is None else F32
                Lm = [UT([64, 2, 8, 64], DT2, "L0"), UT([64, 2, 8, 64], DT2, "L1")]
                Nm = [UT([64, 2, 8, 64], DT2, "N0"), UT([64, 2, 8, 64], DT2, "N1")]
                Pm = [UT([64, 2, 8, 64], DT2, "P0"), UT([64, 2, 8, 64], F32, "P1")]
                Pb = UT([64, 2, 8, 64], DT2, "Pb")
                I64b = identb[:64, :64] if DT2 == BF16 else identf[:64, :64]
                AK = UT([64, 2, 8, 64], F32, "AK")
                RBm = UT([64, 2, 8, 64], F32, "RBm")
                RKm = UT([64, 2, 8, 64], F32, "RKm")
                WT = UT([64, 2, 8, 64], F32, "WT")
                GT = UT([64, 2, 8, 64], F32, "GT")

                def mm8(c, lh, rh, rb):
                    bk = nb()
                    for h in range(8):
                        S.op("pe", lambda e, h=h, bk=bk: e.matmul(PS[bk][:64, h * 64:(h + 1) * 64], lhsT=rr(lh(h)), rhs=rr(rh(h)), start=True, stop=True),
                             r=rb, w=[PSB[bk]])
                    return bk

                def psv(bk):
                    return PS[bk][:64, :].rearrange("p (h t) -> p h t", h=8)

                def mk(m):
                    return CM.ap[:, m, :].unsqueeze(1).to_broadcast([64, 8, 64])
                for c in range(2):
                    for (A_, B_, m, dst) in ((ALT, BTT, 0, Lm[0]), (BTT, ALT, 1, Nm[0]), (ALT, KTT, 0, AK), (BTT, RBT, 2, RBm), (KTT, RBT, 2, RKm)):
                        bk = mm8(c, lambda h, A_=A_, c=c: A_.ap[:, c, h, :], lambda h, B_=B_, c=c: B_.ap[:, c, h, :], [A_.b, B_.b])
                        S.op("dve", lambda e, bk=bk, dst=dst, m=m, c=c: e.tensor_tensor(out=dst.ap[:, c, :, :], in0=psv(bk), in1=mk(m), op=ALU.mult),
                             r=[PSB[bk], CM.b], w=[dst.b])
                    S.op("dve", lambda e, c=c: e.tensor_tensor(out=Pm[0].ap[:, c, :, :], in0=Nm[0].ap[:, c, :, :],
                                                               in1=I64b.unsqueeze(1).to_broadcast([64, 8, 64]), op=ALU.add),
                         r=[Nm[0].b, B_ident], w=[Pm[0].b])
                for j in range(1, 6):
                    a, b_ = (j - 1) % 2, j % 2
                    for c in range(2):
                        bk = mm8(c, lambda h, c=c, a=a: Nm[a].ap[:, c, h, :], lambda h, c=c, a=a: Lm[a].ap[:, c, h, :], [Nm[a].b, Lm[a].b])
                        S.op("act", lambda e, bk=bk, c=c, b_=b_: e.copy(out=Lm[b_].ap[:, c, :, :], in_=psv(bk)), r=[PSB[bk]], w=[Lm[b_].b])
                        if j < 5:
                            bk = mm8(c, lambda h, c=c, a=a: Lm[a].ap[:, c, h, :], lambda h, c=c, a=a: Nm[a].ap[:, c, h, :], [Nm[a].b, Lm[a].b])
                            S.op("act", lambda e, bk=bk, c=c, b_=b_: e.copy(out=Nm[b_].ap[:, c, :, :], in_=psv(bk)), r=[PSB[bk]], w=[Nm[b_].b])
                        pin = Pm[0] if a == 0 else Pb
                        pout = Pm[1] if j == 5 else (Pb if a == 0 else Pm[0])
                        bk = mm8(c, lambda h, c=c, b_=b_: Lm[b_].ap[:, c, h, :], lambda h, c=c, pin=pin: pin.ap[:, c, h, :], [Lm[b_].b, pin.b])
                        S.op("dve", lambda e, bk=bk, c=c, pin=pin, pout=pout: e.tensor_tensor(out=pout.ap[:, c, :, :], in0=psv(bk), in1=pin.ap[:, c, :, :], op=ALU.add),
                             r=[PSB[bk], pin.b], w=[pout.b])
                PF = Pm[1]
                for c in range(2):
                    bk = mm8(c, lambda h, c=c: AL.ap[:, c, h * 64:(h + 1) * 64], lambda h, c=c: PF.ap[:, c, h, :], [AL.b, PF.b])
                    S.op("act", lambda e, bk=bk, c=c: e.copy(out=WT.ap[:, c, :, :], in_=psv(bk)), r=[PSB[bk]], w=[WT.b])
                    bk = mm8(c, lambda h, c=c: AK.ap[:, c, h, :], lambda h, c=c: PF.ap[:, c, h, :], [AK.b, PF.b])
                    S.op("dve", lambda e, bk=bk, c=c: e.tensor_copy(out=GT.ap[:, c, :, :], in_=psv(bk)), r=[PSB[bk]], w=[GT.b])
                if RWL < 3:
                    S.barrier()
                    return cur, Pnext
                U = UT([64, 512], F32, "U")
                Y = UT([64, 2, 512], F32, "Y")
                for c in range(2):
                    Hc, Hn = H[cur], H[1 - cur]

                    def V(h, c=c):
                        return P.ap[:, c, 1024 + h * 64:1024 + (h + 1) * 64]
                    bu = nb()
                    for h in range(8):
                        S.op("pe", lambda e, h=h, c=c, bu=bu, Hc=Hc: e.matmul(PS[bu][:64, h * 64:(h + 1) * 64], lhsT=rr(WT.ap[:, c, h, :]), rhs=rr(Hc.ap[:, h, :]), start=True, stop=False),
                             r=[WT.b, Hc.b], w=[PSB[bu]])
                        S.op("pe", lambda e, h=h, c=c, bu=bu, V=V: e.matmul(PS[bu][:64, h * 64:(h + 1) * 64], lhsT=rr(GT.ap[:, c, h, :]), rhs=rr(V(h)), start=False, stop=True),
                             r=[GT.b, P.b], w=[PSB[bu]])
                    S.op("act", lambda e, bu=bu: e.copy(out=U.ap, in_=PS[bu][:64, :]), r=[PSB[bu]], w=[U.b])
                    bh = nb()
                    for h in range(8):
                        S.op("pe", lambda e, h=h, c=c, bh=bh: e.matmul(PS[bh][:64, h * 64:(h + 1) * 64], lhsT=rr(BT.ap[:, c, h * 64:(h + 1) * 64]), rhs=rr(U.ap[:, h * 64:(h + 1) * 64]), start=True, stop=False),
                             r=[BT.b, U.b], w=[PSB[bh]])
                        S.op("pe", lambda e, h=h, c=c, bh=bh, V=V: e.matmul(PS[bh][:64, h * 64:(h + 1) * 64], lhsT=rr(KT.ap[:, c, h * 64:(h + 1) * 64]), rhs=rr(V(h)), start=False, stop=True),
                             r=[KT.b, P.b], w=[PSB[bh]])
                    by = nb()
                    for h in range(8):
                        S.op("pe", lambda e, h=h, c=c, by=by, Hc=Hc: e.matmul(PS[by][:64, h * 64:(h + 1) * 64], lhsT=rr(RBT.ap[:, c, h, :]), rhs=rr(Hc.ap[:, h, :]), start=True, stop=False),
                             r=[RBT.b, Hc.b], w=[PSB[by]])
                        S.op("pe", lambda e, h=h, c=c, by=by: e.matmul(PS[by][:64, h * 64:(h + 1) * 64], lhsT=rr(RBm.ap[:, c, h, :]), rhs=rr(U.ap[:, h * 64:(h + 1) * 64]), start=False, stop=False),
                             r=[RBm.b, U.b], w=[PSB[by]])
                        S.op("pe", lambda e, h=h, c=c, by=by, V=V: e.matmul(PS[by][:64, h * 64:(h + 1) * 64], lhsT=rr(RKm.ap[:, c, h, :]), rhs=rr(V(h)), start=False, stop=True),
                             r=[RKm.b, P.b], w=[PSB[by]])
                    S.op("act", lambda e, by=by, c=c: e.copy(out=Y.ap[:, c, :], in_=PS[by][:64, :]), r=[PSB[by]], w=[Y.b])
                    S.op("dve", lambda e, bh=bh, Hc=Hc, Hn=Hn: e.tensor_tensor(out=Hn.ap, in0=Hc.ap, in1=PS[bh][:64, :].rearrange("p (h v) -> p h v", h=8), op=ALU.add),
                         r=[Hc.b, PSB[bh]], w=[Hn.b])
                    S.op("dve", lambda e, c=c, Hn=Hn: e.tensor_tensor(out=Hn.ap, in0=Hn.ap, in1=GC.ap[:, c * 8:(c + 1) * 8].unsqueeze(2).to_broadcast([64, 8, 64]), op=ALU.mult),
                         r=[GC.b], w=[Hn.b])
                    cur = 1 - cur
                if RWL >= 4:
                    d4(T, Y.ap, Y.b, u * 128, 2, UT)
                return cur, Pnext

            for u in range(16):
                cur, Pn = do_unit(u, Pn, cur)
            Hf = H[cur]
            HT = Tl([64, 8, 64], F32, "HT")
            bk = nb()
            for h in range(8):
                S.op("pe", lambda e, h=h, bk=bk: e.transpose(PS[bk][:64, h * 64:(h + 1) * 64], Hf.ap[:, h, :], I64), r=[Hf.b, B_ident], w=[PSB[bk]])
            S.op("act", lambda e, bk=bk: e.copy(out=HT.ap, in_=PS[bk][:64, :].rearrange("p (h k) -> p h k", h=8)), r=[PSB[bk]], w=[HT.b])
            S.op("sp", lambda e: e.dma_start(out=wkvp_out.rearrange("h v k -> v h k"), in_=HT.ap), r=[HT.b], dma="HT")
            S.barrier()

        sample_part()
        if os.environ.get("KDEV_NOPROMPT") is None:
            prompt_part()


    LAM_INIT = 0.8 - 0.6 * math.exp(-0.3 * 0)
    SUBLN_EPS = 1e-5
    LF = 383

    def attn_setup():
        A = {}
        LQ = Tl([128, 4, 64], F32, "LQ")
        S.op("sp", lambda e: e.dma_start(out=LQ.ap.rearrange("p a n -> p (a n)"), in_=lamvec.partition_broadcast(128)), w=[LQ.b], dma="LQ")
        LS = Tl([128, 4], F32, "LS")
        LAM = Tl([128, 2], F32, "LAM")
        PRD = Tl([128, 2, 64], F32, "PRD")
        S.op("dve", lambda e: e.tensor_tensor(out=PRD.ap, in0=LQ.ap[:, 0:2, :], in1=LQ.ap[:, 2:4, :], op=ALU.mult), r=[LQ.b], w=[PRD.b])
        S.op("dve", lambda e: e.tensor_reduce(out=LS.ap[:, 0:2], in_=PRD.ap, axis=AX.X, op=ALU.add), r=[PRD.b], w=[LS.b])
        S.op("act", lambda e: e.activation(out=LS.ap[:, 2:4], in_=LS.ap[:, 0:2], func=AF.Exp), r=[LS.b], w=[LS.b])
        S.op("dve", lambda e: e.tensor_tensor(out=LAM.ap[:, 0:1], in0=LS.ap[:, 2:3], in1=LS.ap[:, 3:4], op=ALU.subtract), r=[LS.b], w=[LAM.b])
        S.op("dve", lambda e: e.tensor_scalar(out=LAM.ap[:, 0:1], in0=LAM.ap[:, 0:1], scalar1=LAM_INIT, scalar2=None, op0=ALU.add), r=[LAM.b], w=[LAM.b])
        S.op("dve", lambda e: e.tensor_scalar(out=LAM.ap[:, 1:2], in0=LAM.ap[:, 0:1], scalar1=-1.0, scalar2=None, op0=ALU.mult), r=[LAM.b], w=[LAM.b])
        AS = int(os.environ.get("KDEV_AS", 9))
        if AS < 1:
            return A
        SUBW = Tl([128, 128], F32, "SUBW")
        S.op("sp", lambda e: e.dma_start(out=SUBW.ap, in_=subln.partition_broadcast(128)), w=[SUBW.b], dma="SUBW")
        S.op("dve", lambda e: e.tensor_scalar(out=SUBW.ap, in0=SUBW.ap, scalar1=1.0 - LAM_INIT, scalar2=None, op0=ALU.mult), r=[SUBW.b], w=[SUBW.b])
        if AS < 2:
            return A
        TE = Tl([33, 4], F32, "TE")
        T31 = Tl([33, 4], F32, "T31")
        OH = Tl([33, LF], F32, "OH")
        S.op("sp", lambda e: e.dma_start(out=TE.ap[0:32, :], in_=relb), w=[TE.b], dma="TE")
        S.op("sp", lambda e: e.dma_start(out=T31.ap[0:32, :], in_=relb[31:32, :].partition_broadcast(32)), w=[T31.b], dma="T31")
        S.op("sp", lambda e: e.dma_start(out=OH.ap, in_=onehot), w=[OH.b], dma="OH")
        S.op("dve", lambda e: e.tensor_tensor(out=TE.ap[0:32, :], in0=TE.ap[0:32, :], in1=T31.ap[0:32, :], op=ALU.subtract), r=[T31.b], w=[TE.b])
        S.op("dve", lambda e: e.memset(TE.ap[32:33, :], -30000.0), w=[TE.b])
        if AS < 3:
            return A
        RR = Tl([128, LF], F32, "RR")
        TEB = Tl([33, 4, 128], F32, "TEB")
        S.op("dve", lambda e: e.tensor_copy(out=TEB.ap, in_=TE.ap.unsqueeze(2).to_broadcast([33, 4, 128])), r=[TE.b], w=[TEB.b])
        for h in range(4):
            S.op("pe", lambda e, h=h: e.matmul(PS[1][:, 0:LF], lhsT=TEB.ap[:, h, :], rhs=OH.ap, start=True, stop=True),
                 r=[TEB.b, OH.b], w=[PSB[1]])
            S.op("act", lambda e: e.copy(out=RR.ap, in_=PS[1][:, 0:LF]), r=[PSB[1]], w=[RR.b])
            S.op("sp", lambda e, h=h: e.dma_start(out=r_scr[h], in_=RR.ap), r=[RR.b], dma="RR")
        S.barrier()
        if AS < 4:
            return A
        BF = Tl([128, 4, 2, 128], F32, "BF")
        BB = Tl([128, 4, 2, 128], BF16, "BB")
        for h in range(4):
            for dl in range(2):
                src = bass.AP(tensor=r_scr.tensor, offset=h * 128 * LF + 127 + 128 * dl, ap=[[LF - 1, 128], [1, 128]])
                S.op("sp", lambda e, h=h, dl=dl, src=src: e.dma_start(out=BF.ap[:, h, dl, :], in_=src), w=[BF.b], dma="BF")
        S.op("dve", lambda e: e.tensor_copy(out=BB.ap, in_=BF.ap), r=[BF.b], w=[BB.b])
        if os.environ.get("KDEV_DBG"):
            bfd = dout("bf_dbg", [128, 4 * 2 * 128])
            S.op("sp", lambda e: e.dma_start(out=bfd, in_=BF.ap.rearrange("p h d q -> p (h d q)")), r=[BF.b], dma="bfd")
        A.update(LAM=LAM, SUBW=SUBW, BB=BB)
        A["mark"] = AR.off
        return A

    def subln_tile(OA, rows, G, junk, st, SUBW):
        S.op("dve", lambda e: e.tensor_tensor(out=junk.ap, in0=OA.ap, in1=OA.ap, op=ALU.mult), r=[OA.b], w=[junk.b])
        S.op("dve", lambda e: e.tensor_reduce(out=st.ap, in_=junk.ap, axis=AX.X, op=ALU.add), r=[junk.b], w=[st.b])
        S.op("dve", lambda e: e.tensor_scalar(out=st.ap, in0=st.ap, scalar1=1.0 / 128, scalar2=SUBLN_EPS, op0=ALU.mult, op1=ALU.add), r=[st.b], w=[st.b])
        S.op("act", lambda e: e.sqrt(out=st.ap, in_=st.ap), r=[st.b], w=[st.b])
        S.op("dve", lambda e: e.reciprocal(out=st.ap, in_=st.ap), r=[st.b], w=[st.b])
        S.op("dve", lambda e: e.tensor_tensor(out=OA.ap, in0=OA.ap, in1=st.ap.unsqueeze(2).to_broadcast([rows, G, 128]), op=ALU.mult), r=[st.b], w=[OA.b])
        S.op("dve", lambda e: e.tensor_tensor(out=OA.ap, in0=OA.ap, in1=SUBW.ap[0:rows, :].unsqueeze(1).to_broadcast([rows, G, 128]), op=ALU.mult),
             r=[SUBW.b], w=[OA.b])

    def attn_prompt(A, qT, kT, Vb):
        LAM, SUBW, BB = A["LAM"], A["SUBW"], A["BB"]
        PT = [Tl([128, 2, 256], BF16, "PT%d" % i) for i in range(2)]
        OA = [Tl([128, 4, 128], F32, "OA%d" % i) for i in range(2)]
        RS2 = Tl([128, 2], F32, "RS2")
        junk = Tl([128, 4, 128], F32, "jk")
        st = Tl([128, 4], F32, "st4")
        sb = [(0, 1), (2, 3)]
        it = [0]
        APL = int(os.environ.get("KDEV_AP", 9))
        for G in range(int(os.environ.get("KDEV_APG0", 0)), int(os.environ.get("KDEV_APG", 8))):
            qt0 = 2 * G
            for h in range(4):
                ob = [4 + 2 * (h % 2), 5 + 2 * (h % 2)]
                for j in range(2):
                    S.op("dve", lambda e, j=j, ob=ob: e.memset(PS[ob[j]][:, :], 0.0), w=[PSB[ob[j]]])
                for kb in range(qt0 + 2):
                    i_ = it[0] % 2
                    it[0] += 1
                    far = kb < qt0 - 1
                    j_lo = 0 if kb <= qt0 else 1
                    for m in range(2):
                        bk = sb[i_][m]
                        kop = kT[m * 64:(m + 1) * 64, h, kb * 128:(kb + 1) * 128]
                        if far:
                            for j in range(2):
                                S.op("pe", lambda e, m=m, bk=bk, kop=kop, h=h, qt0=qt0, j=j: e.matmul(
                                    PS[bk][:, j * 128:(j + 1) * 128], lhsT=kop,
                                    rhs=qT[m * 64:(m + 1) * 64, h, (qt0 + j) * 128:(qt0 + j + 1) * 128],
                                    start=True, stop=True), r=[B_qk], w=[PSB[bk]])
                        else:
                            for j in range(j_lo, 2):
                                qt = qt0 + j
                                dl = qt - kb
                                S.op("pe", lambda e, m=m, bk=bk, kop=kop, h=h, qt=qt, j=j, dl=dl: e.matmul(
                                    PS[bk][:, j * 128:(j + 1) * 128], lhsT=kop,
                                    rhs=qT[m * 64:(m + 1) * 64, h, qt * 128:(qt + 1) * 128], start=True, stop=(dl >= 2)),
                                    r=[B_qk], w=[PSB[bk]])
                                if dl < 2:
                                    S.op("pe", lambda e, m=m, bk=bk, h=h, j=j, dl=dl: e.matmul(
                                        PS[bk][:, j * 128:(j + 1) * 128], lhsT=identb[:, :],
                                        rhs=BB.ap[:, h, dl, :], start=False, stop=True),
                                        r=[BB.b, B_ident], w=[PSB[bk]])
                    pt = PT[i_]
                    for m in range(2):
                        bk = sb[i_][m]
                        S.op("act", lambda e, bk=bk, pt=pt, j_lo=j_lo, m=m: e.activation(
                            out=pt.ap[:, m, j_lo * 128:256], in_=PS[bk][:, j_lo * 128:256], func=AF.Exp),
                            r=[PSB[bk]], w=[pt.b])
                    for j in range(j_lo, 2 if APL >= 2 else 0):
                        qt = qt0 + j
                        for m in range(2):
                            S.op("pe", lambda e, j=j, m=m, pt=pt, kb=kb, h=h, qt=qt, ob=ob: e.matmul(
                                PS[ob[j]][:, m * 256:m * 256 + 129], lhsT=pt.ap[:, m, j * 128:(j + 1) * 128],
                                rhs=Vb[:, kb, h, 0:129], start=False, stop=(kb == qt), skip_group_check=True),
                                r=[pt.b, B_vb], w=[PSB[ob[j]]])
                for j in range(2 if APL >= 3 else 0):
                    oa = OA[j]
                    ov = PS[ob[j]][:, :].rearrange("p (m x) -> p m x", m=2)
                    S.op("dve", lambda e, ov=ov: e.reciprocal(out=RS2.ap, in_=ov[:, :, 128]), r=[PSB[ob[j]]], w=[RS2.b])
                    S.op("dve", lambda e: e.tensor_tensor(out=RS2.ap[:, 1:2], in0=RS2.ap[:, 1:2], in1=LAM.ap[:, 1:2], op=ALU.mult), r=[LAM.b], w=[RS2.b])
                    S.op("dve", lambda e, ov=ov, oa=oa, h=h: e.tensor_scalar(out=oa.ap[:, h, :], in0=ov[:, 0, 0:128], scalar1=RS2.ap[:, 0:1], scalar2=None, op0=ALU.mult),
                         r=[PSB[ob[j]], RS2.b], w=[oa.b])
                    S.op("dve", lambda e, ov=ov, oa=oa, h=h: e.scalar_tensor_tensor(out=oa.ap[:, h, :], in0=ov[:, 1, 0:128], scalar=RS2.ap[:, 1:2], in1=oa.ap[:, h, :],
                                                                                  op0=ALU.mult, op1=ALU.add), r=[PSB[ob[j]], RS2.b], w=[oa.b])
            for j in range(2 if APL >= 4 else 0):
                subln_tile(OA[j], 128, 4, junk, st, SUBW)
                S.op("sp", lambda e, j=j, qt0=qt0: e.dma_start(out=oat_scr[(qt0 + j) * 128:(qt0 + j + 1) * 128, :], in_=OA[j].ap.rearrange("p h d -> p (h d)")),
                     r=[OA[j].b], dma="OA%d" % j)
        S.barrier()

    def attn_sample(A, qT, kT):
        LAM, SUBW, BB = A["LAM"], A["SUBW"], A["BB"]
        AR.off = A["mark"]
        PTB = Tl([128, 256], I32, "PTB")
        IOT = Tl([128, 1], I32, "IOT")
        IDX = Tl([128, 256], I32, "IDX")
        S.op("sp", lambda e: e.dma_start(out=PTB.ap, in_=ptab.partition_broadcast(128)), w=[PTB.b], dma="PTB")
        S.op("sp", lambda e: e.dma_start(out=IOT.ap, in_=iotap), w=[IOT.b], dma="IOT")
        S.op("pool", lambda e: e.tensor_scalar(out=IDX.ap, in0=PTB.ap, scalar1=128, scalar2=IOT.ap[:, 0:1], op0=ALU.mult, op1=ALU.add),
             r=[PTB.b, IOT.b], w=[IDX.b])
        QB = Tl([128, 16, 4, 8], BF16, "QB")
        S.op("dve", lambda e: e.memset(QB.ap, 0.0), w=[QB.b])
        for m in range(2):
            S.op("dve", lambda e, m=m: e.tensor_copy(out=QB.ap[m * 64:(m + 1) * 64, :, :, m * 4:(m + 1) * 4],
                                                     in_=qT[m * 64:(m + 1) * 64, :, TP:TT].rearrange("p h (b t) -> p b h t", t=4)),
                 r=[B_qk], w=[QB.b])
        VN = Tl([4, 16, 512], F32, "VN")
        S.op("sp", lambda e: e.dma_start(out=VN.ap, in_=v_out[TP:TT, :].rearrange("(b t) n -> t b n", t=4)), w=[VN.b], dma="VN")
        ONE = Tl([128, 1], F32, "ONE")
        S.op("dve", lambda e: e.memset(ONE.ap, 1.0), w=[ONE.b])
        BS = Tl([128, 4, 2, 4], BF16, "BS")
        BN = Tl([4, 4, 2, 4], BF16, "BN")
        for m in range(2):
            S.op("dve", lambda e, m=m: e.tensor_copy(out=BS.ap[:, :, m, :], in_=BB.ap[:, :, 1, 0:4]), r=[BB.b], w=[BS.b])
            S.op("dve", lambda e, m=m: e.tensor_copy(out=BN.ap[:, :, m, :], in_=BB.ap[0:4, :, 0, 0:4]), r=[BB.b], w=[BN.b])
        OS = Tl([8, 16, 4, 128], F32, "OS")
        SMs = Tl([8, 16, 4], F32, "SMs")
        SEL = Tl([8, 3, 4], F32, "SEL")
        mark2 = AR.off
        NS = 4
        KP = [Tl([128, 512], F32, "KP%d" % i) for i in range(NS)]
        VP = [Tl([128, 512], F32, "VP%d" % i) for i in range(16)]
        B_vpg = [Buf("vpg%d" % i) for i in range(4)]
        KTb = [Tl([128, 4, 128], BF16, "KTb%d" % i) for i in range(2)]
        PTs = Tl([128, 512], F32, "PTs")
        PTN = Tl([4, 32], F32, "PTN")
        PSJ = Tl([128, 32], F32, "PSJ")
        cnt = [0]
        for b in range(16):
            sbk, obk, mbk = 1, 2, 3
            for j in range(16):
                sl = cnt[0] % NS
                cnt[0] += 1
                col = b * 16 + j
                S.op("pool", lambda e, sl=sl, col=col: e.indirect_dma_start(
                    out=KP[sl].ap, out_offset=None, in_=ck2, in_offset=bass.IndirectOffsetOnAxis(ap=IDX.ap[:, col:col + 1], axis=0)),
                    r=[IDX.b], w=[KP[sl].b], dma="KP%d" % sl)
                S.op("pool", lambda e, j=j, col=col: e.indirect_dma_start(
                    out=VP[j].ap, out_offset=None, in_=cv2, in_offset=bass.IndirectOffsetOnAxis(ap=IDX.ap[:, col:col + 1], axis=0)),
                    r=[IDX.b], w=[B_vpg[j % 4]], dma="VP%d" % (j % 4))
                tb = 4 + (j % 2)
                for h in range(4):
                    S.op("pe", lambda e, h=h, tb=tb, sl=sl: e.transpose(PS[tb][:, h * 128:(h + 1) * 128], KP[sl].ap[:, h * 128:(h + 1) * 128], identf[:, :]),
                         r=[KP[sl].b, B_ident], w=[PSB[tb]])
                kt = KTb[j % 2]
                if j % 2 == 0:
                    S.op("act", lambda e, tb=tb, kt=kt: e.copy(out=kt.ap, in_=PS[tb][:, :].rearrange("p (h k) -> p h k", h=4)), r=[PSB[tb]], w=[kt.b])
                else:
                    S.op("dve", lambda e, tb=tb, kt=kt: e.tensor_copy(out=kt.ap, in_=PS[tb][:, :].rearrange("p (h k) -> p h k", h=4)), r=[PSB[tb]], w=[kt.b])
                if j == 15:
                    S.op("pe", lambda e: e.matmul(PS[sbk][:, 480:512], lhsT=identb[:, :], rhs=BS.ap.rearrange("p h m t -> p (h m t)"), start=True, stop=False),
                         r=[BS.b, B_ident], w=[PSB[sbk]])
                for h in range(4):
                    S.op("pe", lambda e, h=h, j=j, kt=kt, b=b: e.matmul(PS[sbk][:, j * 32 + h * 8:j * 32 + (h + 1) * 8], lhsT=kt.ap[:, h, :], rhs=QB.ap[:, b, h, :],
                                                                        start=(j != 15), stop=True, skip_group_check=True), r=[kt.b, QB.b], w=[PSB[sbk]])
            S.op("pe", lambda e: e.matmul(PS[mbk][0:4, 0:32], lhsT=identb[0:4, 0:4], rhs=BN.ap.rearrange("p h m t -> p (h m t)"), start=True, stop=False),
                 r=[BN.b, B_ident], w=[PSB[mbk]])
            for h in range(4):
                S.op("pe", lambda e, h=h, b=b: e.matmul(PS[mbk][0:4, h * 8:(h + 1) * 8], lhsT=kT[:, h, TP + 4 * b:TP + 4 * b + 4], rhs=QB.ap[:, b, h, :],
                                                        start=False, stop=True, skip_group_check=True), r=[B_qk, QB.b], w=[PSB[mbk]])
            S.op("act", lambda e: e.activation(out=PTs.ap, in_=PS[sbk][:, :], func=AF.Exp), r=[PSB[sbk]], w=[PTs.b])
            S.op("act", lambda e: e.activation(out=PTN.ap, in_=PS[mbk][0:4, 0:32], func=AF.Exp), r=[PSB[mbk]], w=[PTN.b])
            S.op("dve", lambda e: e.tensor_reduce(out=PSJ.ap, in_=PTs.ap.rearrange("p (j x) -> p x j", j=16), axis=AX.X, op=ALU.add), r=[PTs.b], w=[PSJ.b])
            for h in range(4):
                S.op("pe", lambda e, h=h: e.matmul(PS[mbk][0:8, 64 + h:65 + h], lhsT=PSJ.ap[:, h * 8:(h + 1) * 8], rhs=ONE.ap[:, 0:1], start=True, stop=False),
                     r=[PSJ.b, ONE.b], w=[PSB[mbk]])
                S.op("pe", lambda e, h=h: e.matmul(PS[mbk][0:8, 64 + h:65 + h], lhsT=PTN.ap[:, h * 8:(h + 1) * 8], rhs=ONE.ap[0:4, 0:1], start=False, stop=True),
                     r=[PTN.b, ONE.b], w=[PSB[mbk]])
            S.op("act", lambda e, b=b: e.copy(out=SMs.ap[:, b, :], in_=PS[mbk][0:8, 64:68]), r=[PSB[mbk]], w=[SMs.b])
            S.op("dve", lambda e: e.memset(PS[obk][0:8, :], 0.0), w=[PSB[obk]])
            for j in range(16):
                sl = j
                for h in range(4):
                    S.op("pe", lambda e, h=h, j=j, sl=sl: e.matmul(PS[obk][0:8, h * 128:(h + 1) * 128], lhsT=PTs.ap[:, j * 32 + h * 8:j * 32 + (h + 1) * 8],
                                                                  rhs=VP[sl].ap[:, h * 128:(h + 1) * 128], start=False, stop=False, skip_group_check=True),
                         r=[PTs.b, B_vpg[sl % 4]], w=[PSB[obk]])
            for h in range(4):
                S.op("pe", lambda e, h=h, b=b: e.matmul(PS[obk][0:8, h * 128:(h + 1) * 128], lhsT=PTN.ap[:, h * 8:(h + 1) * 8],
                                                        rhs=VN.ap[:, b, h * 128:(h + 1) * 128], start=False, stop=True, skip_group_check=True),
                     r=[PTN.b, VN.b], w=[PSB[obk]])
            S.op("act", lambda e, b=b: e.copy(out=OS.ap[:, b, :, :], in_=PS[obk][0:8, :].rearrange("p (h d) -> p h d", h=4)), r=[PSB[obk]], w=[OS.b])
        S.op("dve", lambda e: e.reciprocal(out=SMs.ap, in_=SMs.ap), r=[SMs.b], w=[SMs.b])
        S.op("dve", lambda e: e.tensor_tensor(out=OS.ap, in0=OS.ap, in1=SMs.ap.unsqueeze(3).to_broadcast([8, 16, 4, 128]), op=ALU.mult), r=[SMs.b], w=[OS.b])
        S.op("sp", lambda e: e.dma_start(out=SEL.ap[:, 0:2, :], in_=selc), w=[SEL.b], dma="SEL")
        S.op("dve", lambda e: e.scalar_tensor_tensor(out=SEL.ap[:, 2, :], in0=SEL.ap[:, 1, :], scalar=LAM.ap[0:8, 0:1], in1=SEL.ap[:, 0, :], op0=ALU.mult, op1=ALU.add),
             r=[LAM.b], w=[SEL.b])
        S.barrier()
        AR.off = mark2
        OC = Tl([4, 64, 128], F32, "OC")
        osf = OS.ap.rearrange("p b h d -> p (b h d)")
        ocf = OC.ap.rearrange("p g d -> p (g d)")
        for ch in range(16):
            S.op("pe", lambda e, ch=ch: e.matmul(PS[5][0:4, :], lhsT=SEL.ap[:, 2, :], rhs=osf[:, ch * 512:(ch + 1) * 512], start=True, stop=True),
                 r=[SEL.b, OS.b], w=[PSB[5]])
            S.op("act", lambda e, ch=ch: e.copy(out=ocf[:, ch * 512:(ch + 1) * 512], in_=PS[5][0:4, :]), r=[PSB[5]], w=[OC.b])
        jk = Tl([4, 64, 128], F32, "jk2")
        st = Tl([4, 64], F32, "st64")
        subln_tile(OC, 4, 64, jk, st, SUBW)
        S.op("sp", lambda e: e.dma_start(out=oat_scr[TP:TT, :].rearrange("(b t) n -> t b n", t=4), in_=OC.ap.rearrange("p (b h) d -> p b (h d)", h=4)),
             r=[OC.b], dma="OC")
        S.barrier()

    def merge_phase():
        AR.off = 0
        Wo = AR.alloc([128, 8, D], BF16)
        B_wo = Buf("wo")
        S.op("pool", lambda e: e.dma_start(out=Wo, in_=w_out.rearrange("(k p) n -> p k n", p=128)), w=[B_wo], dma="wg0")
        GA = AR.alloc([128, 2, D], F32)
        load_gain(GA, 0, 3)
        CAT = [Tl([128, D], F32, "CAT%d" % i) for i in range(2)]
        HR = [Tl([128, D], F32, "HR%d" % i) for i in range(2)]
        CB = Tl([128, D], BF16, "CB")
        CT = Tl([128, 8, 128], BF16, "CT")
        MM = Tl([128, D], F32, "MM")
        junk = Tl([128, D], BF16, "junk")
        st = Tl([128, 4], F32, "st")
        for i, (t0, rows) in enumerate(tiles_all):
            sl = i % 2
            cat, hr = CAT[sl], HR[sl]
            S.op("sp", lambda e, cat=cat, t0=t0, rows=rows: e.dma_start(out=cat.ap[:rows, 0:512], in_=orw_scr[t0:t0 + rows, :]), w=[cat.b], dma="CAT%d" % sl)
            S.op("sp", lambda e, cat=cat, t0=t0, rows=rows: e.dma_start(out=cat.ap[:rows, 512:1024], in_=oat_scr[t0:t0 + rows, :]), w=[cat.b], dma="CAT%d" % sl)
            S.op("sp", lambda e, hr=hr, t0=t0, rows=rows: e.dma_start(out=hr.ap[:rows, :], in_=h_scr[t0:t0 + rows, :]), w=[hr.b], dma="HR%d" % sl)
            S.op("dve", lambda e, cat=cat, rows=rows: e.tensor_copy(out=CB.ap[:rows, :], in_=cat.ap[:rows, :]), r=[cat.b], w=[CB.b])
            for kc in range(8):
                S.op("pe", lambda e, kc=kc, rows=rows: e.transpose(psb16(0)[:, kc * 128:kc * 128 + rows], CB.ap[:rows, kc * 128:(kc + 1) * 128], identb[:rows, :rows]),
                     r=[CB.b, B_ident], w=[PSB[0]])
            S.op("act", lambda e, rows=rows: e.copy(out=CT.ap[:, :, 0:rows], in_=psb16(0).rearrange("p (k t) -> p k t", k=8)[:, :, 0:rows]), r=[PSB[0]], w=[CT.b])
            for nh in range(2):
                pc = 1 + nh
                for kc in range(8):
                    S.op("pe", lambda e, kc=kc, nh=nh, pc=pc, rows=rows: e.matmul(PS[pc][:rows, :], lhsT=CT.ap[:, kc, 0:rows], rhs=Wo[:, kc, nh * 512:(nh + 1) * 512],
                                                                                start=(kc == 0), stop=(kc == 7)), r=[CT.b, B_wo], w=[PSB[pc]])
                S.op("act", lambda e, nh=nh, pc=pc, rows=rows: e.copy(out=MM.ap[:rows, nh * 512:(nh + 1) * 512], in_=PS[pc][:rows, :]), r=[PSB[pc]], w=[MM.b])
            rms_rstd(MM.ap[:rows, :], rows, junk.ap[:rows, :], st.ap[:rows, 0:1], st.ap[:rows, 1:2], [MM.b], junk.b, st.b)
            S.op("dve", lambda e, rows=rows: e.scalar_tensor_tensor(out=MM.ap[:rows, :], in0=MM.ap[:rows, :], scalar=st.ap[:rows, 1:2], in1=GA[:rows, 0, :],
                                                                     op0=ALU.mult, op1=ALU.mult), r=[st.b, B_GA[0]], w=[MM.b])
            S.op("dve", lambda e, rows=rows, hr=hr: e.tensor_tensor(out=hr.ap[:rows, :], in0=hr.ap[:rows, :], in1=MM.ap[:rows, :], op=ALU.add), r=[MM.b], w=[hr.b])
            S.op("sp", lambda e, rows=rows, hr=hr, t0=t0: e.dma_start(out=h2_scr[t0:t0 + rows, :], in_=hr.ap[:rows, :]), r=[hr.b], dma="HR%d" % sl)
        S.barrier()

    B_qk = Buf("qk")
    B_vb = Buf("vb")

    ffn_phase("f1", x_all, h_scr, ff1_in, ff1_out, 0, 1)
    AR.off = 0
    qT = AR.alloc([128, 4, TT], BF16)
    kT = AR.alloc([128, 4, TT], BF16)
    Vb = AR.alloc([128, 16, 4, 132], BF16)
    attn_mark = AR.off
    if os.environ.get("KDEV_PROJ", "1") == "1":
        proj_phase(qT, kT, Vb)

    SKIP = os.environ.get("KDEV_SKIP", "")
    if STAGE >= 3:
        AR.off = attn_mark
        A = attn_setup()
        if "P" not in SKIP:
            attn_prompt(A, qT, kT, Vb)
        if os.environ.get("KDEV_NOSAMP") is None:
            attn_sample(A, qT, kT)
    if STAGE >= 2 and "R" not in SKIP:
        rwkv_phase()
    if STAGE >= 3:
        if "M" not in SKIP:
            merge_phase()
        if "F" not in SKIP:
            ffn_phase("f2", h2_scr, y_all, ff2_in, ff2_out, 4, 5)

    S.barrier()
    S.emit()
    return nc


_CACHE = {}


def _bucket(n):
    me = 16
    nf = np.maximum(n, 1).astype(np.float32)
    large = me + (np.log(nf / np.float32(me)) / np.float32(math.log(128 / me)) * np.float32(32 - me)).astype(np.int32)
    large = np.minimum(large, 31)
    return np.where(n < me, n, large)


def _onehot():
    n = np.arange(-127, 256)
    oh = np.zeros((33, 383), np.float32)
    bk = _bucket(np.maximum(n, 0))
    for j, nn in enumerate(n):
        if nn < 0:
            oh[32, j] = 1.0
        else:
            oh[bk[j], j] = 1.0
    return oh


def _selc():
    m = np.zeros((8, 2, 4), np.float32)
    for t in range(4):
        m[t, 0, t] = 1.0
        m[4 + t, 1, t] = -1.0
    return m


def _cmask():
    t = np.arange(64)
    m = np.zeros((64, 4, 64), np.float32)
    m[:, 0, :] = (t[:, None] > t[None, :])
    m[:, 1, :] = (t[:, None] < t[None, :])
    m[:, 2, :] = (t[:, None] <= t[None, :])
    m[63, 3, 0] = 1.0
    return m


def kernel(**inp):
    f32 = np.float32
    if "nc" not in _CACHE:
        _CACHE["nc"] = build_program()
    nc = _CACHE["nc"]
    xp = np.asarray(inp["x_prompt"], f32)
    xs = np.asarray(inp["x_sample"], f32)
    shared = {
        "gains": np.ascontiguousarray(np.asarray(inp["norm_gains"], f32)[0]),
        "ff1_in": np.ascontiguousarray(np.asarray(inp["ff1_in"], f32)[0]),
        "ff1_out": np.ascontiguousarray(np.asarray(inp["ff1_out"], f32)[0]),
        "ff2_in": np.ascontiguousarray(np.asarray(inp["ff2_in"], f32)[0]),
        "ff2_out": np.ascontiguousarray(np.asarray(inp["ff2_out"], f32)[0]),
        "w_in": np.ascontiguousarray(np.asarray(inp["w_in"], f32)[0]),
        "w_out": np.ascontiguousarray(np.asarray(inp["w_out"], f32)[0]),
        "identf": np.eye(128, dtype=f32),
        "rwvec": np.ascontiguousarray(np.concatenate([np.asarray(inp[k], f32).reshape(-1) for k in
                 ("rw_mu", "rw_w0", "rw_a0", "rw_kk", "rw_ka", "rw_rk", "rw_gn_w", "rw_gn_b")])[None, :]),
        "cmask": _cmask(),
        "lamvec": np.ascontiguousarray(np.concatenate([np.asarray(inp[k], f32).reshape(-1) for k in ("da_lq1", "da_lq2", "da_lk1", "da_lk2")])[None, :]),
        "subln": np.ascontiguousarray(np.asarray(inp["da_subln"], f32).reshape(1, 128)),
        "relb": np.ascontiguousarray(np.asarray(inp["rel_bias_table"], f32)),
        "onehot": _onehot(),
        "selc": _selc(),
        "iotap": np.arange(128, dtype=np.int32).reshape(128, 1),
    }
    if STAGE >= 3 and os.environ.get("KDEV_NOSAMP") is None:
        shared["ck2"] = np.asarray(inp["cache_k"], f32).reshape(NPOOL * 128, 512)
        shared["cv2"] = np.asarray(inp["cache_v"], f32).reshape(NPOOL * 128, 512)
    shared.update({
        "rw_w2": np.ascontiguousarray(np.asarray(inp["rw_w2"], f32)[0]),
        "rw_a2": np.ascontiguousarray(np.asarray(inp["rw_a2"], f32)[0]),
        "rw_g2": np.ascontiguousarray(np.asarray(inp["rw_g2"], f32)[0]),
    })
    ssh = np.asarray(inp["state_shift"], f32)[0]
    swkv = np.asarray(inp["state_wkv"], f32)[0]
    in_maps = []
    for c in range(NCORES):
        m = dict(shared)
        m["ptab"] = np.ascontiguousarray(np.asarray(inp["page_table"], np.int32)[16 * c:16 * c + 16].reshape(1, 256))
        m["shift0"] = np.ascontiguousarray(ssh[16 * c:16 * c + 16])
        m["wkv0"] = np.ascontiguousarray(swkv[16 * c:16 * c + 16].reshape(128, 4096))
        m["x_all"] = np.ascontiguousarray(np.concatenate([xp[c], xs[16 * c:16 * c + 16].reshape(TS, D)], axis=0))
        in_maps.append(m)
    ncr = int(os.environ.get("KDEV_CORES", NCORES))
    t0 = time.time()
    if os.environ.get("KDEV_TRACE"):
        res = run_bass_kernel_spmd(nc, in_maps[:ncr], core_ids=list(range(ncr)), trace=True)
        print("EXEC_NS", res.exec_time_ns)
    else:
        res = run_bass_kernel_spmd(nc, in_maps[:ncr], core_ids=list(range(ncr)))
    if os.environ.get("KDEV_CORES"):
        print("run time", time.time() - t0)
    R = list(res.results) + [res.results[0]] * (NCORES - ncr)
    if os.environ.get("KDEV_DBG"):
        _CACHE["R"] = R
    y_prompt = np.stack([R[c]["y_all"][:TP] for c in range(NCORES)], 0)
    y_sample = np.concatenate([R[c]["y_all"][TP:].reshape(16, 4, D) for c in range(NCORES)], 0)
    k_prompt = np.stack([R[c]["k_out"][:TP].reshape(TP, 4, 2, 64) for c in range(NCORES)], 0)[None]
    v_prompt = np.stack([R[c]["v_out"][:TP].reshape(TP, 4, 128) for c in range(NCORES)], 0)[None]
    k_sample = np.concatenate([R[c]["k_out"][TP:].reshape(16, 4, 4, 2, 64) for c in range(NCORES)], 0)[None]
    v_sample = np.concatenate([R[c]["v_out"][TP:].reshape(16, 4, 4, 128) for c in range(NCORES)], 0)[None]
    wkv_prompt = np.stack([R[c]["wkvp_out"] for c in range(NCORES)], 0)[None]
    wkv_sample = np.concatenate([R[c]["wkvs_out"].reshape(16, 8, 64, 64) for c in range(NCORES)], 0)[None]
    shift_prompt = np.concatenate([R[c]["shp_out"] for c in range(NCORES)], 0)[None]
    shift_sample = np.concatenate([R[c]["shs_out"] for c in range(NCORES)], 0)[None]
    outs = (y_prompt, y_sample, k_prompt, v_prompt, k_sample, v_sample, wkv_prompt, wkv_sample, shift_prompt, shift_sample)
    return tuple(np.ascontiguousarray(o, dtype=f32) for o in outs)
```

```python
import math
import os
import time
import numpy as np
import concourse.bass as bass
import concourse.mybir as mybir
from concourse.bass_utils import run_bass_kernel_spmd

F32 = mybir.dt.float32
BF16 = mybir.dt.bfloat16
I32 = mybir.dt.int32
AF = mybir.ActivationFunctionType
ALU = mybir.AluOpType
AX = mybir.AxisListType

NCORES = 8
D = 1024
TP = 2048
TS = 64
TT = TP + TS
DFF = 2816
NFC = 22
PROJ = 3328
RCOLS = 1792
NPOOL = 2560
EPS = 1e-6

STAGE = int(os.environ.get('KDEV_STAGE', 3))


class Buf:
    __slots__ = ("name", "w", "r", "x")

    def __init__(self, name, excl=False):
        self.name = name
        self.w = None
        self.r = {}
        self.x = excl


class Sched:
    def __init__(self, nc):
        self.nc = nc
        self.prog = {k: [] for k in ("pe", "act", "dve", "pool", "sp")}
        self.csem = {k: nc.alloc_semaphore("c_" + k) for k in ("pe", "act", "dve", "pool")}
        self.cnt = {k: 0 for k in self.csem}
        self.known = {k: {} for k in self.prog}
        self.dsem = {}
        self.sems = {}

    def _dsem(self, name):
        if name not in self.dsem:
            s = self.nc.alloc_semaphore("d_" + name)
            self.dsem[name] = [s, 0]
        return self.dsem[name]

    def op(self, e, fn, r=(), w=(), dma=None):
        deps = []
        for b in r:
            if b.w is not None:
                deps.append(b.w)
            if b.x:
                deps.extend(b.r.values())
        for b in w:
            if b.w is not None:
                deps.append(b.w)
            deps.extend(b.r.values())
        waits = {}
        for (key, sem, val, prod) in deps:
            if prod == "pe" and e == "pe" and dma is None:
                continue
            if self.known[e].get(key, 0) >= val:
                continue
            if key not in waits or waits[key][1] < val:
                waits[key] = (sem, val)
        for key, (sem, val) in waits.items():
            self.known[e][key] = val
            self.prog[e].append(("wait", sem, val))
        if dma is None:
            self.cnt[e] += 1
            tok = ("c_" + e, self.csem[e], self.cnt[e], e)
            self.prog[e].append(("op", fn, self.csem[e], 1))
        else:
            d = self._dsem(dma)
            d[1] += 16
            tok = ("d_" + dma, d[0], d[1], None)
            self.prog[e].append(("op", fn, d[0], 16))
        for b in r:
            old = b.r.get(tok[0])
            if old is None or old[2] < tok[2]:
                b.r[tok[0]] = tok
        for b in w:
            b.w = tok
            b.r = {}
        return tok

    def barrier(self):
        for e in self.prog:
            for k in self.csem:
                if self.cnt[k] > self.known[e].get("c_" + k, 0):
                    self.known[e]["c_" + k] = self.cnt[k]
                    self.prog[e].append(("wait", self.csem[k], self.cnt[k]))
            for name, (sem, val) in self.dsem.items():
                if val > self.known[e].get("d_" + name, 0):
                    self.known[e]["d_" + name] = val
                    self.prog[e].append(("wait", sem, val))

    def emit(self):
        nc = self.nc
        engs = {"pe": "tensor", "act": "scalar", "dve": "vector", "pool": "gpsimd", "sp": "sync"}
        with nc.Block() as block:
            for k, attr in engs.items():
                prog = self.prog[k]

                def body(eng, prog=prog):
                    for it in prog:
                        if it[0] == "wait":
                            eng.wait_ge(it[1], it[2])
                        else:
                            it[1](eng).then_inc(it[2], it[3])

                getattr(block, attr)(body)


def build_program():
    nc = bass.Bass("TRN2", target_bir_lowering=False)
    S = Sched(nc)

    def din(name, shape, dt=F32):
        return nc.dram_tensor(name, list(shape), dt, kind="ExternalInput").ap()

    def dout(name, shape, dt=F32):
        return nc.dram_tensor(name, list(shape), dt, kind="ExternalOutput").ap()

    def dscr(name, shape, dt=F32):
        return nc.dram_tensor(name, list(shape), dt, kind="Internal").ap()

    x_all = din("x_all", [TT, D])
    gains = din("gains", [6, D])
    ff1_in = din("ff1_in", [D, 2 * DFF])
    ff1_out = din("ff1_out", [DFF, D])
    ff2_in = din("ff2_in", [D, 2 * DFF])
    ff2_out = din("ff2_out", [DFF, D])
    w_in = din("w_in", [D, PROJ])
    w_out = din("w_out", [D, D])
    identf_d = din("identf", [128, 128])

    y_all = dout("y_all", [TT, D])
    k_out = dout("k_out", [TT, 512])
    v_out = dout("v_out", [TT, 512])
    shp_out = dout("shp_out", [1, RCOLS])
    shs_out = dout("shs_out", [16, RCOLS])
    wkvp_out = dout("wkvp_out", [8, 64, 64])
    wkvs_out = dout("wkvs_out", [128, 4096])

    rwvec = din("rwvec", [1, 5376])
    cmask = din("cmask", [64, 4, 64])
    rw_w2 = din("rw_w2", [64, 512])
    rw_a2 = din("rw_a2", [64, 512])
    rw_g2 = din("rw_g2", [128, 512])
    shift0 = din("shift0", [16, RCOLS])
    wkv0 = din("wkv0", [128, 4096])
    prev_scr = dscr("prev_scr", [TT, RCOLS])
    lamvec = din("lamvec", [1, 256])
    subln = din("subln", [1, 128])
    relb = din("relb", [32, 4])
    onehot = din("onehot", [33, 383])
    selc = din("selc", [8, 2, 4])
    ptab = din("ptab", [1, 256], I32)
    iotap = din("iotap", [128, 1], I32)
    HAVE_SAMP = STAGE >= 3 and os.environ.get("KDEV_NOSAMP") is None
    ck2 = din("ck2", [NPOOL * 128, 512]) if HAVE_SAMP else None
    cv2 = din("cv2", [NPOOL * 128, 512]) if HAVE_SAMP else None
    r_scr = dscr("r_scr", [4, 128, 383])
    oat_scr = (dout if os.environ.get("KDEV_DBG") else dscr)("oat_scr", [TT, 512])
    h2_scr = (dout if os.environ.get("KDEV_DBG") else dscr)("h2_scr", [TT, D])
    rs_scr = dscr("rs_scr", [64, 3072])
    ys_scr = dscr("ys_scr", [128, 256])
    orw_scr = (dout if os.environ.get("KDEV_DBG") else dscr)("orw_scr", [TT, 512])
    h_scr = dout("h_scr", [TT, D]) if os.environ.get("KDEV_DBG") else dscr("h_scr", [TT, D])
    pr_scr = dscr("pr_scr", [TT, RCOLS])

    NW = 52000
    BIG = nc.alloc_sbuf_tensor("BIG", [128, NW], F32)
    identf = nc.alloc_sbuf_tensor("identf_sb", [128, 128], F32)
    identb = nc.alloc_sbuf_tensor("identb_sb", [128, 128], BF16)

    class Arena:
        def __init__(self):
            self.off = 0

        def alloc(self, shape, dt=F32):
            n = 1
            for d_ in shape[1:]:
                n *= d_
            words = n if dt in (F32, I32) else (n + 1) // 2
            assert self.off + words <= NW, ("arena overflow", self.off, words)
            v = BIG[:, self.off:self.off + words]
            self.off += words
            if dt != F32:
                v = v.bitcast(dt)[:, 0:n]
            if len(shape) > 2:
                names = " ".join("a%d" % i for i in range(len(shape) - 1))
                kw = {"a%d" % i: shape[i + 1] for i in range(len(shape) - 1)}
                v = v.rearrange("p (%s) -> p %s" % (names, names), **kw)
            if shape[0] < 128:
                v = v[0:shape[0]]
            return v

    AR = Arena()
    PS = [nc.alloc_psum_tensor("ps%d" % i, [128, 512], F32) for i in range(8)]
    PSB = [Buf("ps%d" % i, True) for i in range(8)]
    B_ident = Buf("ident")
    B_GA = [Buf("ga0"), Buf("ga1")]
    GAh = [None]

    def psb16(i):
        return PS[i][:].bitcast(BF16)

    S.op("sp", lambda e: e.dma_start(out=identf[:], in_=identf_d), w=[B_ident], dma="ident")
    S.op("dve", lambda e: e.tensor_copy(out=identb[:], in_=identf[:]), r=[B_ident], w=[B_ident])

    def load_gain(GA, slot, idx):
        S.op("sp", lambda e: e.dma_start(out=GA[:, slot, :], in_=gains[idx:idx + 1, :].partition_broadcast(128)),
             w=[B_GA[slot]], dma="ga%d" % slot)

    tiles_all = [(t * 128, 128) for t in range(16)] + [(TP, TS)]
    groups = [tiles_all[0:4], tiles_all[4:8], tiles_all[8:12], tiles_all[12:16], tiles_all[16:17]]
    if os.environ.get("KDEV_NG"):
        groups = groups[:int(os.environ["KDEV_NG"])]

    def rms_rstd(src_ap, rows, junk_ap, ss_ap, rstd_ap, rbufs, junk_b, st_b, eps=EPS, n=D):
        S.op("act", lambda e: e.activation(out=junk_ap, in_=src_ap, func=AF.Square, accum_out=ss_ap),
             r=rbufs, w=[junk_b, st_b])
        S.op("dve", lambda e: e.tensor_scalar(out=rstd_ap, in0=ss_ap, scalar1=1.0 / n, scalar2=eps,
                                              op0=ALU.mult, op1=ALU.add), r=[st_b], w=[st_b])
        S.op("act", lambda e: e.sqrt(out=rstd_ap, in_=rstd_ap), r=[st_b], w=[st_b])
        S.op("dve", lambda e: e.reciprocal(out=rstd_ap, in_=rstd_ap), r=[st_b], w=[st_b])

    def ffn_phase(tag, src_d, dst_d, w_in_d, w_out_d, gi, go):
        AR.off = 0
        Win = AR.alloc([128, 8, 2 * DFF], BF16)
        Wout = AR.alloc([128, NFC, D], BF16)
        GA = AR.alloc([128, 2, D], F32)
        B_wg = [Buf("wg%d" % i) for i in range(11)]
        B_wu = [Buf("wu%d" % i) for i in range(11)]
        B_wo = [Buf("wo%d" % i) for i in range(11)]
        w_in_v = w_in_d.rearrange("(k p) n -> p k n", p=128)
        w_out_v = w_out_d.rearrange("(f p) n -> p f n", p=128)
        for i in range(11):
            S.op("pool", lambda e, i=i: e.dma_start(out=Win[:, :, i * 256:(i + 1) * 256],
                                                     in_=w_in_v[:, :, i * 256:(i + 1) * 256]),
                 w=[B_wg[i]], dma="wg%d" % i)
            S.op("pool", lambda e, i=i: e.dma_start(out=Win[:, :, DFF + i * 256:DFF + (i + 1) * 256],
                                                     in_=w_in_v[:, :, DFF + i * 256:DFF + (i + 1) * 256]),
                 w=[B_wu[i]], dma="wu%d" % i)
        for i in range(11):
            S.op("pool", lambda e, i=i: e.dma_start(out=Wout[:, 2 * i:2 * i + 2, :], in_=w_out_v[:, 2 * i:2 * i + 2, :]),
                 w=[B_wo[i]], dma="wo%d" % i)
        load_gain(GA, 0, gi)
        load_gain(GA, 1, go)

        xld = [AR.alloc([128, D], F32) for i in range(2)]
        B_xld = [Buf("xld0"), Buf("xld1")]
        xres = [AR.alloc([128, D], F32)] * 2
        B_xres = [Buf("xres0")] * 2
        junk = AR.alloc([128, D], BF16)
        B_junk = Buf("junk")
        xn = AR.alloc([128, D], BF16)
        B_xn = Buf("xn")
        xnT = AR.alloc([128, 8, 512], BF16)
        B_xnT = [Buf("xnT%d" % i) for i in range(4)]
        hT = AR.alloc([128, NFC, 512], BF16)
        B_hT = [Buf("hT%d" % i) for i in range(NFC)]
        sg = [AR.alloc([128, 512], F32) for i in range(2)]
        B_sg = [Buf("sg0"), Buf("sg1")]
        yb = AR.alloc([128, D], F32)
        B_y = Buf("y")
        hb = [AR.alloc([128, D], F32) for i in range(2)]
        B_hb = [Buf("hb0"), Buf("hb1")]
        st = AR.alloc([128, 8], F32)
        B_st = [Buf("st0"), Buf("st1")]

        cnt = {'x': 0, 'o': 0}

        def do_group(grp):
            ntok = sum(r for _, r in grp)
            for ti, (t0, rows) in enumerate(grp):
                sl = cnt['x'] % 2
                cnt['x'] += 1
                xs = xld[sl]
                S.op("sp", lambda e, xs=xs, t0=t0, rows=rows: e.dma_start(out=xs[:rows, :], in_=src_d[t0:t0 + rows, :]),
                     w=[B_xld[sl]], dma="xld%d" % sl)
                rms_rstd(xs[:rows, :], rows, junk[:rows, :], st[:rows, 0:1], st[:rows, 1:2], [B_xld[sl]], B_junk, B_st[0])
                S.op("dve", lambda e, xs=xs, rows=rows: e.scalar_tensor_tensor(
                    out=xn[:rows, :], in0=xs[:rows, :], scalar=st[:rows, 1:2], in1=GA[:rows, 0, :],
                    op0=ALU.mult, op1=ALU.mult), r=[B_xld[sl], B_st[0], B_GA[0]], w=[B_xn])
                for kc in range(8):
                    S.op("pe", lambda e, kc=kc, rows=rows: e.transpose(
                        psb16(0)[:, kc * 128:kc * 128 + rows], xn[:rows, kc * 128:(kc + 1) * 128], identb[:rows, :rows]),
                        r=[B_xn, B_ident], w=[PSB[0]])
                S.op("act", lambda e, ti=ti, rows=rows: e.copy(
                    out=xnT[:, :, ti * 128:ti * 128 + rows],
                    in_=psb16(0).rearrange("p (k t) -> p k t", k=8)[:, :, 0:rows]),
                    r=[PSB[0]], w=[B_xnT[ti]])
            for fc in range(NFC):
                pa = 1 + (fc % 2) * 2
                pb = pa + 1
                for kc in range(8):
                    S.op("pe", lambda e, fc=fc, kc=kc, pa=pa: e.matmul(
                        PS[pa][:, 0:ntok], lhsT=Win[:, kc, fc * 128:(fc + 1) * 128], rhs=xnT[:, kc, 0:ntok],
                        start=(kc == 0), stop=(kc == 7)),
                        r=[B_wg[fc // 2]] + B_xnT[:len(grp)], w=[PSB[pa]])
                for kc in range(8):
                    S.op("pe", lambda e, fc=fc, kc=kc, pb=pb: e.matmul(
                        PS[pb][:, 0:ntok], lhsT=Win[:, kc, DFF + fc * 128:DFF + (fc + 1) * 128], rhs=xnT[:, kc, 0:ntok],
                        start=(kc == 0), stop=(kc == 7)),
                        r=[B_wu[fc // 2]] + B_xnT[:len(grp)], w=[PSB[pb]])
                sgi = fc % 2
                S.op("act", lambda e, pa=pa, sgi=sgi: e.activation(out=sg[sgi][:, 0:ntok], in_=PS[pa][:, 0:ntok], func=AF.Silu),
                     r=[PSB[pa]], w=[B_sg[sgi]])
                S.op("dve", lambda e, fc=fc, pb=pb, sgi=sgi: e.tensor_tensor(
                    out=hT[:, fc, 0:ntok], in0=sg[sgi][:, 0:ntok], in1=PS[pb][:, 0:ntok], op=ALU.mult),
                    r=[B_sg[sgi], PSB[pb]], w=[B_hT[fc]])
            for ti, (t0, rows) in enumerate(grp):
                rs = cnt['o'] % 2
                cnt['o'] += 1
                xr_ = xres[rs]
                S.op("sp", lambda e, xr_=xr_, t0=t0, rows=rows: e.dma_start(out=xr_[:rows, :], in_=src_d[t0:t0 + rows, :]),
                     w=[B_xres[rs]], dma="xres0")
                for nh in range(2):
                    pc = 5 + nh
                    for fc in range(NFC):
                        S.op("pe", lambda e, fc=fc, nh=nh, pc=pc, ti=ti, rows=rows: e.matmul(
                            PS[pc][:rows, :], lhsT=hT[:, fc, ti * 128:ti * 128 + rows], rhs=Wout[:, fc, nh * 512:(nh + 1) * 512],
                            start=(fc == 0), stop=(fc == NFC - 1)),
                            r=[B_hT[fc], B_wo[fc // 2]], w=[PSB[pc]])
                    S.op("act", lambda e, nh=nh, pc=pc, rows=rows: e.copy(out=yb[:rows, nh * 512:(nh + 1) * 512], in_=PS[pc][:rows, :]),
                         r=[PSB[pc]], w=[B_y])
                rms_rstd(yb[:rows, :], rows, junk[:rows, :], st[:rows, 2:3], st[:rows, 3:4], [B_y], B_junk, B_st[1])
                ho = hb[rs]
                S.op("dve", lambda e, rows=rows: e.scalar_tensor_tensor(
                    out=yb[:rows, :], in0=yb[:rows, :], scalar=st[:rows, 3:4], in1=GA[:rows, 1, :],
                    op0=ALU.mult, op1=ALU.mult), r=[B_st[1], B_GA[1]], w=[B_y])
                S.op("dve", lambda e, rows=rows, ho=ho, xr_=xr_: e.scalar_tensor_tensor(
                    out=ho[:rows, :], in0=yb[:rows, :], scalar=0.5, in1=xr_[:rows, :],
                    op0=ALU.mult, op1=ALU.add), r=[B_y, B_xres[rs]], w=[B_hb[rs]])
                S.op("sp", lambda e, rows=rows, ho=ho, t0=t0: e.dma_start(out=dst_d[t0:t0 + rows, :], in_=ho[:rows, :]),
                     r=[B_hb[rs]], dma="hb%d" % rs)

        for grp in groups:
            do_group(grp)
        S.barrier()

    def proj_phase(qT, kT, Vb):
        Wi = AR.alloc([128, 8, PROJ], BF16)
        GA = AR.alloc([128, 2, D], F32)
        blocks = [(0, 512), (512, 1024), (1024, 1536), (1536, 2048), (2048, 2560), (2560, 3072), (3072, 3328)]
        B_wi = [Buf("wi%d" % i) for i in range(7)]
        w_in_v = w_in.rearrange("(k p) n -> p k n", p=128)
        for i, (c0, c1) in enumerate(blocks):
            S.op("pool", lambda e, c0=c0, c1=c1: e.dma_start(out=Wi[:, :, c0:c1], in_=w_in_v[:, :, c0:c1]),
                 w=[B_wi[i]], dma="wg%d" % i)
        load_gain(GA, 0, 2)
        S.op("dve", lambda e: e.memset(Vb[:, :, :, 128:129], 1.0), w=[B_vb])
        xld = [AR.alloc([128, D], F32) for i in range(2)]
        B_xld = [Buf("xld0"), Buf("xld1")]
        junk = AR.alloc([128, D], BF16)
        B_junk = Buf("junk")
        xn = AR.alloc([128, D], BF16)
        B_xn = Buf("xn")
        xnT = AR.alloc([128, 8, 512], BF16)
        B_xnT = [Buf("xnT%d" % i) for i in range(4)]
        ob = [AR.alloc([128, 512], F32) for i in range(3)]
        B_ob = [Buf("hb0"), Buf("hb1"), Buf("y")]
        st = AR.alloc([128, 8], F32)
        B_st = Buf("st0")
        cnt = {'x': 0, 'o': 0}

        def do_group(grp):
            ntok = sum(r for _, r in grp)
            g0 = grp[0][0]
            for ti, (t0, rows) in enumerate(grp):
                sl = cnt['x'] % 2
                cnt['x'] += 1
                xs = xld[sl]
                S.op("sp", lambda e, xs=xs, t0=t0, rows=rows: e.dma_start(out=xs[:rows, :], in_=h_scr[t0:t0 + rows, :]),
                     w=[B_xld[sl]], dma="xld%d" % sl)
                rms_rstd(xs[:rows, :], rows, junk[:rows, :], st[:rows, 0:1], st[:rows, 1:2], [B_xld[sl]], B_junk, B_st)
                S.op("dve", lambda e, xs=xs, rows=rows: e.scalar_tensor_tensor(
                    out=xn[:rows, :], in0=xs[:rows, :], scalar=st[:rows, 1:2], in1=GA[:rows, 0, :],
                    op0=ALU.mult, op1=ALU.mult), r=[B_xld[sl], B_st, B_GA[0]], w=[B_xn])
                for kc in range(8):
                    S.op("pe", lambda e, kc=kc, rows=rows: e.transpose(
                        psb16(0)[:, kc * 128:kc * 128 + rows], xn[:rows, kc * 128:(kc + 1) * 128], identb[:rows, :rows]),
                        r=[B_xn, B_ident], w=[PSB[0]])
                S.op("act", lambda e, ti=ti, rows=rows: e.copy(
                    out=xnT[:, :, ti * 128:ti * 128 + rows],
                    in_=psb16(0).rearrange("p (k t) -> p k t", k=8)[:, :, 0:rows]),
                    r=[PSB[0]], w=[B_xnT[ti]])
            for which in range(2):
                for h in range(4):
                    pa = 1 + (h % 2)
                    c0 = which * 512 + h * 128
                    for kc in range(8):
                        S.op("pe", lambda e, kc=kc, pa=pa, c0=c0: e.matmul(
                            PS[pa][:, 0:ntok], lhsT=Wi[:, kc, c0:c0 + 128], rhs=xnT[:, kc, 0:ntok],
                            start=(kc == 0), stop=(kc == 7)),
                            r=[B_wi[which]] + B_xnT[:len(grp)], w=[PSB[pa]])
                    dstT = qT if which == 0 else kT
                    sc = 0.125 if which == 0 else 1.0
                    S.op("act", lambda e, pa=pa, dstT=dstT, h=h, sc=sc: e.activation(
                        out=dstT[:, h, g0:g0 + ntok], in_=PS[pa][:, 0:ntok], func=AF.Copy, scale=sc),
                        r=[PSB[pa]], w=[B_qk])
            for ti, (t0, rows) in enumerate(grp):
                for bi in range(1, 7):
                    c0, c1 = blocks[bi]
                    wdt = c1 - c0
                    pc = 3 + (cnt['o'] % 3)
                    oi = cnt['o'] % 3
                    cnt['o'] += 1
                    for kc in range(8):
                        S.op("pe", lambda e, kc=kc, pc=pc, c0=c0, c1=c1, wdt=wdt, ti=ti, rows=rows: e.matmul(
                            PS[pc][:rows, 0:wdt], lhsT=xnT[:, kc, ti * 128:ti * 128 + rows], rhs=Wi[:, kc, c0:c1],
                            start=(kc == 0), stop=(kc == 7)),
                            r=[B_wi[bi], B_xnT[ti]], w=[PSB[pc]])
                    o_ = ob[oi]
                    S.op("act", lambda e, pc=pc, o_=o_, wdt=wdt, rows=rows: e.copy(out=o_[:rows, 0:wdt], in_=PS[pc][:rows, 0:wdt]),
                         r=[PSB[pc]], w=[B_ob[oi]])
                    if bi == 1:
                        dst = k_out[t0:t0 + rows, :]
                    elif bi == 2:
                        dst = v_out[t0:t0 + rows, :]
                        if t0 < TP:
                            tl = t0 // 128
                            S.op("dve", lambda e, pc=pc, tl=tl: e.tensor_copy(
                                out=Vb[:, tl, :, 0:128], in_=PS[pc][:, :].rearrange("p (h d) -> p h d", h=4)),
                                r=[PSB[pc]], w=[B_vb])
                    else:
                        dst = pr_scr[t0:t0 + rows, c0 - 1536:c1 - 1536]
                    S.op("sp", lambda e, o_=o_, dst=dst, wdt=wdt, rows=rows: e.dma_start(out=dst, in_=o_[:rows, 0:wdt]),
                         r=[B_ob[oi]], dma="ob%d" % oi)

        for grp in groups:
            do_group(grp)
        S.barrier()
        if os.environ.get("KDEV_NOSH"):
            return
        S.op("sp", lambda e: e.dma_start(out=shp_out, in_=pr_scr[TP - 1:TP, :]), dma="sh0")
        S.op("sp", lambda e: e.dma_start(out=shs_out, in_=pr_scr[TP:TT, :].rearrange("(b t) n -> b t n", t=4)[:, 3, :]), dma="sh1")


    C0 = math.exp(-0.5)

    class Tl:
        def __init__(self, shape, dt=F32, name="t"):
            self.ap = AR.alloc(shape, dt)
            self.b = Buf(name)

    def rwkv_phase():
        AR.off = 0
        RV = Tl([64, 5376], F32, "RV")
        LW = Tl([128, 2, 512], BF16, "LW")
        CM = Tl([64, 4, 64], F32, "CM")
        S.op("sp", lambda e: e.dma_start(out=RV.ap, in_=rwvec.partition_broadcast(64)), w=[RV.b], dma="RV")
        S.op("sp", lambda e: e.dma_start(out=CM.ap, in_=cmask), w=[CM.b], dma="CM")
        S.op("pool", lambda e: e.dma_start(out=LW.ap[0:64, 0, :], in_=rw_w2), w=[LW.b], dma="LW")
        S.op("pool", lambda e: e.dma_start(out=LW.ap[64:128, 0, :], in_=rw_a2), w=[LW.b], dma="LW")
        S.op("pool", lambda e: e.dma_start(out=LW.ap[:, 1, :], in_=rw_g2), w=[LW.b], dma="LW")
        ZJ = Tl([1, 8], F32, "ZJ")
        ZR = Tl([1, RCOLS], F32, "ZR")
        S.op("dve", lambda e: e.memset(ZR.ap, 0.0), w=[ZR.b])
        S.op("sp", lambda e: e.dma_start(out=prev_scr[0:1, :], in_=ZR.ap), r=[ZR.b], dma="pv0")
        S.op("sp", lambda e: e.dma_start(out=prev_scr[1:TP, :], in_=pr_scr[0:TP - 1, :]), dma="pv1")
        S.op("sp", lambda e: e.dma_start(
            out=prev_scr[TP:TT, :].rearrange("(b t) n -> b t n", t=4)[:, 1:4, :],
            in_=pr_scr[TP:TT, :].rearrange("(b t) n -> b t n", t=4)[:, 0:3, :]), dma="pv2")
        S.op("sp", lambda e: e.dma_start(
            out=prev_scr[TP:TT, :].rearrange("(b t) n -> b t n", t=4)[:, 0, :], in_=shift0), dma="pv3")
        S.barrier()
        mark = AR.off

        def vec(i0, n, nch):
            return RV.ap[:, i0:i0 + n].unsqueeze(1).to_broadcast([64, nch, n])

        def d1a(tok0, nch, TF=Tl, pname="P"):
            P = TF([64, nch, RCOLS], F32, pname)
            PV = TF([64, nch, RCOLS], F32, "PV")
            S.op("sp", lambda e: e.dma_start(out=P.ap, in_=pr_scr[tok0:tok0 + 64 * nch, :].rearrange("(c t) n -> t c n", t=64)),
                 w=[P.b], dma=pname)
            S.op("sp", lambda e: e.dma_start(out=PV.ap, in_=prev_scr[tok0:tok0 + 64 * nch, :].rearrange("(c t) n -> t c n", t=64)),
                 w=[PV.b], dma="PV")
            if False:
                P2b, PV2b = Buf("P2"), Buf("PV2")
                P2b.w, PV2b.w = P.b.w, PV.b.w
                CS = 1152
                for (eng, c0, c1, pb, pvb) in (("dve", 0, CS, P.b, PV.b), ("pool", CS, RCOLS, P2b, PV2b)):
                    S.op(eng, lambda e, c0=c0, c1=c1: e.tensor_tensor(out=PV.ap[:, :, c0:c1], in0=PV.ap[:, :, c0:c1], in1=P.ap[:, :, c0:c1], op=ALU.subtract), r=[pb], w=[pvb])
                    S.op(eng, lambda e, c0=c0, c1=c1: e.tensor_tensor(out=PV.ap[:, :, c0:c1], in0=PV.ap[:, :, c0:c1],
                                                                      in1=RV.ap[:, c0:c1].unsqueeze(1).to_broadcast([64, nch, c1 - c0]), op=ALU.mult), r=[RV.b], w=[pvb])
                    S.op(eng, lambda e, c0=c0, c1=c1: e.tensor_tensor(out=P.ap[:, :, c0:c1], in0=P.ap[:, :, c0:c1], in1=PV.ap[:, :, c0:c1], op=ALU.add), r=[pvb], w=[pb])
                S.op("pool", lambda e: e.memset(ZJ.ap, 0.0), r=[P2b], w=[P.b, ZJ.b])
            else:
                S.op("dve", lambda e: e.tensor_tensor(out=PV.ap, in0=PV.ap, in1=P.ap, op=ALU.subtract), r=[P.b], w=[PV.b])
                S.op("dve", lambda e: e.tensor_tensor(out=PV.ap, in0=PV.ap, in1=vec(0, RCOLS, nch), op=ALU.mult), r=[RV.b], w=[PV.b])
                S.op("dve", lambda e: e.tensor_tensor(out=P.ap, in0=P.ap, in1=PV.ap, op=ALU.add), r=[PV.b], w=[P.b])
            return P

        def d1(tok0, nch, TF=Tl, P=None):
            T = {}
            if P is None:
                P = d1a(tok0, nch)
            LO = TF([64, nch, 256], BF16, "LO")
            S.op("act", lambda e: e.activation(out=LO.ap[:, :, 0:64], in_=P.ap[:, :, 1536:1600], func=AF.Tanh), r=[P.b], w=[LO.b])
            S.op("act", lambda e: e.copy(out=LO.ap[:, :, 64:128], in_=P.ap[:, :, 1600:1664]), r=[P.b], w=[LO.b])
            S.op("act", lambda e: e.activation(out=LO.ap[:, :, 128:256], in_=P.ap[:, :, 1664:1792], func=AF.Sigmoid), r=[P.b], w=[LO.b])
            LOT = TF([128, nch, 2, 64], BF16, "LOT")
            for c in range(nch):
                for j in range(2):
                    S.op("pe", lambda e, c=c, j=j: e.transpose(
                        psb16(0)[:, (c * 2 + j) * 64:(c * 2 + j + 1) * 64], LO.ap[:, c, j * 128:(j + 1) * 128], identb[:64, :64]),
                        r=[LO.b, B_ident], w=[PSB[0]])
            S.op("act", lambda e: e.copy(out=LOT.ap, in_=psb16(0)[:, 0:nch * 128].rearrange("p (c j t) -> p c j t", c=nch, j=2)),
                 r=[PSB[0]], w=[LOT.b])
            SG = TF([64, nch, 512], F32, "SG")
            AA = TF([64, nch, 512], F32, "AA")
            GG = TF([64, nch, 512], F32, "GG")
            for c in range(nch):
                S.op("pe", lambda e, c=c: e.matmul(PS[1][:64, :], lhsT=LOT.ap[0:64, c, 0, :], rhs=LW.ap[0:64, 0, :], start=True, stop=True),
                     r=[LOT.b, LW.b], w=[PSB[1]])
                S.op("pe", lambda e, c=c: e.matmul(PS[2][:64, :], lhsT=LOT.ap[64:128, c, 0, :], rhs=LW.ap[64:128, 0, :], start=True, stop=True),
                     r=[LOT.b, LW.b], w=[PSB[2]])
                S.op("pe", lambda e, c=c: e.matmul(PS[3][:64, :], lhsT=LOT.ap[:, c, 1, :], rhs=LW.ap[:, 1, :], start=True, stop=True),
                     r=[LOT.b, LW.b], w=[PSB[3]])
                S.op("dve", lambda e, c=c: e.tensor_tensor(out=SG.ap[:, c, :], in0=PS[1][:64, :], in1=RV.ap[:, 1792:2304], op=ALU.add),
                     r=[PSB[1], RV.b], w=[SG.b])
                S.op("dve", lambda e, c=c: e.tensor_tensor(out=AA.ap[:, c, :], in0=PS[2][:64, :], in1=RV.ap[:, 2304:2816], op=ALU.add),
                     r=[PSB[2], RV.b], w=[AA.b])
                S.op("act", lambda e, c=c: e.copy(out=GG.ap[:, c, :], in_=PS[3][:64, :]), r=[PSB[3]], w=[GG.b])
            S.op("act", lambda e: e.activation(out=SG.ap, in_=SG.ap, func=AF.Sigmoid), r=[SG.b], w=[SG.b])
            S.op("act", lambda e: e.activation(out=AA.ap, in_=AA.ap, func=AF.Sigmoid), r=[AA.b], w=[AA.b])
            KK = TF([64, nch, 512], F32, "KK")
            T1 = TF([64, nch, 512], F32, "T1")
            SS = TF([64, nch * 8], F32, "SS")
            kr = P.ap[:, :, 512:1024]
            PL = "pool" if os.environ.get("KDEV_NOPOOL") is None else "dve"
            S.op(PL, lambda e: e.tensor_tensor(out=KK.ap, in0=kr, in1=vec(2816, 512, nch), op=ALU.mult), r=[P.b, RV.b], w=[KK.b])
            S.op(PL, lambda e: e.tensor_tensor(out=T1.ap, in0=KK.ap, in1=KK.ap, op=ALU.mult), r=[KK.b], w=[T1.b])
            S.op("dve", lambda e: e.tensor_reduce(out=SS.ap, in_=T1.ap.rearrange("p c (h n) -> p (c h) n", n=64), axis=AX.X, op=ALU.add),
                 r=[T1.b], w=[SS.b])
            S.op("dve", lambda e: e.tensor_scalar(out=SS.ap, in0=SS.ap, scalar1=1e-24, scalar2=None, op0=ALU.add), r=[SS.b], w=[SS.b])
            S.op("act", lambda e: e.sqrt(out=SS.ap, in_=SS.ap), r=[SS.b], w=[SS.b])
            S.op("dve", lambda e: e.reciprocal(out=SS.ap, in_=SS.ap), r=[SS.b], w=[SS.b])
            S.op(PL, lambda e: e.tensor_tensor(
                out=KK.ap.rearrange("p c (h n) -> p (c h) n", n=64), in0=KK.ap.rearrange("p c (h n) -> p (c h) n", n=64),
                in1=SS.ap.unsqueeze(2).to_broadcast([64, nch * 8, 64]), op=ALU.mult), r=[SS.b], w=[KK.b])
            KM = TF([64, nch, 512], F32, "KM")
            S.op("dve", lambda e: e.scalar_tensor_tensor(out=T1.ap, in0=AA.ap, scalar=-1.0, in1=vec(3328, 512, nch), op0=ALU.add, op1=ALU.mult),
                 r=[AA.b, RV.b], w=[T1.b])
            S.op("dve", lambda e: e.scalar_tensor_tensor(out=KM.ap, in0=T1.ap, scalar=1.0, in1=kr, op0=ALU.add, op1=ALU.mult),
                 r=[T1.b, P.b], w=[KM.b])
            BE = TF([64, nch, 512], F32, "BE")
            S.op("dve", lambda e: e.tensor_tensor(out=BE.ap, in0=KK.ap, in1=AA.ap, op=ALU.mult), r=[KK.b, AA.b], w=[BE.b])
            T.update(P=P, SG=SG, AA=AA, GG=GG, KK=KK, KM=KM, BE=BE, T1=T1)
            return T

        def sample_part():
            AR.off = mark
            T = d1(TP, 1)
            P, SG, KK, KM, BE = T["P"], T["SG"], T["KK"], T["KM"], T["BE"]
            RS = Tl([64, 8, 6, 64], F32, "RS")

            def hv(ap):
                return ap.rearrange("p c (h n) -> p (c h) n", n=64)
            S.op("act", lambda e: e.activation(out=RS.ap[:, :, 0, :], in_=hv(SG.ap), func=AF.Exp, scale=-C0), r=[SG.b], w=[RS.b])
            S.op("dve", lambda e: e.tensor_copy(out=RS.ap[:, :, 1, :], in_=hv(KM.ap)), r=[KM.b], w=[RS.b])
            S.op("dve", lambda e: e.tensor_copy(out=RS.ap[:, :, 2, :], in_=hv(P.ap[:, :, 1024:1536])), r=[P.b], w=[RS.b])
            S.op("dve", lambda e: e.tensor_scalar(out=RS.ap[:, :, 3, :], in0=hv(KK.ap), scalar1=-1.0, scalar2=None, op0=ALU.mult), r=[KK.b], w=[RS.b])
            S.op("dve", lambda e: e.tensor_copy(out=RS.ap[:, :, 4, :], in_=hv(BE.ap)), r=[BE.b], w=[RS.b])
            S.op("dve", lambda e: e.tensor_copy(out=RS.ap[:, :, 5, :], in_=hv(P.ap[:, :, 0:512])), r=[P.b], w=[RS.b])
            S.op("sp", lambda e: e.dma_start(out=rs_scr, in_=RS.ap.rearrange("p h q n -> p (h q n)")), r=[RS.b], dma="RSo")
            X = Tl([128, 4, 6, 64], F32, "X")
            St = Tl([128, 64, 64], F32, "St")
            TM = Tl([128, 64, 64], F32, "TM")
            SA = Tl([128, 64], F32, "SA")
            YS = Tl([128, 4, 64], F32, "YS")
            S.op("sp", lambda e: e.dma_start(out=St.ap.rearrange("p v k -> p (v k)"), in_=wkv0), w=[St.b], dma="St")
            S.barrier()
            for b in range(16):
                S.op("sp", lambda e, b=b: e.dma_start(
                    out=X.ap[8 * b:8 * b + 8, :, :, :].rearrange("p t q n -> p t (q n)"),
                    in_=rs_scr[4 * b:4 * b + 4, :].rearrange("t (h x) -> h t x", h=8)), w=[X.b], dma="X")
            for t in range(4):
                def bk(q, t=t):
                    return X.ap[:, t, q, :].unsqueeze(1).to_broadcast([128, 64, 64])
                S.op("dve", lambda e, bk=bk: e.tensor_tensor(out=TM.ap, in0=St.ap, in1=bk(3), op=ALU.mult), r=[St.b, X.b], w=[TM.b])
                S.op("dve", lambda e: e.tensor_reduce(out=SA.ap, in_=TM.ap, axis=AX.X, op=ALU.add), r=[TM.b], w=[SA.b])
                S.op("dve", lambda e, bk=bk: e.tensor_tensor(out=St.ap, in0=St.ap, in1=bk(0), op=ALU.mult), r=[X.b], w=[St.b])
                S.op("dve", lambda e, bk=bk: e.tensor_tensor(out=TM.ap, in0=SA.ap.unsqueeze(2).to_broadcast([128, 64, 64]), in1=bk(4), op=ALU.mult),
                     r=[SA.b, X.b], w=[TM.b])
                S.op("dve", lambda e: e.tensor_tensor(out=St.ap, in0=St.ap, in1=TM.ap, op=ALU.add), r=[TM.b], w=[St.b])
                S.op("dve", lambda e, bk=bk, t=t: e.tensor_tensor(out=TM.ap, in0=X.ap[:, t, 2, :].unsqueeze(2).to_broadcast([128, 64, 64]), in1=bk(1), op=ALU.mult),
                     r=[X.b], w=[TM.b])
                S.op("dve", lambda e: e.tensor_tensor(out=St.ap, in0=St.ap, in1=TM.ap, op=ALU.add), r=[TM.b], w=[St.b])
                S.op("dve", lambda e, bk=bk: e.tensor_tensor(out=TM.ap, in0=St.ap, in1=bk(5), op=ALU.mult), r=[St.b, X.b], w=[TM.b])
                S.op("dve", lambda e, t=t: e.tensor_reduce(out=YS.ap[:, t, :], in_=TM.ap, axis=AX.X, op=ALU.add), r=[TM.b], w=[YS.b])
            S.op("sp", lambda e: e.dma_start(out=wkvs_out, in_=St.ap.rearrange("p v k -> p (v k)")), r=[St.b], dma="St")
            S.op("sp", lambda e: e.dma_start(out=ys_scr, in_=YS.ap.rearrange("p t n -> p (t n)")), r=[YS.b], dma="YSo")
            S.barrier()
            YT = Tl([64, 1, 512], F32, "YT")
            for b in range(16):
                S.op("sp", lambda e, b=b: e.dma_start(
                    out=YT.ap[4 * b:4 * b + 4, 0, :].rearrange("t (h n) -> t h n", h=8),
                    in_=ys_scr[8 * b:8 * b + 8, :].rearrange("h (t n) -> t h n", t=4)), w=[YT.b], dma="YT")
            d4(T, YT.ap, YT.b, TP, 1)
            S.barrier()

        GN_EPS = 64e-5

        def hv(ap):
            return ap.rearrange("p c (h n) -> p (c h) n", n=64)

        def d4(T, Yap, Yb, tok0, nch, TF=Tl):
            P, AA, GG, KM, T1 = T["P"], T["AA"], T["GG"], T["KM"], T["T1"]
            G = nch * 8
            MU = TF([64, G], F32, "MU")
            VR = TF([64, G], F32, "VR")
            YC = TF([64, nch, 512], F32, "YC")

            def hv(ap):
                return ap.rearrange("p c (h n) -> p c h n", n=64)

            def g3(t_):
                return t_.ap.rearrange("p (c h) -> p c h", h=8)

            def bl(t_):
                return g3(t_).unsqueeze(3).to_broadcast([64, nch, 8, 64])
            S.op("dve", lambda e: e.tensor_reduce(out=g3(MU), in_=hv(Yap), axis=AX.X, op=ALU.add), r=[Yb], w=[MU.b])
            S.op("dve", lambda e: e.tensor_scalar(out=MU.ap, in0=MU.ap, scalar1=1.0 / 64, scalar2=None, op0=ALU.mult), r=[MU.b], w=[MU.b])
            S.op("dve", lambda e: e.tensor_tensor(out=hv(YC.ap), in0=hv(Yap), in1=bl(MU), op=ALU.subtract), r=[Yb, MU.b], w=[YC.b])
            S.op("dve", lambda e: e.tensor_tensor(out=T1.ap, in0=YC.ap, in1=YC.ap, op=ALU.mult), r=[YC.b], w=[T1.b])
            S.op("dve", lambda e: e.tensor_reduce(out=g3(VR), in_=hv(T1.ap), axis=AX.X, op=ALU.add), r=[T1.b], w=[VR.b])
            S.op("dve", lambda e: e.tensor_scalar(out=VR.ap, in0=VR.ap, scalar1=1.0 / 64, scalar2=GN_EPS, op0=ALU.mult, op1=ALU.add), r=[VR.b], w=[VR.b])
            S.op("act", lambda e: e.sqrt(out=VR.ap, in_=VR.ap), r=[VR.b], w=[VR.b])
            S.op("dve", lambda e: e.reciprocal(out=VR.ap, in_=VR.ap), r=[VR.b], w=[VR.b])
            S.op("dve", lambda e: e.tensor_tensor(out=hv(YC.ap), in0=hv(YC.ap), in1=bl(VR), op=ALU.mult), r=[VR.b], w=[YC.b])
            S.op("dve", lambda e: e.tensor_tensor(out=YC.ap, in0=YC.ap, in1=vec(4352, 512, nch), op=ALU.mult), r=[RV.b], w=[YC.b])
            S.op("dve", lambda e: e.tensor_tensor(out=YC.ap, in0=YC.ap, in1=vec(4864, 512, nch), op=ALU.add), r=[RV.b], w=[YC.b])
            S.op("dve", lambda e: e.tensor_tensor(out=T1.ap, in0=P.ap[:, :, 0:512], in1=KM.ap, op=ALU.mult), r=[P.b, KM.b], w=[T1.b])
            S.op("dve", lambda e: e.tensor_tensor(out=T1.ap, in0=T1.ap, in1=vec(3840, 512, nch), op=ALU.mult), r=[RV.b], w=[T1.b])
            S.op("dve", lambda e: e.tensor_reduce(out=g3(MU), in_=hv(T1.ap), axis=AX.X, op=ALU.add), r=[T1.b], w=[MU.b])
            S.op("dve", lambda e: e.tensor_tensor(out=hv(T1.ap), in0=hv(P.ap[:, :, 1024:1536]), in1=bl(MU), op=ALU.mult), r=[P.b, MU.b], w=[T1.b])
            S.op("dve", lambda e: e.tensor_tensor(out=YC.ap, in0=YC.ap, in1=T1.ap, op=ALU.add), r=[T1.b], w=[YC.b])
            S.op("dve", lambda e: e.tensor_tensor(out=YC.ap, in0=YC.ap, in1=GG.ap, op=ALU.mult), r=[GG.b], w=[YC.b])
            S.op("sp", lambda e: e.dma_start(out=orw_scr[tok0:tok0 + 64 * nch, :].rearrange("(c t) n -> t c n", t=64), in_=YC.ap),
                 r=[YC.b], dma="YC")

        bank = [0]

        def nb():
            bank[0] = (bank[0] % 7) + 1
            return bank[0]

        def prompt_part():
            AR.off = mark
            H = [Tl([64, 8, 64], F32, "H0"), Tl([64, 8, 64], F32, "H1")]
            S.op("dve", lambda e: e.memset(H[0].ap, 0.0), w=[H[0].b])
            umark = AR.off
            ucache = {}

            def UT(shape, dt=F32, name="t"):
                if name not in ucache:
                    ucache[name] = Tl(shape, dt, name)
                return ucache[name]
            Pn = d1a(0, 2, UT, "Pp0")
            cur = 0
            I64 = identf[:64, :64]
            F32R = mybir.dt.float32r
            USE_R = False

            def rr(ap):
                return ap.bitcast(F32R) if (USE_R and ap.dtype == F32) else ap
            def do_unit(u, Pin, cur):
                Pnext = None
                T = d1(u * 128, 2, UT, Pin)
                P, SG, AA, KK, KM, BE, T1 = T["P"], T["SG"], T["AA"], T["KK"], T["KM"], T["BE"], T["T1"]
                RWL = int(os.environ.get("KDEV_RW", 9))
                if RWL < 1:
                    S.barrier()
                    return cur, Pnext
                GM = UT([64, 2, 512], F32, "GM")
                GI = UT([64, 2, 512], F32, "GI")
                GP = UT([64, 2, 512], F32, "GP")
                for c in range(2):
                    bk = nb()
                    S.op("pe", lambda e, c=c, bk=bk: e.matmul(PS[bk][:64, :], lhsT=CM.ap[:, 2, :], rhs=SG.ap[:, c, :], start=True, stop=True),
                         r=[CM.b, SG.b], w=[PSB[bk]])
                    S.op("act", lambda e, c=c, bk=bk: e.activation(out=GM.ap[:, c, :], in_=PS[bk][:64, :], func=AF.Exp, scale=-C0), r=[PSB[bk]], w=[GM.b])
                    S.op("act", lambda e, c=c, bk=bk: e.activation(out=GI.ap[:, c, :], in_=PS[bk][:64, :], func=AF.Exp, scale=C0), r=[PSB[bk]], w=[GI.b])
                    S.op("dve", lambda e, c=c, bk=bk: e.tensor_tensor(out=T1.ap[:, c, :], in0=PS[bk][:64, :], in1=SG.ap[:, c, :], op=ALU.subtract),
                         r=[PSB[bk], SG.b], w=[T1.b])
                S.op("act", lambda e: e.activation(out=GP.ap, in_=T1.ap, func=AF.Exp, scale=-C0), r=[T1.b], w=[GP.b])
                AL = UT([64, 2, 512], F32, "AL")
                BT = UT([64, 2, 512], F32, "BT")
                KT = UT([64, 2, 512], F32, "KT")
                RB = UT([64, 2, 512], F32, "RB")
                S.op("dve", lambda e: e.scalar_tensor_tensor(out=AL.ap, in0=KK.ap, scalar=-1.0, in1=GP.ap, op0=ALU.mult, op1=ALU.mult), r=[KK.b, GP.b], w=[AL.b])
                PL = "pool" if os.environ.get("KDEV_NOPOOL") is None else "dve"
                S.op(PL, lambda e: e.tensor_tensor(out=BT.ap, in0=BE.ap, in1=GI.ap, op=ALU.mult), r=[BE.b, GI.b], w=[BT.b])
                S.op("dve", lambda e: e.tensor_tensor(out=KT.ap, in0=KM.ap, in1=GI.ap, op=ALU.mult), r=[KM.b, GI.b], w=[KT.b])
                S.op(PL, lambda e: e.tensor_tensor(out=RB.ap, in0=P.ap[:, :, 0:512], in1=GM.ap, op=ALU.mult), r=[P.b, GM.b], w=[RB.b])
                GC = UT([64, 16], F32, "GC")
                bk = nb()
                for c in range(2):
                    for h in range(8):
                        S.op("pe", lambda e, c=c, h=h, bk=bk: e.matmul(
                            PS[bk][:64, c * 8 + h:c * 8 + h + 1], lhsT=GM.ap[:, c, h * 64:(h + 1) * 64], rhs=CM.ap[:, 3, 0:1], start=True, stop=True),
                            r=[GM.b, CM.b], w=[PSB[bk]])
                S.op("act", lambda e, bk=bk: e.copy(out=GC.ap, in_=PS[bk][:64, 0:16]), r=[PSB[bk]], w=[GC.b])
                if RWL < 2:
                    S.barrier()
                    return cur, Pnext
                XT = {}
                for nm, src in (("RB", RB), ("AL", AL), ("BT", BT), ("KT", KT)):
                    dst = UT([64, 2, 8, 64], F32, nm + "T")
                    XT[nm] = dst
                    for c in range(2):
                        bk = nb()
                        for h in range(8):
                            S.op("pe", lambda e, c=c, h=h, bk=bk, src=src: e.transpose(
                                PS[bk][:64, h * 64:(h + 1) * 64], src.ap[:, c, h * 64:(h + 1) * 64], I64),
                                r=[src.b, B_ident], w=[PSB[bk]])
                        eng = "act" if c == 0 else "dve"
                        if eng == "act":
                            S.op("act", lambda e, c=c, bk=bk, dst=dst: e.copy(out=dst.ap[:, c, :, :], in_=PS[bk][:64, :].rearrange("p (h t) -> p h t", h=8)),
                                 r=[PSB[bk]], w=[dst.b])
                        else:
                            S.op("dve", lambda e, c=c, bk=bk, dst=dst: e.tensor_copy(out=dst.ap[:, c, :, :], in_=PS[bk][:64, :].rearrange("p (h t) -> p h t", h=8)),
                                 r=[PSB[bk]], w=[dst.b])
                RBT, ALT, BTT, KTT = XT["RB"], XT["AL"], XT["BT"], XT["KT"]
                if u + 1 < 16:
                    Pnext = d1a((u + 1) * 128, 2, UT, "Pp%d" % ((u + 1) % 2))
                DT2 = BF16 if os.environ.get("KDEV_D2F32") is None else F32
                Lm = [UT([64, 2, 8, 64], DT2, "L0"), UT([64, 2, 8, 64], DT2, "L1")]
                Nm = [UT([64, 2, 8, 64], DT2, "N0"), UT([64, 2, 8, 64], DT2, "N1")]
                Pm = [UT([64, 2, 8, 64], DT2, "P0"), UT([64, 2, 8, 64], F32, "P1")]
                Pb = UT([64, 2, 8, 64], DT2, "Pb")
                I64b = identb[:64, :64] if DT2 == BF16 else identf[:64, :64]
                AK = UT([64, 2, 8, 64], F32, "AK")
                RBm = UT([64, 2, 8, 64], F32, "RBm")
                RKm = UT([64, 2, 8, 64], F32, "RKm")
                WT = UT([64, 2, 8, 64], F32, "WT")
                GT = UT([64, 2, 8, 64], F32, "GT")

                def mm8(c, lh, rh, rb):
                    bk = nb()
                    for h in range(8):
                        S.op("pe", lambda e, h=h, bk=bk: e.matmul(PS[bk][:64, h * 64:(h + 1) * 64], lhsT=rr(lh(h)), rhs=rr(rh(h)), start=True, stop=True),
                             r=rb, w=[PSB[bk]])
                    return bk

                def psv(bk):
                    return PS[bk][:64, :].rearrange("p (h t) -> p h t", h=8)

                def mk(m):
                    return CM.ap[:, m, :].unsqueeze(1).to_broadcast([64, 8, 64])
                for c in range(2):
                    for (A_, B_, m, dst) in ((ALT, BTT, 0, Lm[0]), (BTT, ALT, 1, Nm[0]), (ALT, KTT, 0, AK), (BTT, RBT, 2, RBm), (KTT, RBT, 2, RKm)):
                        bk = mm8(c, lambda h, A_=A_, c=c: A_.ap[:, c, h, :], lambda h, B_=B_, c=c: B_.ap[:, c, h, :], [A_.b, B_.b])
                        S.op("dve", lambda e, bk=bk, dst=dst, m=m, c=c: e.tensor_tensor(out=dst.ap[:, c, :, :], in0=psv(bk), in1=mk(m), op=ALU.mult),
                             r=[PSB[bk], CM.b], w=[dst.b])
                    S.op("dve", lambda e, c=c: e.tensor_tensor(out=Pm[0].ap[:, c, :, :], in0=Nm[0].ap[:, c, :, :],
                                                               in1=I64b.unsqueeze(1).to_broadcast([64, 8, 64]), op=ALU.add),
                         r=[Nm[0].b, B_ident], w=[Pm[0].b])
                for j in range(1, 6):
                    a, b_ = (j - 1) % 2, j % 2
                    for c in range(2):
                        bk = mm8(c, lambda h, c=c, a=a: Nm[a].ap[:, c, h, :], lambda h, c=c, a=a: Lm[a].ap[:, c, h, :], [Nm[a].b, Lm[a].b])
                        S.op("act", lambda e, bk=bk, c=c, b_=b_: e.copy(out=Lm[b_].ap[:, c, :, :], in_=psv(bk)), r=[PSB[bk]], w=[Lm[b_].b])
                        if j < 5:
                            bk = mm8(c, lambda h, c=c, a=a: Lm[a].ap[:, c, h, :], lambda h, c=c, a=a: Nm[a].ap[:, c, h, :], [Nm[a].b, Lm[a].b])
                            S.op("act", lambda e, bk=bk, c=c, b_=b_: e.copy(out=Nm[b_].ap[:, c, :, :], in_=psv(bk)), r=[PSB[bk]], w=[Nm[b_].b])
                        pin = Pm[0] if a == 0 else Pb
                        pout = Pm[1] if j == 5 else (Pb if a == 0 else Pm[0])
                        bk = mm8(c, lambda h, c=c, b_=b_: Lm[b_].ap[:, c, h, :], lambda h, c=c, pin=pin: pin.ap[:, c, h, :], [Lm[b_].b, pin.b])
                        S.op("dve", lambda e, bk=bk, c=c, pin=pin, pout=pout: e.tensor_tensor(out=pout.ap[:, c, :, :], in0=psv(bk), in1=pin.ap[:, c, :, :], op=ALU.add),
                             r=[PSB[bk], pin.b], w=[pout.b])
                PF = Pm[1]
                for c in range(2):
                    bk = mm8(c, lambda h, c=c: AL.ap[:, c, h * 64:(h + 1) * 64], lambda h, c=c: PF.ap[:, c, h, :], [AL.b, PF.b])
                    S.op("act", lambda e, bk=bk, c=c: e.copy(out=WT.ap[:, c, :, :], in_=psv(bk)), r=[PSB[bk]], w=[WT.b])
                    bk = mm8(c, lambda h, c=c: AK.ap[:, c, h, :], lambda h, c=c: PF.ap[:, c, h, :], [AK.b, PF.b])
                    S.op("dve", lambda e, bk=bk, c=c: e.tensor_copy(out=GT.ap[:, c, :, :], in_=psv(bk)), r=[PSB[bk]], w=[GT.b])
                if RWL < 3:
                    S.barrier()
                    return cur, Pnext
                U = UT([64, 512], F32, "U")
                Y = UT([64, 2, 512], F32, "Y")
                for c in range(2):
                    Hc, Hn = H[cur], H[1 - cur]

                    def V(h, c=c):
                        return P.ap[:, c, 1024 + h * 64:1024 + (h + 1) * 64]
                    bu = nb()
                    for h in range(8):
                        S.op("pe", lambda e, h=h, c=c, bu=bu, Hc=Hc: e.matmul(PS[bu][:64, h * 64:(h + 1) * 64], lhsT=rr(WT.ap[:, c, h, :]), rhs=rr(Hc.ap[:, h, :]), start=True, stop=False),
                             r=[WT.b, Hc.b], w=[PSB[bu]])
                        S.op("pe", lambda e, h=h, c=c, bu=bu, V=V: e.matmul(PS[bu][:64, h * 64:(h + 1) * 64], lhsT=rr(GT.ap[:, c, h, :]), rhs=rr(V(h)), start=False, stop=True),
                             r=[GT.b, P.b], w=[PSB[bu]])
                    S.op("act", lambda e, bu=bu: e.copy(out=U.ap, in_=PS[bu][:64, :]), r=[PSB[bu]], w=[U.b])
                    bh = nb()
                    for h in range(8):
                        S.op("pe", lambda e, h=h, c=c, bh=bh: e.matmul(PS[bh][:64, h * 64:(h + 1) * 64], lhsT=rr(BT.ap[:, c, h * 64:(h + 1) * 64]), rhs=rr(U.ap[:, h * 64:(h + 1) * 64]), start=True, stop=False),
                             r=[BT.b, U.b], w=[PSB[bh]])
                        S.op("pe", lambda e, h=h, c=c, bh=bh, V=V: e.matmul(PS[bh][:64, h * 64:(h + 1) * 64], lhsT=rr(KT.ap[:, c, h * 64:(h + 1) * 64]), rhs=rr(V(h)), start=False, stop=True),
                             r=[KT.b, P.b], w=[PSB[bh]])
                    by = nb()
                    for h in range(8):
                        S.op("pe", lambda e, h=h, c=c, by=by, Hc=Hc: e.matmul(PS[by][:64, h * 64:(h + 1) * 64], lhsT=rr(RBT.ap[:, c, h, :]), rhs=rr(Hc.ap[:, h, :]), start=True, stop=False),
                             r=[RBT.b, Hc.b], w=[PSB[by]])
                        S.op("pe", lambda e, h=h, c=c, by=by: e.matmul(PS[by][:64, h * 64:(h + 1) * 64], lhsT=rr(RBm.ap[:, c, h, :]), rhs=rr(U.ap[:, h * 64:(h + 1) * 64]), start=False, stop=False),
                             r=[RBm.b, U.b], w=[PSB[by]])
                        S.op("pe", lambda e, h=h, c=c, by=by, V=V: e.matmul(PS[by][:64, h * 64:(h + 1) * 64], lhsT=rr(RKm.ap[:, c, h, :]), rhs=rr(V(h)), start=False, stop=True),
                             r=[RKm.b, P.b], w=[PSB[by]])
                    S.op("act", lambda e, by=by, c=c: e.copy(out=Y.ap[:, c, :], in_=PS[by][:64, :]), r=[PSB[by]], w=[Y.b])
                    S.op("dve", lambda e, bh=bh, Hc=Hc, Hn=Hn: e.tensor_tensor(out=Hn.ap, in0=Hc.ap, in1=PS[bh][:64, :].rearrange("p (h v) -> p h v", h=8), op=ALU.add),
                         r=[Hc.b, PSB[bh]], w=[Hn.b])
                    S.op("dve", lambda e, c=c, Hn=Hn: e.tensor_tensor(out=Hn.ap, in0=Hn.ap, in1=GC.ap[:, c * 8:(c + 1) * 8].unsqueeze(2).to_broadcast([64, 8, 64]), op=ALU.mult),
                         r=[GC.b], w=[Hn.b])
                    cur = 1 - cur
                if RWL >= 4:
                    d4(T, Y.ap, Y.b, u * 128, 2, UT)
                return cur, Pnext

            for u in range(16):
                cur, Pn = do_unit(u, Pn, cur)
            Hf = H[cur]
            HT = Tl([64, 8, 64], F32, "HT")
            bk = nb()
            for h in range(8):
                S.op("pe", lambda e, h=h, bk=bk: e.transpose(PS[bk][:64, h * 64:(h + 1) * 64], Hf.ap[:, h, :], I64), r=[Hf.b, B_ident], w=[PSB[bk]])
            S.op("act", lambda e, bk=bk: e.copy(out=HT.ap, in_=PS[bk][:64, :].rearrange("p (h k) -> p h k", h=8)), r=[PSB[bk]], w=[HT.b])
            S.op("sp", lambda e: e.dma_start(out=wkvp_out.rearrange("h v k -> v h k"), in_=HT.ap), r=[HT.b], dma="HT")
            S.barrier()

        sample_part()
        if os.environ.get("KDEV_NOPROMPT") is None:
            prompt_part()


    LAM_INIT = 0.8 - 0.6 * math.exp(-0.3 * 0)
    SUBLN_EPS = 1e-5
    LF = 383

    def attn_setup():
        A = {}
        LQ = Tl([128, 4, 64], F32, "LQ")
        S.op("sp", lambda e: e.dma_start(out=LQ.ap.rearrange("p a n -> p (a n)"), in_=lamvec.partition_broadcast(128)), w=[LQ.b], dma="LQ")
        LS = Tl([128, 4], F32, "LS")
        LAM = Tl([128, 2], F32, "LAM")
        PRD = Tl([128, 2, 64], F32, "PRD")
        S.op("dve", lambda e: e.tensor_tensor(out=PRD.ap, in0=LQ.ap[:, 0:2, :], in1=LQ.ap[:, 2:4, :], op=ALU.mult), r=[LQ.b], w=[PRD.b])
        S.op("dve", lambda e: e.tensor_reduce(out=LS.ap[:, 0:2], in_=PRD.ap, axis=AX.X, op=ALU.add), r=[PRD.b], w=[LS.b])
        S.op("act", lambda e: e.activation(out=LS.ap[:, 2:4], in_=LS.ap[:, 0:2], func=AF.Exp), r=[LS.b], w=[LS.b])
        S.op("dve", lambda e: e.tensor_tensor(out=LAM.ap[:, 0:1], in0=LS.ap[:, 2:3], in1=LS.ap[:, 3:4], op=ALU.subtract), r=[LS.b], w=[LAM.b])
        S.op("dve", lambda e: e.tensor_scalar(out=LAM.ap[:, 0:1], in0=LAM.ap[:, 0:1], scalar1=LAM_INIT, scalar2=None, op0=ALU.add), r=[LAM.b], w=[LAM.b])
        S.op("dve", lambda e: e.tensor_scalar(out=LAM.ap[:, 1:2], in0=LAM.ap[:, 0:1], scalar1=-1.0, scalar2=None, op0=ALU.mult), r=[LAM.b], w=[LAM.b])
        AS = int(os.environ.get("KDEV_AS", 9))
        if AS < 1:
            return A
        SUBW = Tl([128, 128], F32, "SUBW")
        S.op("sp", lambda e: e.dma_start(out=SUBW.ap, in_=subln.partition_broadcast(128)), w=[SUBW.b], dma="SUBW")
        S.op("dve", lambda e: e.tensor_scalar(out=SUBW.ap, in0=SUBW.ap, scalar1=1.0 - LAM_INIT, scalar2=None, op0=ALU.mult), r=[SUBW.b], w=[SUBW.b])
        if AS < 2:
            return A
        TE = Tl([33, 4], F32, "TE")
        T31 = Tl([33, 4], F32, "T31")
        OH = Tl([33, LF], F32, "OH")
        S.op("sp", lambda e: e.dma_start(out=TE.ap[0:32, :], in_=relb), w=[TE.b], dma="TE")
        S.op("sp", lambda e: e.dma_start(out=T31.ap[0:32, :], in_=relb[31:32, :].partition_broadcast(32)), w=[T31.b], dma="T31")
        S.op("sp", lambda e: e.dma_start(out=OH.ap, in_=onehot), w=[OH.b], dma="OH")
        S.op("dve", lambda e: e.tensor_tensor(out=TE.ap[0:32, :], in0=TE.ap[0:32, :], in1=T31.ap[0:32, :], op=ALU.subtract), r=[T31.b], w=[TE.b])
        S.op("dve", lambda e: e.memset(TE.ap[32:33, :], -30000.0), w=[TE.b])
        if AS < 3:
            return A
        RR = Tl([128, LF], F32, "RR")
        TEB = Tl([33, 4, 128], F32, "TEB")
        S.op("dve", lambda e: e.tensor_copy(out=TEB.ap, in_=TE.ap.unsqueeze(2).to_broadcast([33, 4, 128])), r=[TE.b], w=[TEB.b])
        for h in range(4):
            S.op("pe", lambda e, h=h: e.matmul(PS[1][:, 0:LF], lhsT=TEB.ap[:, h, :], rhs=OH.ap, start=True, stop=True),
                 r=[TEB.b, OH.b], w=[PSB[1]])
            S.op("act", lambda e: e.copy(out=RR.ap, in_=PS[1][:, 0:LF]), r=[PSB[1]], w=[RR.b])
            S.op("sp", lambda e, h=h: e.dma_start(out=r_scr[h], in_=RR.ap), r=[RR.b], dma="RR")
        S.barrier()
        if AS < 4:
            return A
        BF = Tl([128, 4, 2, 128], F32, "BF")
        BB = Tl([128, 4, 2, 128], BF16, "BB")
        for h in range(4):
            for dl in range(2):
                src = bass.AP(tensor=r_scr.tensor, offset=h * 128 * LF + 127 + 128 * dl, ap=[[LF - 1, 128], [1, 128]])
                S.op("sp", lambda e, h=h, dl=dl, src=src: e.dma_start(out=BF.ap[:, h, dl, :], in_=src), w=[BF.b], dma="BF")
        S.op("dve", lambda e: e.tensor_copy(out=BB.ap, in_=BF.ap), r=[BF.b], w=[BB.b])
        if os.environ.get("KDEV_DBG"):
            bfd = dout("bf_dbg", [128, 4 * 2 * 128])
            S.op("sp", lambda e: e.dma_start(out=bfd, in_=BF.ap.rearrange("p h d q -> p (h d q)")), r=[BF.b], dma="bfd")
        A.update(LAM=LAM, SUBW=SUBW, BB=BB)
        A["mark"] = AR.off
        return A

    def subln_tile(OA, rows, G, junk, st, SUBW):
        S.op("dve", lambda e: e.tensor_tensor(out=junk.ap, in0=OA.ap, in1=OA.ap, op=ALU.mult), r=[OA.b], w=[junk.b])
        S.op("dve", lambda e: e.tensor_reduce(out=st.ap, in_=junk.ap, axis=AX.X, op=ALU.add), r=[junk.b], w=[st.b])
        S.op("dve", lambda e: e.tensor_scalar(out=st.ap, in0=st.ap, scalar1=1.0 / 128, scalar2=SUBLN_EPS, op0=ALU.mult, op1=ALU.add), r=[st.b], w=[st.b])
        S.op("act", lambda e: e.sqrt(out=st.ap, in_=st.ap), r=[st.b], w=[st.b])
        S.op("dve", lambda e: e.reciprocal(out=st.ap, in_=st.ap), r=[st.b], w=[st.b])
        S.op("dve", lambda e: e.tensor_tensor(out=OA.ap, in0=OA.ap, in1=st.ap.unsqueeze(2).to_broadcast([rows, G, 128]), op=ALU.mult), r=[st.b], w=[OA.b])
        S.op("dve", lambda e: e.tensor_tensor(out=OA.ap, in0=OA.ap, in1=SUBW.ap[0:rows, :].unsqueeze(1).to_broadcast([rows, G, 128]), op=ALU.mult),
             r=[SUBW.b], w=[OA.b])

    def attn_prompt(A, qT, kT, Vb):
        LAM, SUBW, BB = A["LAM"], A["SUBW"], A["BB"]
        PT = [Tl([128, 2, 256], BF16, "PT%d" % i) for i in range(2)]
        OA = [Tl([128, 4, 128], F32, "OA%d" % i) for i in range(2)]
        RS2 = Tl([128, 2], F32, "RS2")
        junk = Tl([128, 4, 128], F32, "jk")
        st = Tl([128, 4], F32, "st4")
        sb = [(0, 1), (2, 3)]
        iters = []
        for G in range(8):
            for h in range(4):
                nkb = 2 * G + 2
                for kb in range(nkb):
                    iters.append((G, h, kb, nkb))

        def obank(h):
            return [4 + 2 * (h % 2), 5 + 2 * (h % 2)]

        def emit_S(i):
            G, h, kb, nkb = iters[i]
            qt0 = 2 * G
            i_ = i % 2
            far = kb < qt0 - 1
            j_lo = 0 if kb <= qt0 else 1
            for m in range(2):
                bk = sb[i_][m]
                kop = kT[m * 64:(m + 1) * 64, h, kb * 128:(kb + 1) * 128]
                for j in range(j_lo, 2):
                    qt = qt0 + j
                    dl = qt - kb
                    S.op("pe", lambda e, m=m, bk=bk, kop=kop, h=h, qt=qt, j=j, dl=dl: e.matmul(
                        PS[bk][:, j * 128:(j + 1) * 128], lhsT=kop,
                        rhs=qT[m * 64:(m + 1) * 64, h, qt * 128:(qt + 1) * 128], start=True, stop=(dl >= 2)),
                        r=[B_qk], w=[PSB[bk]])
                    if dl < 2:
                        S.op("pe", lambda e, m=m, bk=bk, h=h, j=j, dl=dl: e.matmul(
                            PS[bk][:, j * 128:(j + 1) * 128], lhsT=identb[:, :],
                            rhs=BB.ap[:, h, dl, :], start=False, stop=True),
                            r=[BB.b, B_ident], w=[PSB[bk]])

        def emit_exp(i):
            G, h, kb, nkb = iters[i]
            i_ = i % 2
            j_lo = 0 if kb <= 2 * G else 1
            pt = PT[i_]
            for m in range(2):
                bk = sb[i_][m]
                S.op("act", lambda e, bk=bk, pt=pt, j_lo=j_lo, m=m: e.activation(
                    out=pt.ap[:, m, j_lo * 128:256], in_=PS[bk][:, j_lo * 128:256], func=AF.Exp),
                    r=[PSB[bk]], w=[pt.b])

        def emit_PV(i):
            G, h, kb, nkb = iters[i]
            qt0 = 2 * G
            i_ = i % 2
            pt = PT[i_]
            ob = obank(h)
            j_lo = 0 if kb <= qt0 else 1
            if kb == 0:
                for j in range(2):
                    S.op("dve", lambda e, j=j, ob=ob: e.memset(PS[ob[j]][:, :], 0.0), w=[PSB[ob[j]]])
            for j in range(j_lo, 2):
                qt = qt0 + j
                for m in range(2):
                    S.op("pe", lambda e, j=j, m=m, pt=pt, kb=kb, h=h, qt=qt, ob=ob: e.matmul(
                        PS[ob[j]][:, m * 256:m * 256 + 129], lhsT=pt.ap[:, m, j * 128:(j + 1) * 128],
                        rhs=Vb[:, kb, h, 0:129], start=False, stop=(kb == qt), skip_group_check=True),
                        r=[pt.b, B_vb], w=[PSB[ob[j]]])
            if kb == nkb - 1:
                for j in range(2):
                    oa = OA[j]
                    ov = PS[ob[j]][:, :].rearrange("p (m x) -> p m x", m=2)
                    S.op("dve", lambda e, ov=ov: e.reciprocal(out=RS2.ap, in_=ov[:, :, 128]), r=[PSB[ob[j]]], w=[RS2.b])
                    S.op("dve", lambda e: e.tensor_tensor(out=RS2.ap[:, 1:2], in0=RS2.ap[:, 1:2], in1=LAM.ap[:, 1:2], op=ALU.mult), r=[LAM.b], w=[RS2.b])
                    S.op("dve", lambda e, ov=ov, oa=oa, h=h: e.tensor_scalar(out=oa.ap[:, h, :], in0=ov[:, 0, 0:128], scalar1=RS2.ap[:, 0:1], scalar2=None, op0=ALU.mult),
                         r=[PSB[ob[j]], RS2.b], w=[oa.b])
                    S.op("dve", lambda e, ov=ov, oa=oa, h=h: e.scalar_tensor_tensor(out=oa.ap[:, h, :], in0=ov[:, 1, 0:128], scalar=RS2.ap[:, 1:2], in1=oa.ap[:, h, :],
                                                                                  op0=ALU.mult, op1=ALU.add), r=[PSB[ob[j]], RS2.b], w=[oa.b])
                if h == 3:
                    for j in range(2):
                        subln_tile(OA[j], 128, 4, junk, st, SUBW)
                        S.op("sp", lambda e, j=j, qt0=qt0: e.dma_start(out=oat_scr[(qt0 + j) * 128:(qt0 + j + 1) * 128, :], in_=OA[j].ap.rearrange("p h d -> p (h d)")),
                             r=[OA[j].b], dma="OA%d" % j)

        emit_S(0)
        for i in range(len(iters)):
            if i + 1 < len(iters):
                emit_S(i + 1)
            emit_exp(i)
            emit_PV(i)
        S.barrier()

    def attn_sample(A, qT, kT):
        LAM, SUBW, BB = A["LAM"], A["SUBW"], A["BB"]
        AR.off = A["mark"]
        PTB = Tl([128, 256], I32, "PTB")
        IOT = Tl([128, 1], I32, "IOT")
        IDX = Tl([128, 256], I32, "IDX")
        S.op("sp", lambda e: e.dma_start(out=PTB.ap, in_=ptab.partition_broadcast(128)), w=[PTB.b], dma="PTB")
        S.op("sp", lambda e: e.dma_start(out=IOT.ap, in_=iotap), w=[IOT.b], dma="IOT")
        S.op("pool", lambda e: e.tensor_scalar(out=IDX.ap, in0=PTB.ap, scalar1=128, scalar2=IOT.ap[:, 0:1], op0=ALU.mult, op1=ALU.add),
             r=[PTB.b, IOT.b], w=[IDX.b])
        QB = Tl([128, 16, 4, 8], BF16, "QB")
        S.op("dve", lambda e: e.memset(QB.ap, 0.0), w=[QB.b])
        for m in range(2):
            S.op("dve", lambda e, m=m: e.tensor_copy(out=QB.ap[m * 64:(m + 1) * 64, :, :, m * 4:(m + 1) * 4],
                                                     in_=qT[m * 64:(m + 1) * 64, :, TP:TT].rearrange("p h (b t) -> p b h t", t=4)),
                 r=[B_qk], w=[QB.b])
        VN = Tl([4, 16, 512], F32, "VN")
        S.op("sp", lambda e: e.dma_start(out=VN.ap, in_=v_out[TP:TT, :].rearrange("(b t) n -> t b n", t=4)), w=[VN.b], dma="VN")
        ONE = Tl([128, 1], F32, "ONE")
        S.op("dve", lambda e: e.memset(ONE.ap, 1.0), w=[ONE.b])
        BS = Tl([128, 4, 2, 4], BF16, "BS")
        BN = Tl([4, 4, 2, 4], BF16, "BN")
        for m in range(2):
            S.op("dve", lambda e, m=m: e.tensor_copy(out=BS.ap[:, :, m, :], in_=BB.ap[:, :, 1, 0:4]), r=[BB.b], w=[BS.b])
            S.op("dve", lambda e, m=m: e.tensor_copy(out=BN.ap[:, :, m, :], in_=BB.ap[0:4, :, 0, 0:4]), r=[BB.b], w=[BN.b])
        OS = Tl([8, 16, 4, 128], F32, "OS")
        SMs = Tl([8, 16, 4], F32, "SMs")
        SEL = Tl([8, 3, 4], F32, "SEL")
        mark2 = AR.off
        NS = 4
        KP = [Tl([128, 512], F32, "KP%d" % i) for i in range(NS)]
        VP = [Tl([128, 512], F32, "VP%d" % i) for i in range(16)]
        B_vpg = [Buf("vpg%d" % i) for i in range(4)]
        KTb = [Tl([128, 4, 128], BF16, "KTb%d" % i) for i in range(2)]
        PTs = Tl([128, 512], F32, "PTs")
        PTN = Tl([4, 32], F32, "PTN")
        PSJ = Tl([128, 32], F32, "PSJ")
        cnt = [0]
        for b in range(16):
            sbk, obk, mbk = 1, 2, 3
            for j in range(16):
                sl = cnt[0] % NS
                cnt[0] += 1
                col = b * 16 + j
                S.op("pool", lambda e, sl=sl, col=col: e.indirect_dma_start(
                    out=KP[sl].ap, out_offset=None, in_=ck2, in_offset=bass.IndirectOffsetOnAxis(ap=IDX.ap[:, col:col + 1], axis=0)),
                    r=[IDX.b], w=[KP[sl].b], dma="KP%d" % sl)
                S.op("pool", lambda e, j=j, col=col: e.indirect_dma_start(
                    out=VP[j].ap, out_offset=None, in_=cv2, in_offset=bass.IndirectOffsetOnAxis(ap=IDX.ap[:, col:col + 1], axis=0)),
                    r=[IDX.b], w=[B_vpg[j % 4]], dma="VP%d" % (j % 4))
                tb = 4 + (j % 2)
                for h in range(4):
                    S.op("pe", lambda e, h=h, tb=tb, sl=sl: e.transpose(PS[tb][:, h * 128:(h + 1) * 128], KP[sl].ap[:, h * 128:(h + 1) * 128], identf[:, :]),
                         r=[KP[sl].b, B_ident], w=[PSB[tb]])
                kt = KTb[j % 2]
                if j % 2 == 0:
                    S.op("act", lambda e, tb=tb, kt=kt: e.copy(out=kt.ap, in_=PS[tb][:, :].rearrange("p (h k) -> p h k", h=4)), r=[PSB[tb]], w=[kt.b])
                else:
                    S.op("dve", lambda e, tb=tb, kt=kt: e.tensor_copy(out=kt.ap, in_=PS[tb][:, :].rearrange("p (h k) -> p h k", h=4)), r=[PSB[tb]], w=[kt.b])
                if j == 15:
                    S.op("pe", lambda e: e.matmul(PS[sbk][:, 480:512], lhsT=identb[:, :], rhs=BS.ap.rearrange("p h m t -> p (h m t)"), start=True, stop=False),
                         r=[BS.b, B_ident], w=[PSB[sbk]])
                for h in range(4):
                    S.op("pe", lambda e, h=h, j=j, kt=kt, b=b: e.matmul(PS[sbk][:, j * 32 + h * 8:j * 32 + (h + 1) * 8], lhsT=kt.ap[:, h, :], rhs=QB.ap[:, b, h, :],
                                                                        start=(j != 15), stop=True, skip_group_check=True), r=[kt.b, QB.b], w=[PSB[sbk]])
            S.op("pe", lambda e: e.matmul(PS[mbk][0:4, 0:32], lhsT=identb[0:4, 0:4], rhs=BN.ap.rearrange("p h m t -> p (h m t)"), start=True, stop=False),
                 r=[BN.b, B_ident], w=[PSB[mbk]])
            for h in range(4):
                S.op("pe", lambda e, h=h, b=b: e.matmul(PS[mbk][0:4, h * 8:(h + 1) * 8], lhsT=kT[:, h, TP + 4 * b:TP + 4 * b + 4], rhs=QB.ap[:, b, h, :],
                                                        start=False, stop=True, skip_group_check=True), r=[B_qk, QB.b], w=[PSB[mbk]])
            S.op("act", lambda e: e.activation(out=PTs.ap, in_=PS[sbk][:, :], func=AF.Exp), r=[PSB[sbk]], w=[PTs.b])
            S.op("act", lambda e: e.activation(out=PTN.ap, in_=PS[mbk][0:4, 0:32], func=AF.Exp), r=[PSB[mbk]], w=[PTN.b])
            S.op("dve", lambda e: e.tensor_reduce(out=PSJ.ap, in_=PTs.ap.rearrange("p (j x) -> p x j", j=16), axis=AX.X, op=ALU.add), r=[PTs.b], w=[PSJ.b])
            for h in range(4):
                S.op("pe", lambda e, h=h: e.matmul(PS[mbk][0:8, 64 + h:65 + h], lhsT=PSJ.ap[:, h * 8:(h + 1) * 8], rhs=ONE.ap[:, 0:1], start=True, stop=False),
                     r=[PSJ.b, ONE.b], w=[PSB[mbk]])
                S.op("pe", lambda e, h=h: e.matmul(PS[mbk][0:8, 64 + h:65 + h], lhsT=PTN.ap[:, h * 8:(h + 1) * 8], rhs=ONE.ap[0:4, 0:1], start=False, stop=True),
                     r=[PTN.b, ONE.b], w=[PSB[mbk]])
            S.op("act", lambda e, b=b: e.copy(out=SMs.ap[:, b, :], in_=PS[mbk][0:8, 64:68]), r=[PSB[mbk]], w=[SMs.b])
            S.op("dve", lambda e: e.memset(PS[obk][0:8, :], 0.0), w=[PSB[obk]])
            for j in range(16):
                sl = j
                for h in range(4):
                    S.op("pe", lambda e, h=h, j=j, sl=sl: e.matmul(PS[obk][0:8, h * 128:(h + 1) * 128], lhsT=PTs.ap[:, j * 32 + h * 8:j * 32 + (h + 1) * 8],
                                                                  rhs=VP[sl].ap[:, h * 128:(h + 1) * 128], start=False, stop=False, skip_group_check=True),
                         r=[PTs.b, B_vpg[sl % 4]], w=[PSB[obk]])
            for h in range(4):
                S.op("pe", lambda e, h=h, b=b: e.matmul(PS[obk][0:8, h * 128:(h + 1) * 128], lhsT=PTN.ap[:, h * 8:(h + 1) * 8],
                                                        rhs=VN.ap[:, b, h * 128:(h + 1) * 128], start=False, stop=True, skip_group_check=True),
                     r=[PTN.b, VN.b], w=[PSB[obk]])
            S.op("act", lambda e, b=b: e.copy(out=OS.ap[:, b, :, :], in_=PS[obk][0:8, :].rearrange("p (h d) -> p h d", h=4)), r=[PSB[obk]], w=[OS.b])
        S.op("dve", lambda e: e.reciprocal(out=SMs.ap, in_=SMs.ap), r=[SMs.b], w=[SMs.b])
        S.op("dve", lambda e: e.tensor_tensor(out=OS.ap, in0=OS.ap, in1=SMs.ap.unsqueeze(3).to_broadcast([8, 16, 4, 128]), op=ALU.mult), r=[SMs.b], w=[OS.b])
        S.op("sp", lambda e: e.dma_start(out=SEL.ap[:, 0:2, :], in_=selc), w=[SEL.b], dma="SEL")
        S.op("dve", lambda e: e.scalar_tensor_tensor(out=SEL.ap[:, 2, :], in0=SEL.ap[:, 1, :], scalar=LAM.ap[0:8, 0:1], in1=SEL.ap[:, 0, :], op0=ALU.mult, op1=ALU.add),
             r=[LAM.b], w=[SEL.b])
        S.barrier()
        AR.off = mark2
        OC = Tl([4, 64, 128], F32, "OC")
        osf = OS.ap.rearrange("p b h d -> p (b h d)")
        ocf = OC.ap.rearrange("p g d -> p (g d)")
        for ch in range(16):
            S.op("pe", lambda e, ch=ch: e.matmul(PS[5][0:4, :], lhsT=SEL.ap[:, 2, :], rhs=osf[:, ch * 512:(ch + 1) * 512], start=True, stop=True),
                 r=[SEL.b, OS.b], w=[PSB[5]])
            S.op("act", lambda e, ch=ch: e.copy(out=ocf[:, ch * 512:(ch + 1) * 512], in_=PS[5][0:4, :]), r=[PSB[5]], w=[OC.b])
        jk = Tl([4, 64, 128], F32, "jk2")
        st = Tl([4, 64], F32, "st64")
        subln_tile(OC, 4, 64, jk, st, SUBW)
        S.op("sp", lambda e: e.dma_start(out=oat_scr[TP:TT, :].rearrange("(b t) n -> t b n", t=4), in_=OC.ap.rearrange("p (b h) d -> p b (h d)", h=4)),
             r=[OC.b], dma="OC")
        S.barrier()

    def merge_phase():
        AR.off = 0
        Wo = AR.alloc([128, 8, D], BF16)
        B_wo = Buf("wo")
        S.op("pool", lambda e: e.dma_start(out=Wo, in_=w_out.rearrange("(k p) n -> p k n", p=128)), w=[B_wo], dma="wg0")
        GA = AR.alloc([128, 2, D], F32)
        load_gain(GA, 0, 3)
        CAT = [Tl([128, D], F32, "CAT%d" % i) for i in range(2)]
        HR = [Tl([128, D], F32, "HR%d" % i) for i in range(2)]
        CB = Tl([128, D], BF16, "CB")
        CT = Tl([128, 8, 128], BF16, "CT")
        MM = Tl([128, D], F32, "MM")
        junk = Tl([128, D], BF16, "junk")
        st = Tl([128, 4], F32, "st")
        for i, (t0, rows) in enumerate(tiles_all):
            sl = i % 2
            cat, hr = CAT[sl], HR[sl]
            S.op("sp", lambda e, cat=cat, t0=t0, rows=rows: e.dma_start(out=cat.ap[:rows, 0:512], in_=orw_scr[t0:t0 + rows, :]), w=[cat.b], dma="CAT%d" % sl)
            S.op("sp", lambda e, cat=cat, t0=t0, rows=rows: e.dma_start(out=cat.ap[:rows, 512:1024], in_=oat_scr[t0:t0 + rows, :]), w=[cat.b], dma="CAT%d" % sl)
            S.op("sp", lambda e, hr=hr, t0=t0, rows=rows: e.dma_start(out=hr.ap[:rows, :], in_=h_scr[t0:t0 + rows, :]), w=[hr.b], dma="HR%d" % sl)
            S.op("dve", lambda e, cat=cat, rows=rows: e.tensor_copy(out=CB.ap[:rows, :], in_=cat.ap[:rows, :]), r=[cat.b], w=[CB.b])
            for kc in range(8):
                S.op("pe", lambda e, kc=kc, rows=rows: e.transpose(psb16(0)[:, kc * 128:kc * 128 + rows], CB.ap[:rows, kc * 128:(kc + 1) * 128], identb[:rows, :rows]),
                     r=[CB.b, B_ident], w=[PSB[0]])
            S.op("act", lambda e, rows=rows: e.copy(out=CT.ap[:, :, 0:rows], in_=psb16(0).rearrange("p (k t) -> p k t", k=8)[:, :, 0:rows]), r=[PSB[0]], w=[CT.b])
            for nh in range(2):
                pc = 1 + nh
                for kc in range(8):
                    S.op("pe", lambda e, kc=kc, nh=nh, pc=pc, rows=rows: e.matmul(PS[pc][:rows, :], lhsT=CT.ap[:, kc, 0:rows], rhs=Wo[:, kc, nh * 512:(nh + 1) * 512],
                                                                                start=(kc == 0), stop=(kc == 7)), r=[CT.b, B_wo], w=[PSB[pc]])
                S.op("act", lambda e, nh=nh, pc=pc, rows=rows: e.copy(out=MM.ap[:rows, nh * 512:(nh + 1) * 512], in_=PS[pc][:rows, :]), r=[PSB[pc]], w=[MM.b])
            rms_rstd(MM.ap[:rows, :], rows, junk.ap[:rows, :], st.ap[:rows, 0:1], st.ap[:rows, 1:2], [MM.b], junk.b, st.b)
            S.op("dve", lambda e, rows=rows: e.scalar_tensor_tensor(out=MM.ap[:rows, :], in0=MM.ap[:rows, :], scalar=st.ap[:rows, 1:2], in1=GA[:rows, 0, :],
                                                                     op0=ALU.mult, op1=ALU.mult), r=[st.b, B_GA[0]], w=[MM.b])
            S.op("dve", lambda e, rows=rows, hr=hr: e.tensor_tensor(out=hr.ap[:rows, :], in0=hr.ap[:rows, :], in1=MM.ap[:rows, :], op=ALU.add), r=[MM.b], w=[hr.b])
            S.op("sp", lambda e, rows=rows, hr=hr, t0=t0: e.dma_start(out=h2_scr[t0:t0 + rows, :], in_=hr.ap[:rows, :]), r=[hr.b], dma="HR%d" % sl)
        S.barrier()

    B_qk = Buf("qk")
    B_vb = Buf("vb")

    ffn_phase("f1", x_all, h_scr, ff1_in, ff1_out, 0, 1)
    AR.off = 0
    qT = AR.alloc([128, 4, TT], BF16)
    kT = AR.alloc([128, 4, TT], BF16)
    Vb = AR.alloc([128, 16, 4, 132], BF16)
    attn_mark = AR.off
    if os.environ.get("KDEV_PROJ", "1") == "1":
        proj_phase(qT, kT, Vb)

    SKIP = os.environ.get("KDEV_SKIP", "")
    if STAGE >= 3:
        AR.off = attn_mark
        A = attn_setup()
        if "P" not in SKIP:
            attn_prompt(A, qT, kT, Vb)
        if os.environ.get("KDEV_NOSAMP") is None:
            attn_sample(A, qT, kT)
    if STAGE >= 2 and "R" not in SKIP:
        rwkv_phase()
    if STAGE >= 3:
        if "M" not in SKIP:
            merge_phase()
        if "F" not in SKIP:
            ffn_phase("f2", h2_scr, y_all, ff2_in, ff2_out, 4, 5)

    S.barrier()
    S.emit()
    return nc


_CACHE = {}


def _bucket(n):
    me = 16
    nf = np.maximum(n, 1).astype(np.float32)
    large = me + (np.log(nf / np.float32(me)) / np.float32(math.log(128 / me)) * np.float32(32 - me)).astype(np.int32)
    large = np.minimum(large, 31)
    return np.where(n < me, n, large)


def _onehot():
    n = np.arange(-127, 256)
    oh = np.zeros((33, 383), np.float32)
    bk = _bucket(np.maximum(n, 0))
    for j, nn in enumerate(n):
        if nn < 0:
            oh[32, j] = 1.0
        else:
            oh[bk[j], j] = 1.0
    return oh


def _selc():
    m = np.zeros((8, 2, 4), np.float32)
    for t in range(4):
        m[t, 0, t] = 1.0
        m[4 + t, 1, t] = -1.0
    return m


def _cmask():
    t = np.arange(64)
    m = np.zeros((64, 4, 64), np.float32)
    m[:, 0, :] = (t[:, None] > t[None, :])
    m[:, 1, :] = (t[:, None] < t[None, :])
    m[:, 2, :] = (t[:, None] <= t[None, :])
    m[63, 3, 0] = 1.0
    return m


def kernel(**inp):
    f32 = np.float32
    if "nc" not in _CACHE:
        _CACHE["nc"] = build_program()
    nc = _CACHE["nc"]
    xp = np.asarray(inp["x_prompt"], f32)
    xs = np.asarray(inp["x_sample"], f32)
    shared = {
        "gains": np.ascontiguousarray(np.asarray(inp["norm_gains"], f32)[0]),
        "ff1_in": np.ascontiguousarray(np.asarray(inp["ff1_in"], f32)[0]),
        "ff1_out": np.ascontiguousarray(np.asarray(inp["ff1_out"], f32)[0]),
        "ff2_in": np.ascontiguousarray(np.asarray(inp["ff2_in"], f32)[0]),
        "ff2_out": np.ascontiguousarray(np.asarray(inp["ff2_out"], f32)[0]),
        "w_in": np.ascontiguousarray(np.asarray(inp["w_in"], f32)[0]),
        "w_out": np.ascontiguousarray(np.asarray(inp["w_out"], f32)[0]),
        "identf": np.eye(128, dtype=f32),
        "rwvec": np.ascontiguousarray(np.concatenate([np.asarray(inp[k], f32).reshape(-1) for k in
                 ("rw_mu", "rw_w0", "rw_a0", "rw_kk", "rw_ka", "rw_rk", "rw_gn_w", "rw_gn_b")])[None, :]),
        "cmask": _cmask(),
        "lamvec": np.ascontiguousarray(np.concatenate([np.asarray(inp[k], f32).reshape(-1) for k in ("da_lq1", "da_lq2", "da_lk1", "da_lk2")])[None, :]),
        "subln": np.ascontiguousarray(np.asarray(inp["da_subln"], f32).reshape(1, 128)),
        "relb": np.ascontiguousarray(np.asarray(inp["rel_bias_table"], f32)),
        "onehot": _onehot(),
        "selc": _selc(),
        "iotap": np.arange(128, dtype=np.int32).reshape(128, 1),
    }
    if STAGE >= 3 and os.environ.get("KDEV_NOSAMP") is None:
        shared["ck2"] = np.asarray(inp["cache_k"], f32).reshape(NPOOL * 128, 512)
        shared["cv2"] = np.asarray(inp["cache_v"], f32).reshape(NPOOL * 128, 512)
    shared.update({
        "rw_w2": np.ascontiguousarray(np.asarray(inp["rw_w2"], f32)[0]),
        "rw_a2": np.ascontiguousarray(np.asarray(inp["rw_a2"], f32)[0]),
        "rw_g2": np.ascontiguousarray(np.asarray(inp["rw_g2"], f32)[0]),
    })
    ssh = np.asarray(inp["state_shift"], f32)[0]
    swkv = np.asarray(inp["state_wkv"], f32)[0]
    in_maps = []
    for c in range(NCORES):
        m = dict(shared)
        m["ptab"] = np.ascontiguousarray(np.asarray(inp["page_table"], np.int32)[16 * c:16 * c + 16].reshape(1, 256))
        m["shift0"] = np.ascontiguousarray(ssh[16 * c:16 * c + 16])
        m["wkv0"] = np.ascontiguousarray(swkv[16 * c:16 * c + 16].reshape(128, 4096))
        m["x_all"] = np.ascontiguousarray(np.concatenate([xp[c], xs[16 * c:16 * c + 16].reshape(TS, D)], axis=0))
        in_maps.append(m)
    ncr = int(os.environ.get("KDEV_CORES", NCORES))
    t0 = time.time()
    if os.environ.get("KDEV_TRACE"):
        res = run_bass_kernel_spmd(nc, in_maps[:ncr], core_ids=list(range(ncr)), trace=True)
        print("EXEC_NS", res.exec_time_ns)
    else:
        res = run_bass_kernel_spmd(nc, in_maps[:ncr], core_ids=list(range(ncr)))
    if os.environ.get("KDEV_CORES"):
        print("run time", time.time() - t0)
    R = list(res.results) + [res.results[0]] * (NCORES - ncr)
    if os.environ.get("KDEV_DBG"):
        _CACHE["R"] = R
    y_prompt = np.stack([R[c]["y_all"][:TP] for c in range(NCORES)], 0)
    y_sample = np.concatenate([R[c]["y_all"][TP:].reshape(16, 4, D) for c in range(NCORES)], 0)
    k_prompt = np.stack([R[c]["k_out"][:TP].reshape(TP, 4, 2, 64) for c in range(NCORES)], 0)[None]
    v_prompt = np.stack([R[c]["v_out"][:TP].reshape(TP, 4, 128) for c in range(NCORES)], 0)[None]
    k_sample = np.concatenate([R[c]["k_out"][TP:].reshape(16, 4, 4, 2, 64) for c in range(NCORES)], 0)[None]
    v_sample = np.concatenate([R[c]["v_out"][TP:].reshape(16, 4, 4, 128) for c in range(NCORES)], 0)[None]
    wkv_prompt = np.stack([R[c]["wkvp_out"] for c in range(NCORES)], 0)[None]
    wkv_sample = np.concatenate([R[c]["wkvs_out"].reshape(16, 8, 64, 64) for c in range(NCORES)], 0)[None]
    shift_prompt = np.concatenate([R[c]["shp_out"] for c in range(NCORES)], 0)[None]
    shift_sample = np.concatenate([R[c]["shs_out"] for c in range(NCORES)], 0)[None]
    outs = (y_prompt, y_sample, k_prompt, v_prompt, k_sample, v_sample, wkv_prompt, wkv_sample, shift_prompt, shift_sample)
    return tuple(np.ascontiguousarray(o, dtype=f32) for o in outs)
```
